# Optimizing a Trainium2 kernel written in Bass

```python
import math
import jax, jax.numpy as jnp
from jax import lax
import numpy as np

D_MODEL = 1024
BATCH = 16
SEQ = 2048
DEPTH = 4

BRANCH_WIDTH = D_MODEL // 2
N_BRANCH = 3
POOL_WINDOWS = (2, 4, 8, 16)
POOL_GROUPS = len(POOL_WINDOWS)
POOL_WIDTH = BRANCH_WIDTH
POOL_GROUP_DIM = POOL_WIDTH // POOL_GROUPS
DA_HEAD_DIM = 64
DA_WIDTH = BRANCH_WIDTH
DA_HEADS = DA_WIDTH // (2 * DA_HEAD_DIM)
Q_BLOCK = 128
REL_BUCKETS = 32
REL_MAX_DIST = 128
RW_WIDTH = BRANCH_WIDTH
RW_HEAD_DIM = 64
RW_HEADS = RW_WIDTH // RW_HEAD_DIM
DECAY_RANK = 64
ICLR_RANK = 64
GATE_RANK = 128
RW_LOW = 2 * DECAY_RANK + 2 * ICLR_RANK + GATE_RANK
RW_SHIFT_WIDTH = 3 * RW_WIDTH + RW_LOW
RW_SPLITS = [RW_WIDTH, 2 * RW_WIDTH, 3 * RW_WIDTH,
             3 * RW_WIDTH + 2 * DECAY_RANK, 3 * RW_WIDTH + 2 * DECAY_RANK + 2 * ICLR_RANK]
GN_EPS = 64e-5
IN_SPLITS = [POOL_WIDTH, POOL_WIDTH + DA_WIDTH, POOL_WIDTH + 2 * DA_WIDTH,
             POOL_WIDTH + 3 * DA_WIDTH, POOL_WIDTH + 3 * DA_WIDTH + RW_SHIFT_WIDTH]
D_IN = POOL_WIDTH + 3 * DA_WIDTH + RW_SHIFT_WIDTH + N_BRANCH * D_MODEL
D_FF = 4 * D_MODEL
LN_EPS = 1e-5

kernel_name = 'hybrid_pool_diffattn_rwkv7_encoder'


def layer_norm(x, g, b):
    xf = x.astype(jnp.float32)
    mu = jnp.mean(xf, axis=-1, keepdims=True)
    var = jnp.mean(jnp.square(xf - mu), axis=-1, keepdims=True)
    return ((xf - mu) * lax.rsqrt(var + LN_EPS) * g + b).astype(x.dtype)


def multiscale_pool(u, pool_w, pool_scale):
    b, s, _ = u.shape
    uf = u.astype(jnp.float32)
    csum = jnp.concatenate([jnp.zeros((b, 1, POOL_WIDTH), jnp.float32), jnp.cumsum(uf, axis=1)], axis=1)
    t = jnp.arange(s)
    means = []
    for gi, win in enumerate(POOL_WINDOWS):
        left = win // 2
        right = win - 1 - left
        hi = jnp.minimum(t + right + 1, s)
        lo = jnp.maximum(t - left, 0)
        cg = csum[..., gi * POOL_GROUP_DIM:(gi + 1) * POOL_GROUP_DIM]
        cnt = (hi - lo).astype(jnp.float32)[None, :, None]
        means.append((cg[:, hi] - cg[:, lo]) / cnt)
    pooled = jnp.stack(means, axis=2)
    mixed = (pooled - uf.reshape(b, s, POOL_GROUPS, POOL_GROUP_DIM)).astype(u.dtype)
    y = jnp.einsum('bsgc,gcd->bsgd', mixed, pool_w).reshape(b, s, POOL_WIDTH)
    return y * pool_scale


def rel_bucket(rel):
    half = REL_BUCKETS // 2
    max_exact = half // 2
    n = jnp.abs(rel)
    nf = jnp.maximum(n, 1).astype(jnp.float32)
    large = max_exact + (jnp.log(nf / max_exact) / math.log(REL_MAX_DIST / max_exact)
                         * (half - max_exact)).astype(jnp.int32)
    large = jnp.minimum(large, half - 1)
    return jnp.where(rel > 0, half, 0) + jnp.where(n < max_exact, n, large)


def diff_attention(q, k, v, lam_vecs, subln_g, rel_table, layer_idx):
    b, s, _ = q.shape
    dtype = q.dtype
    q = q.reshape(b, s, DA_HEADS, 2, DA_HEAD_DIM) * (DA_HEAD_DIM ** -0.5)
    k = k.reshape(b, s, DA_HEADS, 2, DA_HEAD_DIM)
    v = v.reshape(b, s, DA_HEADS, 2 * DA_HEAD_DIM)
    lam_init = 0.8 - 0.6 * math.exp(-0.3 * layer_idx)
    lf = lam_vecs.astype(jnp.float32)
    lam = jnp.exp(jnp.sum(lf[0] * lf[1])) - jnp.exp(jnp.sum(lf[2] * lf[3])) + lam_init
    n_blk = s // Q_BLOCK
    qb = q.reshape(b, n_blk, Q_BLOCK, DA_HEADS, 2, DA_HEAD_DIM).transpose(1, 0, 2, 3, 4, 5)
    k_pos = jnp.arange(s)

    def block(args):
        q_blk, start = args
        logits = jnp.einsum('bqhmd,bkhmd->bhmqk', q_blk, k).astype(jnp.float32)
        rel = k_pos[None, :] - (start + jnp.arange(Q_BLOCK))[:, None]
        bias = rel_table[rel_bucket(rel)].astype(jnp.float32)
        logits = logits + bias.transpose(2, 0, 1)[None, :, None]
        p = jax.nn.softmax(logits, axis=-1)
        attn = p[:, :, 0] - lam * p[:, :, 1]
        return jnp.einsum('bhqk,bkhd->bqhd', attn.astype(v.dtype), v)

    starts = jnp.arange(n_blk) * Q_BLOCK
    o = lax.map(block, (qb, starts))
    o = o.transpose(1, 0, 2, 3, 4).reshape(b, s, DA_HEADS, 2 * DA_HEAD_DIM).astype(jnp.float32)
    o = o * lax.rsqrt(jnp.mean(o * o, axis=-1, keepdims=True) + 1e-5) * subln_g
    o = o * (1.0 - lam_init)
    return o.reshape(b, s, DA_WIDTH).astype(dtype)


def centred_token_shift(z, mu_prev, mu_next):
    prev = jnp.pad(z, ((0, 0), (1, 0), (0, 0)))[:, :-1]
    nxt = jnp.pad(z, ((0, 0), (0, 1), (0, 0)))[:, 1:]
    return z + mu_prev * (prev - z) + mu_next * (nxt - z)


def wkv7_scan(r, w, k, v, kk, a, reverse):
    b, s, h, n = r.shape

    def step(state, inp):
        r_t, w_t, k_t, v_t, kk_t, a_t = inp
        sa = jnp.einsum('bhij,bhj->bhi', state, -kk_t)
        state = (state * w_t[:, :, None, :] + sa[..., None] * (kk_t * a_t)[:, :, None, :]
                 + v_t[..., None] * k_t[:, :, None, :])
        return state, jnp.einsum('bhij,bhj->bhi', state, r_t)

    xs = tuple(jnp.moveaxis(t, 1, 0) for t in (r, w, k, v, kk, a))
    _, y = lax.scan(step, jnp.zeros((b, h, n, n), jnp.float32), xs, reverse=reverse)
    return jnp.moveaxis(y, 0, 1)


def rwkv7_bidir(z, mu_prev, mu_next, w0, w2, a0, a2, g2, k_k, k_a, r_k, gn_g, gn_b):
    out_dtype = z.dtype
    b, s, _ = z.shape
    z = centred_token_shift(z.astype(jnp.float32), mu_prev, mu_next)
    r, k, v, lw, la, lg = jnp.split(z, RW_SPLITS, axis=-1)

    def heads(t):
        return t.reshape(t.shape[:-1] + (RW_HEADS, RW_HEAD_DIM))

    lw = lw.reshape(b, s, 2, DECAY_RANK)
    la = la.reshape(b, s, 2, ICLR_RANK)
    w_log = -jax.nn.softplus(-(w0 + jnp.einsum('bsdr,drc->bsdc', jnp.tanh(lw), w2))) - 0.5
    decay = heads(jnp.exp(-jnp.exp(w_log)))
    iclr = heads(jax.nn.sigmoid(a0 + jnp.einsum('bsdr,drc->bsdc', la, a2)))
    gate = jnp.matmul(jax.nn.sigmoid(lg), g2)
    rh, kh, vh = heads(r), heads(k), heads(v)
    kk = kh * heads(k_k)
    kk = kk / jnp.maximum(jnp.sqrt(jnp.sum(kk * kk, axis=-1, keepdims=True)), 1e-12)
    k_a_h = heads(k_a)

    def direction(d, reverse):
        a_d = iclr[:, :, d]
        k_d = kh * (1.0 + (a_d - 1.0) * k_a_h)
        return wkv7_scan(rh, decay[:, :, d], k_d, vh, kk, a_d, reverse)

    y = direction(0, False) + direction(1, True)
    mu = jnp.mean(y, axis=-1, keepdims=True)
    var = jnp.mean(jnp.square(y - mu), axis=-1, keepdims=True)
    y = ((y - mu) * lax.rsqrt(var + GN_EPS)).reshape(b, s, RW_WIDTH) * gn_g + gn_b
    bonus = jnp.sum(rh * kh * heads(r_k), axis=-1, keepdims=True) * vh
    out = (y + bonus.reshape(b, s, RW_WIDTH)) * gate
    return out.astype(out_dtype)


def hybrid_mixer(h, w_in, pool_w, pool_scale, da_lambda, da_subln_g, rel_bias,
                 rw_mu_prev, rw_mu_next, rw_w0, rw_w2, rw_a0, rw_a2, rw_g2, rw_k_k, rw_k_a, rw_r_k,
                 rw_gn_g, rw_gn_b, w_branch, b_gate, w_out, layer_idx):
    b, s, _ = h.shape
    p = jnp.einsum('bsd,de->bse', h, w_in)
    u, q, k, v, zr, gl = jnp.split(p, IN_SPLITS, axis=-1)
    y_a = multiscale_pool(u, pool_w, pool_scale)
    y_b = diff_attention(q, k, v, da_lambda, da_subln_g, rel_bias, layer_idx)
    y_c = rwkv7_bidir(zr, rw_mu_prev, rw_mu_next, rw_w0, rw_w2, rw_a0, rw_a2, rw_g2,
                      rw_k_k, rw_k_a, rw_r_k, rw_gn_g, rw_gn_b)
    ys = jnp.stack([y_a, y_b, y_c], axis=2)
    branch = jnp.einsum('bsnc,ncd->bsnd', ys, w_branch)
    gates = jax.nn.sigmoid(gl.reshape(b, s, N_BRANCH, D_MODEL) + b_gate)
    merged = jnp.sum(gates * branch, axis=2)
    return jnp.einsum('bsd,de->bse', merged, w_out)


def setup_inputs(seed: int = 0) -> dict:
    key = jax.random.key(seed)
    ks = iter(jax.random.split(key, 40))
    L = DEPTH
    W = BRANCH_WIDTH
    beta = (8.0 * DEPTH) ** -0.25

    def nrm(shape, scale):
        return scale * jax.random.normal(next(ks), shape, jnp.float32)

    def unif(shape, lo, hi):
        return jax.random.uniform(next(ks), shape, jnp.float32, lo, hi)

    return {
        'x': nrm((BATCH, SEQ, D_MODEL), 1.0),
        'ln0_g': 1.0 + nrm((D_MODEL,), 0.05),
        'ln0_b': nrm((D_MODEL,), 0.01),
        'w_in': nrm((L, D_MODEL, D_IN), D_MODEL ** -0.5),
        'pool_w': nrm((L, POOL_GROUPS, POOL_GROUP_DIM, POOL_GROUP_DIM), POOL_GROUP_DIM ** -0.5),
        'pool_scale': 1.0 + nrm((L, POOL_WIDTH), 0.05),
        'da_lambda': nrm((L, 4, DA_HEAD_DIM), 0.1),
        'da_subln_g': 1.0 + nrm((L, 2 * DA_HEAD_DIM), 0.05),
        'rel_bias': nrm((REL_BUCKETS, DA_HEADS), 0.5),
        'rw_mu_prev': unif((L, RW_SHIFT_WIDTH), 0.0, 0.5),
        'rw_mu_next': unif((L, RW_SHIFT_WIDTH), 0.0, 0.5),
        'rw_w0': unif((L, 2, RW_WIDTH), -5.0, 2.0),
        'rw_w2': nrm((L, 2, DECAY_RANK, RW_WIDTH), 0.5 * DECAY_RANK ** -0.5),
        'rw_a0': nrm((L, 2, RW_WIDTH), 0.1),
        'rw_a2': nrm((L, 2, ICLR_RANK, RW_WIDTH), ICLR_RANK ** -0.5),
        'rw_g2': nrm((L, GATE_RANK, RW_WIDTH), GATE_RANK ** -0.5),
        'rw_k_k': 0.85 + nrm((L, RW_WIDTH), 0.05),
        'rw_k_a': 1.0 + nrm((L, RW_WIDTH), 0.05),
        'rw_r_k': nrm((L, RW_WIDTH), 0.1),
        'rw_gn_g': 1.0 + nrm((L, RW_WIDTH), 0.05),
        'rw_gn_b': nrm((L, RW_WIDTH), 0.01),
        'w_branch': nrm((L, N_BRANCH, W, D_MODEL), beta * W ** -0.5),
        'b_gate': nrm((L, N_BRANCH, D_MODEL), 0.01),
        'w_out': nrm((L, D_MODEL, D_MODEL), beta * D_MODEL ** -0.5),
        'ln1_g': 1.0 + nrm((L, D_MODEL), 0.05),
        'ln1_b': nrm((L, D_MODEL), 0.01),
        'w_up': nrm((L, D_MODEL, D_FF), D_MODEL ** -0.5),
        'w_down': nrm((L, D_FF, D_MODEL), beta * D_FF ** -0.5),
        'ln2_g': 1.0 + nrm((L, D_MODEL), 0.05),
        'ln2_b': nrm((L, D_MODEL), 0.01),
    }


def reference(x, ln0_g, ln0_b, w_in, pool_w, pool_scale, da_lambda, da_subln_g, rel_bias,
              rw_mu_prev, rw_mu_next, rw_w0, rw_w2, rw_a0, rw_a2, rw_g2, rw_k_k, rw_k_a, rw_r_k,
              rw_gn_g, rw_gn_b, w_branch, b_gate, w_out, ln1_g, ln1_b, w_up, w_down, ln2_g, ln2_b):
    alpha = (2.0 * DEPTH) ** 0.25
    x = layer_norm(x, ln0_g, ln0_b)
    for l in range(DEPTH):
        mix = hybrid_mixer(x, w_in[l], pool_w[l], pool_scale[l], da_lambda[l], da_subln_g[l], rel_bias,
                           rw_mu_prev[l], rw_mu_next[l], rw_w0[l], rw_w2[l], rw_a0[l], rw_a2[l], rw_g2[l],
                           rw_k_k[l], rw_k_a[l], rw_r_k[l], rw_gn_g[l], rw_gn_b[l],
                           w_branch[l], b_gate[l], w_out[l], l)
        x = layer_norm(alpha * x + mix, ln1_g[l], ln1_b[l])
        hid = jnp.square(jax.nn.relu(jnp.einsum('bsd,df->bsf', x, w_up[l])))
        x = layer_norm(alpha * x + jnp.einsum('bsf,fd->bsd', hid, w_down[l]), ln2_g[l], ln2_b[l])
    return x
```

```python
import math
import contextlib
import numpy as np
import concourse.bass as bass
import concourse.mybir as mybir
from concourse.bass_utils import run_bass_kernel_spmd

F32 = mybir.dt.float32
BF16 = mybir.dt.bfloat16
AF = mybir.ActivationFunctionType
ALU = mybir.AluOpType

D_MODEL = 1024
DEPTH = 4
NSEQ = 2
SEQ = 2048
D_IN = 7040
D_FF = 4096
ALPHA = (2.0 * DEPTH) ** 0.25
LN_EPS = 1e-5
GN_EPS = 64e-5
NPV = 96
C = 64
NCG = 4

ENGS = ("pe", "act", "dve", "pool", "sp")
N_DMA_SEMS = 24


class Em:
    def __init__(self, nc, stack):
        self.nc = nc
        self.sem = {e: stack.enter_context(nc.semaphore("s_" + e)) for e in ENGS}
        self.semobj = {"s_" + e: self.sem[e] for e in ENGS}
        for i in range(N_DMA_SEMS):
            self.semobj["d%d" % i] = stack.enter_context(nc.semaphore("d%d" % i))
        self.val = {k: 0 for k in self.semobj}
        self.seen = {e: {k: 0 for k in self.semobj} for e in ENGS}
        self.ops = {e: [] for e in ENGS}
        self.state = {}
        self.dma_rr = 0
        self.nops = 0

    def _deps(self, reads, writes):
        toks = []
        for k in reads:
            st = self.state.get(k)
            if st and st[0]:
                toks.append(st[0])
        for k in writes:
            st = self.state.get(k)
            if st:
                if st[0]:
                    toks.append(st[0])
                toks.extend(st[1])
        return toks

    def _commit(self, tok, reads, writes):
        for k in reads:
            st = self.state.setdefault(k, [None, []])
            st[1].append(tok)
            if len(st[1]) > 6:
                best = {}
                for s, v in st[1]:
                    if best.get(s, 0) < v:
                        best[s] = v
                st[1] = list(best.items())
        for k in writes:
            self.state[k] = [tok, []]

    def _waits(self, eng, toks):
        need = {}
        for s, v in toks:
            if eng == "pe" and s == "s_pe":
                continue
            if v > self.seen[eng][s] and v > need.get(s, 0):
                need[s] = v
        for s, v in need.items():
            self.seen[eng][s] = v
        return list(need.items())

    def op(self, eng, fn, reads=(), writes=()):
        waits = self._waits(eng, self._deps(reads, writes))
        s = "s_" + eng
        self.val[s] += 1
        tok = (s, self.val[s])
        self.ops[eng].append((waits, fn, s, 1))
        self._commit(tok, reads, writes)
        self.nops += 1
        return tok

    def dma(self, eng, out, in_, reads=(), writes=(), **kw):
        i = self.dma_rr
        self.dma_rr = (self.dma_rr + 1) % N_DMA_SEMS
        s = "d%d" % i
        toks = self._deps(reads, writes)
        toks.append((s, self.val[s]))
        waits = self._waits(eng, toks)
        self.val[s] += 16
        tok = (s, self.val[s])

        def fn(e, out=out, in_=in_, kw=kw):
            return e.dma_start(out=out, in_=in_, **kw)
        self.ops[eng].append((waits, fn, s, 16))
        self._commit(tok, reads, writes)
        self.nops += 1
        return tok

    def barrier(self):
        for e in ENGS:
            toks = [(s, v) for s, v in self.val.items() if v > 0]
            waits = self._waits(e, toks)
            if waits:
                self.ops[e].append((waits, None, None, 0))
        self.state = {}

    def flush(self):
        self.barrier()
        nc = self.nc
        with nc.Block() as block:
            def mk(ename):
                def body(e):
                    for waits, fn, s, inc in self.ops[ename]:
                        for ws, wv in waits:
                            e.wait_ge(self.semobj[ws], wv)
                        if fn is not None:
                            fn(e).then_inc(self.semobj[s], inc)
                return body
            block.tensor(mk("pe"))
            block.scalar(mk("act"))
            block.vector(mk("dve"))
            block.gpsimd(mk("pool"))
            block.sync(mk("sp"))
        self.ops = {e: [] for e in ENGS}


_UID = [0]


def _nm(name):
    _UID[0] += 1
    return "%s_u%d" % (name, _UID[0])


class Rot:
    def __init__(self, nc, stack, name, n, shape, dtype):
        self.tiles = [stack.enter_context(nc.sbuf_tensor(_nm("%s_%d") % (name, i), shape, dtype))
                      for i in range(n)]
        self.name = name
        self.i = 0

    def next(self):
        t = self.tiles[self.i]
        k = (self.name, self.i)
        self.i = (self.i + 1) % len(self.tiles)
        return t, k


def _rel_bucket_np(rel):
    half = 16
    max_exact = 8
    n = np.abs(rel)
    nf = np.maximum(n, 1).astype(np.float32)
    large = max_exact + (np.log(nf / np.float32(max_exact)) / np.float32(math.log(128 / max_exact))
                         * np.float32(half - max_exact)).astype(np.int32)
    large = np.minimum(large, half - 1)
    return np.where(rel > 0, half, 0) + np.where(n < max_exact, n, large)


def host_consts(S):
    cs = {}
    cs["ident"] = np.eye(128, dtype=np.float32)
    p = np.arange(128)[:, None]
    f = np.arange(128)[None, :]
    cs["relidx"] = np.stack([_rel_bucket_np(128 * d + p - f) for d in (-1, 0, 1)]).astype(np.float32)
    t = np.arange(S)
    ic = []
    for win in (2, 4, 8, 16):
        left = win // 2
        right = win - 1 - left
        hi = np.minimum(t + right + 1, S)
        lo = np.maximum(t - left, 0)
        ic.append(1.0 / (hi - lo).astype(np.float32))
    cs["pool_icnt"] = np.stack(ic).astype(np.float32)
    r = np.arange(128)
    rl = r % 64
    same = (r[:, None] // 64) == (r[None, :] // 64)
    SL = (same & (rl[:, None] > rl[None, :])).astype(np.float32)
    SU = (same & (rl[:, None] < rl[None, :])).astype(np.float32)
    c64 = np.arange(64)
    IU = (rl[:, None] <= c64[None, :]).astype(np.float32)
    IL = (rl[:, None] >= c64[None, :]).astype(np.float32)
    cs["wkvmask"] = np.concatenate([SL, SU, IU, IL], axis=1).astype(np.float32)
    cs["bones"] = same.astype(np.float32)
    hm = np.zeros((128, 2), np.float32)
    hm[:64, 0] = 1.0
    hm[64:, 1] = 1.0
    cs["hmask"] = hm
    m = np.ones((128, S), np.float32)
    m[:, ::C] = 0.0
    cs["scanmask"] = m
    return cs


def pack_pvec(inp, L):
    out = np.zeros((L, 128, NPV), np.float32)
    for l in range(L):
        col = 0

        def put(vec):
            nonlocal col
            v = np.asarray(vec, np.float32).reshape(-1, 128)
            out[l, :, col:col + v.shape[0]] = v.T
            col += v.shape[0]
        put(inp["b_gate"][l].reshape(-1))
        put(inp["pool_scale"][l])
        put(inp["rw_mu_prev"][l])
        put(inp["rw_mu_next"][l])
        put(inp["rw_w0"][l].reshape(-1))
        put(inp["rw_a0"][l].reshape(-1))
        put(inp["rw_k_k"][l])
        put(inp["rw_k_a"][l])
        put(inp["rw_r_k"][l])
        put(inp["rw_gn_g"][l])
        put(inp["rw_gn_b"][l])
        put(inp["da_subln_g"][l])
        assert col == 95
    return out


PV_BG, PV_PS, PV_MUP, PV_MUN, PV_W0, PV_A0, PV_KK, PV_KA, PV_RK, PV_GG, PV_GB, PV_SG = \
    0, 24, 28, 43, 58, 66, 74, 78, 82, 86, 90, 94


class Builder:
    def __init__(self, S=SEQ, L=DEPTH, upto="all", dump=(), dbg=0):
        self.dbg = dbg
        self.S, self.L, self.upto, self.dump = S, L, upto, set(dump)
        self.T = NSEQ * S
        self.nc = bass.Bass("TRN2", target_bir_lowering=False)
        self.ps_i = 0

    def din(self, name, shape, dtype=F32):
        return self.nc.dram_tensor(name, list(shape), dtype, kind="ExternalInput").ap()

    def dscr(self, name, shape, dtype=F32):
        kind = "ExternalOutput" if name in self.dump else "Internal"
        return self.nc.dram_tensor(name, list(shape), dtype, kind=kind).ap()

    def psb(self):
        t = self.psum[self.ps_i]
        k = ("ps", self.ps_i)
        self.ps_i = (self.ps_i + 1) % 8
        return t, k

    def mm(self, out, lhsT, rhs, start=True, stop=True, r=(), w=()):
        return self.em.op("pe", lambda e: e.matmul(out, lhsT=lhsT, rhs=rhs, start=start, stop=stop,
                                                   skip_group_check=True), r, w)

    def tr(self, out, in_, ident, r=(), w=()):
        return self.em.op("pe", lambda e: e.transpose(out, in_, ident), r, w)

    def act(self, out, in_, func, bias=None, scale=None, r=(), w=()):
        kw = {}
        if bias is not None:
            kw["bias"] = bias
        if scale is not None:
            kw["scale"] = scale
        return self.em.op("act", lambda e: e.activation(out=out, in_=in_, func=func, **kw), r, w)

    def tt(self, eng, out, in0, in1, op, r=(), w=()):
        return self.em.op(eng, lambda e: e.tensor_tensor(out=out, in0=in0, in1=in1, op=op), r, w)

    def ts(self, eng, out, in0, s1, s2, op0, op1=None, r=(), w=()):
        if op1 is None:
            return self.em.op(eng, lambda e: e.tensor_scalar(out=out, in0=in0, scalar1=s1, scalar2=None,
                                                             op0=op0), r, w)
        return self.em.op(eng, lambda e: e.tensor_scalar(out=out, in0=in0, scalar1=s1, scalar2=s2,
                                                         op0=op0, op1=op1), r, w)

    def stt(self, out, in0, scalar, in1, op0, op1, r=(), w=()):
        return self.em.op("dve", lambda e: e.scalar_tensor_tensor(out=out, in0=in0, scalar=scalar, in1=in1,
                                                                  op0=op0, op1=op1), r, w)

    def cp(self, eng, out, in_, r=(), w=()):
        if eng == "act":
            return self.em.op("act", lambda e: e.copy(out=out, in_=in_), r, w)
        return self.em.op(eng, lambda e: e.tensor_copy(out=out, in_=in_), r, w)

    def memset(self, eng, ap, val, w=()):
        return self.em.op(eng, lambda e: e.memset(ap, val), (), w)

    def recip(self, out, in_, r=(), w=()):
        return self.em.op("dve", lambda e: e.reciprocal(out=out, in_=in_), r, w)

    def declare(self):
        S, T, L = self.S, self.T, self.L
        d = self.din
        self.x = d("x", [T, D_MODEL])
        self.ln0_g = d("ln0_g", [D_MODEL]); self.ln0_b = d("ln0_b", [D_MODEL])
        self.w_in = d("w_in", [L, D_MODEL, D_IN])
        self.pool_w = d("pool_w", [L, 4, 128, 128])
        self.da_lambda = d("da_lambda", [L, 256])
        self.rel_bias = d("rel_bias", [128])
        self.rw_w2 = d("rw_w2", [L, 2, 64, 512]); self.rw_a2 = d("rw_a2", [L, 2, 64, 512])
        self.rw_g2 = d("rw_g2", [L, 128, 512])
        self.w_branch = d("w_branch", [L, 3, 512, D_MODEL])
        self.w_out = d("w_out", [L, D_MODEL, D_MODEL])
        self.ln1_g = d("ln1_g", [L, D_MODEL]); self.ln1_b = d("ln1_b", [L, D_MODEL])
        self.w_up = d("w_up", [L, D_MODEL, D_FF]); self.w_down = d("w_down", [L, D_FF, D_MODEL])
        self.ln2_g = d("ln2_g", [L, D_MODEL]); self.ln2_b = d("ln2_b", [L, D_MODEL])
        self.pvec_d = d("pvec", [L, 128, NPV])
        self.c_ident = d("ident", [128, 128]); self.c_relidx = d("relidx", [3, 128, 128])
        self.c_icnt = d("pool_icnt", [4, S]); self.c_wkvmask = d("wkvmask", [128, 384])
        self.c_bones = d("bones", [128, 128]); self.c_hmask = d("hmask", [128, 2])
        self.c_scanmask = d("scanmask", [128, S])
        self.out = self.nc.dram_tensor("out", [T, D_MODEL], F32, kind="ExternalOutput").ap()
        s = self.dscr
        self.x_tm = s("x_tm", [T, D_MODEL])
        self.x1_tm = s("x1_tm", [T, D_MODEL])
        self.uT = s("uT", [512, T]); self.qT = s("qT", [512, T], BF16); self.kT = s("kT", [512, T], BF16)
        self.v_tm = s("v_tm", [T, 512], BF16)
        self.zrT = s("zrT", [1920, T]); self.gT = s("gT", [3072, T])
        self.yT = s("yT", [3, 512, T], BF16)
        self.shT = s("shT", [3, 128, T])
        self.wup_bf = s("wup_bf", [128, 8, D_FF], BF16)
        self.wdn_bf = s("wdn_bf", [8, 128, 32, 128], BF16)

    def ln_tile(self, src, ksrc, dst, kdst, g_bc, b_bc, kgb):
        st, kst = self.lnst.next()
        em = self.em
        em.op("dve", lambda e: e.bn_stats(out=st[:, 0:6], in_=src[:, 0:512]), [ksrc], [kst])
        em.op("dve", lambda e: e.bn_stats(out=st[:, 6:12], in_=src[:, 512:1024]), [ksrc], [kst])
        em.op("dve", lambda e: e.bn_aggr(out=st[:, 12:14], in_=st[:, 0:12]), [kst], [kst])
        self.ts("dve", st[:, 14:15], st[:, 13:14], LN_EPS, None, ALU.add, r=[kst], w=[kst])
        self.act(st[:, 14:15], st[:, 14:15], AF.Sqrt, r=[kst], w=[kst])
        self.recip(st[:, 14:15], st[:, 14:15], r=[kst], w=[kst])
        self.ts("dve", dst[:], src[:], st[:, 12:13], st[:, 14:15], ALU.subtract, ALU.mult, r=[ksrc, kst], w=[kdst])
        self.tt("pool", dst[:], dst[:], g_bc[:], ALU.mult, r=[kdst, kgb], w=[kdst])
        self.tt("pool", dst[:], dst[:], b_bc[:], ALU.add, r=[kdst, kgb], w=[kdst])

    def transpose_into_xT(self, src, ksrc, tt):
        for half in range(2):
            ps, kps = self.psb()
            for c4 in range(4):
                c = half * 4 + c4
                self.tr(ps[:, c4 * 128:(c4 + 1) * 128], src[:, c * 128:(c + 1) * 128], self.ident_f[:],
                        r=[ksrc, "ident"], w=[kps])
            eng = "act" if half == 0 else "dve"
            self.cp(eng, self.xT[:, half * 4:half * 4 + 4, tt * 128:(tt + 1) * 128],
                    ps[:].rearrange("p (c t) -> p c t", c=4), r=[kps], w=[("xT", tt)])

    def load_bc(self, tile, key, vec_ap):
        self.em.dma("sp", tile[:], vec_ap.partition_broadcast(128), writes=[key])

    def stage_ln_in(self):
        nc, em = self.nc, self.em
        with contextlib.ExitStack() as st:
            xin = Rot(nc, st, "lnx", 2, [128, 1024], F32)
            xo = Rot(nc, st, "lno", 2, [128, 1024], F32)
            g_bc = st.enter_context(nc.sbuf_tensor(_nm("lng"), [128, 1024], F32))
            b_bc = st.enter_context(nc.sbuf_tensor(_nm("lnb"), [128, 1024], F32))
            self.load_bc(g_bc, "lngb", self.ln0_g)
            self.load_bc(b_bc, "lngb", self.ln0_b)
            for tt in range(self.T // 128):
                xt, kx = xin.next()
                em.dma("sp", xt[:], self.x[tt * 128:(tt + 1) * 128, :], writes=[kx])
                xn, kn = xo.next()
                self.ln_tile(xt, kx, xn, kn, g_bc, b_bc, "lngb")
                em.dma("sp", self.x_tm[tt * 128:(tt + 1) * 128, :], xn[:], reads=[kn], writes=[("x_tm", tt)])
                self.transpose_into_xT(xn, kn, tt)
            em.flush()

    def stage_proj(self, l):
        nc, em, T = self.nc, self.em, self.T
        with contextlib.ExitStack() as st:
            wst = Rot(nc, st, "wst", 2, [128, 8, 512], F32)
            wbf = Rot(nc, st, "wbf", 2, [128, 8, 512], BF16)
            ev32 = Rot(nc, st, "ev32", 4, [128, 512], F32)
            ev16 = Rot(nc, st, "ev16", 4, [128, 512], BF16)
            wv = self.w_in[l].rearrange("(c p) n -> p c n", p=128)
            ntr = T // 512
            xkeys = [("xT", tt) for tt in range(T // 128)]
            for cg in range(14):
                c0 = cg * 512
                cw = min(512, D_IN - c0)
                ws, wsk = wst.next()
                em.dma("sp", ws[:, :, 0:cw], wv[:, :, c0:c0 + cw], writes=[wsk])
                wb, wbk = wbf.next()
                self.cp("pool", wb[:, :, 0:cw], ws[:, :, 0:cw], r=[wsk], w=[wbk])
                if cg == 3:
                    for tt in range(T // 128):
                        ps, kps = self.psb()
                        for dc in range(8):
                            self.mm(ps[:], self.xT[:, dc, tt * 128:(tt + 1) * 128], wb[:, dc, :],
                                    start=(dc == 0), stop=(dc == 7), r=[wbk, ("xT", tt)], w=[kps])
                        ev, kev = ev16.next()
                        self.cp("dve" if tt % 2 else "act", ev[:], ps[:], r=[kps], w=[kev])
                        em.dma("sp", self.v_tm[tt * 128:(tt + 1) * 128, :], ev[:], reads=[kev], writes=[("v_tm", tt)])
                    continue
                for cb in range(cw // 128):
                    gb = cg * 4 + cb
                    for tr_ in range(ntr):
                        ps, kps = self.psb()
                        xk = xkeys[tr_ * 4:(tr_ + 1) * 4]
                        for dc in range(8):
                            self.mm(ps[:], wb[:, dc, cb * 128:(cb + 1) * 128], self.xT[:, dc, tr_ * 512:(tr_ + 1) * 512],
                                    start=(dc == 0), stop=(dc == 7), r=[wbk] + xk, w=[kps])
                        tsl = slice(tr_ * 512, (tr_ + 1) * 512)
                        if gb < 4:
                            ev, kev = ev32.next()
                            self.cp("dve", ev[:], ps[:], r=[kps], w=[kev])
                            dst = self.uT[gb * 128:(gb + 1) * 128, tsl]
                        elif gb < 8:
                            ev, kev = ev16.next()
                            self.ts("dve", ev[:], ps[:], 0.125, None, ALU.mult, r=[kps], w=[kev])
                            dst = self.qT[(gb - 4) * 128:(gb - 3) * 128, tsl]
                        elif gb < 12:
                            ev, kev = ev16.next()
                            self.cp("dve", ev[:], ps[:], r=[kps], w=[kev])
                            dst = self.kT[(gb - 8) * 128:(gb - 7) * 128, tsl]
                        elif gb < 31:
                            ev, kev = ev32.next()
                            self.cp("dve" if (gb + tr_) % 2 else "act", ev[:], ps[:], r=[kps], w=[kev])
                            dst = self.zrT[(gb - 16) * 128:(gb - 15) * 128, tsl]
                        else:
                            ev, kev = ev32.next()
                            gi = gb - 31
                            self.act(ev[:], ps[:], AF.Sigmoid, bias=self.pv[:, PV_BG + gi:PV_BG + gi + 1],
                                     r=[kps, "pv"], w=[kev])
                            dst = self.gT[gi * 128:(gi + 1) * 128, tsl]
                        em.dma("sp", dst, ev[:], reads=[kev], writes=[("scr", gb, tr_)])
            em.flush()

    def stage_pool(self, l):
        nc, em, S = self.nc, self.em, self.S
        with contextlib.ExitStack() as st:
            pw32 = st.enter_context(nc.sbuf_tensor(_nm("pw32"), [128, 4, 128], F32))
            pwbf = st.enter_context(nc.sbuf_tensor(_nm("pwbf"), [128, 4, 128], BF16))
            em.dma("sp", pw32[:], self.pool_w[l].rearrange("g c d -> c g d"), writes=["pw32"])
            self.cp("dve", pwbf[:], pw32[:], r=["pw32"], w=["pwbf"])
            icr = Rot(nc, st, "icnt", 2, [128, S], F32)
            ub = Rot(nc, st, "ub", 2, [128, S + 32], F32)
            sa = Rot(nc, st, "psa", 2, [128, S + 32], F32)
            sb = Rot(nc, st, "psb_", 2, [128, S + 32], F32)
            mxr = Rot(nc, st, "pmx", 2, [128, S], BF16)
            evr = Rot(nc, st, "pev", 3, [128, 512], BF16)
            for t_, k_ in zip(ub.tiles, [("ub", 0), ("ub", 1)]):
                self.memset("pool", t_[:], 0.0, w=[k_])
            for g in range(4):
                ic, kic = icr.next()
                em.dma("sp", ic[:], self.c_icnt[g:g + 1, :].partition_broadcast(128), writes=[kic])
                win = 2 ** (g + 1)
                right = win - 1 - win // 2
                for b in range(NSEQ):
                    u, ku = ub.next()
                    em.dma("sp", u[:, 16:16 + S], self.uT[g * 128:(g + 1) * 128, b * S:(b + 1) * S], writes=[ku])
                    cur, kcur = u, ku
                    for k in range(g + 1):
                        sh = 2 ** k
                        lo = 2 ** (k + 1) - 1
                        nxt, knxt = (sa if k % 2 == 0 else sb).next()
                        self.tt("pool" if k % 2 else "dve", nxt[:, lo:S + 32], cur[:, lo:S + 32],
                                cur[:, lo - sh:S + 32 - sh], ALU.add, r=[kcur], w=[knxt])
                        cur, kcur = nxt, knxt
                    tmp, ktmp = (sa if (g + 1) % 2 == 0 else sb).next()
                    self.tt("dve", tmp[:, 0:S], cur[:, 16 + right:16 + right + S], ic[:], ALU.mult,
                            r=[kcur, kic], w=[ktmp])
                    mx, kmx = mxr.next()
                    self.tt("pool", mx[:], tmp[:, 0:S], u[:, 16:16 + S], ALU.subtract, r=[ktmp, ku], w=[kmx])
                    for tr_ in range(S // 512):
                        ps, kps = self.psb()
                        self.mm(ps[:], pwbf[:, g, :], mx[:, tr_ * 512:(tr_ + 1) * 512], r=["pwbf", kmx], w=[kps])
                        ev, kev = evr.next()
                        self.ts("dve", ev[:], ps[:], self.pv[:, PV_PS + g:PV_PS + g + 1], None, ALU.mult,
                                r=[kps, "pv"], w=[kev])
                        em.dma("sp", self.yT[0, g * 128:(g + 1) * 128, b * S + tr_ * 512:b * S + (tr_ + 1) * 512],
                               ev[:], reads=[kev], writes=[("yT0", g, b, tr_)])
            em.flush()

    def build_band(self):
        nc, em = self.nc, self.em
        with contextlib.ExitStack() as st:
            idx = st.enter_context(nc.sbuf_tensor(_nm("ridx"), [128, 3, 128], F32))
            eq = Rot(nc, st, "req", 2, [128, 128], F32)
            em.dma("sp", idx[:], self.c_relidx.rearrange("d p f -> p d f"), writes=["ridx"])
            em.dma("sp", self.relb[:], self.rel_bias.partition_broadcast(128), writes=["relb"])
            self.memset("dve", self.band[:], 0.0, w=["band"])
            for d3 in range(3):
                for bk in range(32):
                    e_, ke = eq.next()
                    self.ts("pool", e_[:], idx[:, d3, :], float(bk), None, ALU.is_equal, r=["ridx"], w=[ke])
                    for h in range(4):
                        self.stt(self.band[:, h, d3, :], e_[:], self.relb[:, bk * 4 + h:bk * 4 + h + 1],
                                 self.band[:, h, d3, :], ALU.mult, ALU.add, r=[ke, "relb", "band"], w=["band"])
            em.flush()

    def stage_attn(self, l):
        nc, em, S = self.nc, self.em, self.S
        nqr, nkb = S // 512, S // 128
        lam_init = 0.8 - 0.6 * math.exp(-0.3 * l)
        with contextlib.ExitStack() as st:
            strips = st.enter_context(nc.sbuf_tensor(_nm("strips"), [128, 4, 6, 512], F32))
            zeros = st.enter_context(nc.sbuf_tensor(_nm("azero"), [128, 128], F32))
            ones_b = st.enter_context(nc.sbuf_tensor(_nm("aones_b"), [128, 128], BF16))
            ones_f = st.enter_context(nc.sbuf_tensor(_nm("aones_f"), [128, 128], F32))
            lamv = st.enter_context(nc.sbuf_tensor(_nm("lamv"), [128, 256], F32))
            lsm = st.enter_context(nc.sbuf_tensor(_nm("lsm"), [128, 8], F32))
            self.memset("dve", zeros[:], 0.0, w=["azero"])
            self.memset("dve", ones_b[:], 1.0, w=["aones_b"])
            self.memset("dve", ones_f[:], 1.0, w=["aones_f"])
            for h in range(4):
                for dl in range(-1, 5):
                    for bq in range(4):
                        d = dl - bq
                        dst = strips[:, h, dl + 1, bq * 128:(bq + 1) * 128]
                        if abs(d) <= 1:
                            self.cp("pool", dst, self.band[:, h, d + 1, :], r=["band"], w=["strips"])
                        else:
                            col = (31 if d >= 2 else 15) * 4 + h
                            self.ts("dve", dst, zeros[:], self.relb[:, col:col + 1], None, ALU.add,
                                    r=["azero", "relb"], w=["strips"])
            em.dma("sp", lamv[:], self.da_lambda[l:l + 1, :].partition_broadcast(128), writes=["lamv"])
            self.tt("dve", lamv[:, 0:64], lamv[:, 0:64], lamv[:, 64:128], ALU.mult, r=["lamv"], w=["lamv"])
            self.tt("dve", lamv[:, 128:192], lamv[:, 128:192], lamv[:, 192:256], ALU.mult, r=["lamv"], w=["lamv"])
            em.op("dve", lambda e: e.tensor_reduce(out=lsm[:, 0:1], in_=lamv[:, 0:64], axis=mybir.AxisListType.X,
                                                   op=ALU.add), ["lamv"], ["lsm"])
            em.op("dve", lambda e: e.tensor_reduce(out=lsm[:, 1:2], in_=lamv[:, 128:192], axis=mybir.AxisListType.X,
                                                   op=ALU.add), ["lamv"], ["lsm"])
            self.act(lsm[:, 2:4], lsm[:, 0:2], AF.Exp, r=["lsm"], w=["lsm"])
            self.tt("dve", lsm[:, 4:5], lsm[:, 3:4], lsm[:, 2:3], ALU.subtract, r=["lsm"], w=["lsm"])
            self.ts("dve", lsm[:, 4:5], lsm[:, 4:5], -lam_init, None, ALU.add, r=["lsm"], w=["lsm"])
            self.ts("dve", lsm[:, 5:6], self.pv[:, PV_SG:PV_SG + 1], 1.0 - lam_init, None, ALU.mult,
                    r=["pv", "lsm"], w=["lsm"])
            neglam = lsm[:, 4:5]
            sgl = lsm[:, 5:6]

            qr = Rot(nc, st, "aq", 2, [128, S], BF16)
            kr = Rot(nc, st, "ak", 2, [128, S], BF16)
            vr = Rot(nc, st, "av", 2, [128, nkb, 128], BF16)
            ptr = Rot(nc, st, "apt", 4, [128, 512], BF16)
            tmr = Rot(nc, st, "atm", 3, [128, 512], F32)
            e1 = Rot(nc, st, "ae1", 2, [128, 512], F32)
            e2 = Rot(nc, st, "ae2", 2, [128, 512], F32)
            e3 = Rot(nc, st, "ae3", 2, [128, 512], F32)
            yo = Rot(nc, st, "ayo", 2, [128, 512], BF16)
            P = self.psum
            sbank = 0
            for b in range(NSEQ):
                for h in range(4):
                    qh, kq = qr.next(); kh, kk_ = kr.next(); vh, kv = vr.next()
                    em.dma("sp", qh[:], self.qT[h * 128:(h + 1) * 128, b * S:(b + 1) * S], writes=[kq])
                    em.dma("sp", kh[:], self.kT[h * 128:(h + 1) * 128, b * S:(b + 1) * S], writes=[kk_])
                    em.dma("sp", vh[:], self.v_tm[b * S:(b + 1) * S, h * 128:(h + 1) * 128]
                           .rearrange("(j p) c -> p j c", p=128), writes=[kv])
                    for i in range(nqr):
                        for m in range(2):
                            O, kO = P[2 * m], ("ps", 2 * m)
                            Z, kZ = P[2 * m + 1], ("ps", 2 * m + 1)
                            for j in range(nkb):
                                sb_i = 4 + sbank
                                sbank = (sbank + 1) % 3
                                ps, kps = P[sb_i], ("ps", sb_i)
                                self.mm(ps[:], kh[m * 64:(m + 1) * 64, j * 128:(j + 1) * 128],
                                        qh[m * 64:(m + 1) * 64, i * 512:(i + 1) * 512], r=[kk_, kq], w=[kps])
                                dl = j - 4 * i
                                pt, kpt = ptr.next()
                                if -1 <= dl <= 4:
                                    tm, ktm = tmr.next()
                                    self.tt("dve", tm[:], ps[:], strips[:, h, dl + 1, :], ALU.add,
                                            r=[kps, "strips"], w=[ktm])
                                    self.act(pt[:], tm[:], AF.Exp, r=[ktm], w=[kpt])
                                else:
                                    col = (31 if dl >= 5 else 15) * 4 + h
                                    self.act(pt[:], ps[:], AF.Exp, bias=self.relb[:, col:col + 1],
                                             r=[kps, "relb"], w=[kpt])
                                self.mm(O[:], vh[:, j, :], pt[:], start=(j == 0), stop=(j == nkb - 1),
                                        r=[kv, kpt], w=[kO])
                                self.mm(Z[:], ones_b[:], pt[:], start=(j == 0), stop=(j == nkb - 1),
                                        r=["aones_b", kpt], w=[kZ])
                        a1, ka1 = e1.next(); a2, ka2 = e2.next(); a3, ka3 = e3.next()
                        self.recip(a1[:], P[1][:], r=[("ps", 1)], w=[ka1])
                        self.tt("dve", a2[:], P[0][:], a1[:], ALU.mult, r=[("ps", 0), ka1], w=[ka2])
                        self.recip(a1[:], P[3][:], r=[("ps", 3), ka1], w=[ka1])
                        self.tt("dve", a3[:], P[2][:], a1[:], ALU.mult, r=[("ps", 2), ka1], w=[ka3])
                        self.stt(a2[:], a3[:], neglam, a2[:], ALU.mult, ALU.add, r=[ka3, ka2, "lsm"], w=[ka2])
                        self.tt("pool", a3[:], a2[:], a2[:], ALU.mult, r=[ka2, ka3], w=[ka3])
                        self.mm(P[7][:], ones_f[:], a3[:], r=["aones_f", ka3], w=[("ps", 7)])
                        self.act(a1[:], P[7][:], AF.Sqrt, bias=self.epsv[:, 0:1], scale=1.0 / 128.0,
                                 r=[("ps", 7), ka1, "epsv"], w=[ka1])
                        self.recip(a1[:], a1[:], r=[ka1], w=[ka1])
                        y_, ky = yo.next()
                        self.stt(y_[:], a2[:], sgl, a1[:], ALU.mult, ALU.mult, r=[ka2, ka1, "lsm"], w=[ky])
                        em.dma("sp", self.yT[1, h * 128:(h + 1) * 128, b * S + i * 512:b * S + (i + 1) * 512],
                               y_[:], reads=[ky], writes=[("yT1", h, b, i)])
            em.flush()

    def stage_rwkv(self, l):
        nc, em, S = self.nc, self.em, self.S
        WD = F32
        GT = NCG * C
        ngr = S // GT
        n512 = S // 512
        NEG_E = -math.exp(-0.5)
        pv = self.pv
        with contextlib.ExitStack() as st:
            def alloc(name, shape, dt=F32):
                return st.enter_context(nc.sbuf_tensor(_nm(name), shape, dt))
            msk = alloc("wmsk", [128, 384])
            bones = alloc("wbones", [128, 128]); bones64 = alloc("wbones64", [128, 128])
            hmask = alloc("whmask", [128, 2])
            scanm = alloc("wscanm", [128, GT])
            w2sb = alloc("w2sb", [128, 512]); a2sb = alloc("a2sb", [128, 512]); g2sb = alloc("g2sb", [128, 512])
            c0v = alloc("wc0", [128, 16]); omka = alloc("womka", [128, 4])
            em.dma("sp", msk[:], self.c_wkvmask, writes=["wmsk"])
            em.dma("sp", bones[:], self.c_bones, writes=["wbones"])
            em.dma("sp", hmask[:], self.c_hmask, writes=["whmask"])
            em.dma("sp", scanm[:], self.c_scanmask[:, 0:GT], writes=["wscanm"])
            em.dma("sp", w2sb[:], self.rw_w2[l].rearrange("d r c -> (d r) c"), writes=["w2sb"])
            em.dma("sp", a2sb[:], self.rw_a2[l].rearrange("d r c -> (d r) c"), writes=["a2sb"])
            em.dma("sp", g2sb[:], self.rw_g2[l], writes=["g2sb"])
            self.ts("dve", bones64[:], bones[:], 1.0 / 64.0, None, ALU.mult, r=["wbones"], w=["wbones64"])
            self.tt("dve", c0v[:, 0:15], pv[:, PV_MUP:PV_MUP + 15], pv[:, PV_MUN:PV_MUN + 15], ALU.add, r=["pv"], w=["wc0"])
            self.ts("dve", c0v[:, 0:15], c0v[:, 0:15], -1.0, 1.0, ALU.mult, ALU.add, r=["wc0"], w=["wc0"])
            self.ts("dve", omka[:], pv[:, PV_KA:PV_KA + 4], -1.0, 1.0, ALU.mult, ALU.add, r=["pv"], w=["womka"])
            SLm, SUm, IUm, ILm = msk[:, 0:128], msk[:, 128:256], msk[:, 256:320], msk[:, 320:384]

            big = self.xT[:].rearrange("p c t -> p (c t)").bitcast(F32).rearrange("p (a s) -> p a s", a=8)
            r_, k_, v_, kkn, yacc, logw, a_, kd = [big[:, i, :] for i in range(8)]
            kb = ["xT%d" % i for i in range(self.T // 128)]
            K_R, K_K, K_V, K_KK, K_Y, K_LW, K_A, K_KD = ["wbig%d" % i for i in range(8)]

            zb = alloc("wzb", [128, S + 2])
            self.memset("dve", zb[:], 0.0, w=["wzb"])
            shr = Rot(nc, st, "wsh", 3, [128, 512], F32)
            t5 = Rot(nc, st, "wt5", 4, [128, 512], F32)
            yo = Rot(nc, st, "wyo", 2, [128, 512], BF16)
            gtile = {}
            for nm in ("G", "tA", "E1", "at", "E2", "rt", "E3", "ba", "bt", "kt", "tB", "E4", "bh", "kh"):
                gtile[nm] = alloc("wg_" + nm, [128, GT])
            gt = alloc("wgt", [128, NCG]); gam = alloc("wgam", [128, NCG])
            atp = alloc("watp", [128, NCG, 128], WD); btp = alloc("wbtp", [128, NCG, 128], WD)
            ktp = alloc("wktp", [128, NCG, 128], WD)
            padr = Rot(nc, st, "wpad", 2, [128, NCG, 128], F32)
            atT = alloc("watT", [128, NCG, 128], WD)
            setT = [{nm: alloc("w%s%d" % (nm, s_), [128, NCG, 128], WD) for nm in ("vT", "bhT", "khT")}
                    for s_ in range(2)]
            U = []
            for ci in range(NCG):
                d_ = {}
                for nm in ("X0", "X1", "XT0", "XT1", "PT0", "PT1", "AKT", "AKV"):
                    d_[nm] = alloc("wu%s_%d" % (nm, ci), [128, 128], WD)
                for s_ in range(2):
                    d_["WU%d" % s_] = alloc("wuWU%d_%d" % (s_, ci), [128, 256], WD)
                    d_["M%d" % s_] = alloc("wuM%d_%d" % (s_, ci), [128, 128], WD)
                    d_["ARBT%d" % s_] = alloc("wuARBT%d_%d" % (s_, ci), [128, 64], WD)
                    d_["ARKT%d" % s_] = alloc("wuARKT%d_%d" % (s_, ci), [128, 64], WD)
                    d_["QT%d" % s_] = alloc("wuQT%d_%d" % (s_, ci), [128, 64], WD)
                U.append(d_)
            STt = [alloc("wST%d" % i, [128, 128], WD) for i in range(2)]
            ident = self.ident_f

            def shift_block(blk, b, dst, kdst, post=None):
                em.dma("sp", zb[:, 1:S + 1], self.zrT[blk * 128:(blk + 1) * 128, b * S:(b + 1) * S], writes=["wzb"])
                self.act(dst, zb[:, 1:S + 1], AF.Identity, scale=c0v[:, blk:blk + 1], r=["wzb", "wc0"], w=[kdst])
                self.stt(dst, zb[:, 0:S], pv[:, PV_MUP + blk:PV_MUP + blk + 1], dst, ALU.mult, ALU.add,
                         r=["wzb", "pv", kdst], w=[kdst])
                self.stt(dst, zb[:, 2:S + 2], pv[:, PV_MUN + blk:PV_MUN + blk + 1], dst, ALU.mult, ALU.add,
                         r=["wzb", "pv", kdst], w=[kdst])
                if post is not None:
                    self.act(dst, dst, post, r=[kdst], w=[kdst])

            cnt = [0]

            def evac_copy(out, in_, r, w):
                cnt[0] += 1
                self.cp("act" if cnt[0] % 2 else "dve", out, in_, r=r, w=w)

            for b in range(NSEQ):
                for si, (blk, post) in enumerate(((12, AF.Tanh), (13, None), (14, AF.Sigmoid))):
                    shift_block(blk, b, r_, K_R, post)
                    em.dma("sp", self.shT[si, :, b * S:(b + 1) * S], r_, reads=[K_R], writes=[("shT", si, b)])
                if self.dbg == 1:
                    em.flush(); return
                for hp in range(4):
                    shift_block(hp, b, r_, K_R)
                    shift_block(4 + hp, b, k_, K_K)
                    shift_block(8 + hp, b, v_, K_V)
                    self.ts("pool", kkn, k_, pv[:, PV_KK + hp:PV_KK + hp + 1], None, ALU.mult, r=[K_K, "pv"], w=[K_KK])
                    for q4 in range(n512):
                        rg = slice(q4 * 512, (q4 + 1) * 512)
                        t1, k1 = t5.next()
                        self.tt("pool", t1[:], kkn[:, rg], kkn[:, rg], ALU.mult, r=[K_KK], w=[k1])
                        ps, kps = self.psb()
                        self.mm(ps[:], bones[:], t1[:], r=["wbones", k1], w=[kps])
                        self.act(t1[:], ps[:], AF.Sqrt, r=[kps], w=[k1])
                        self.ts("dve", t1[:], t1[:], 1e-12, None, ALU.max, r=[k1], w=[k1])
                        self.recip(t1[:], t1[:], r=[k1], w=[k1])
                        self.tt("dve", kkn[:, rg], kkn[:, rg], t1[:], ALU.mult, r=[K_KK, k1], w=[K_KK])
                    if self.dbg == 2:
                        em.flush(); return
                    for d in range(2):
                        rev = (d == 1)
                        dsl = slice(d * 64, (d + 1) * 64)
                        for q4 in range(n512):
                            rg = slice(q4 * 512, (q4 + 1) * 512)
                            grg = slice(b * S + q4 * 512, b * S + (q4 + 1) * 512)
                            s1, ks1 = shr.next()
                            em.dma("sp", s1[:], self.shT[0, :, grg], reads=[("shT", 0, b)], writes=[ks1])
                            ps, kps = self.psb()
                            self.mm(ps[:], w2sb[dsl, hp * 128:(hp + 1) * 128], s1[dsl, :], r=["w2sb", ks1], w=[kps])
                            t1, k1 = t5.next()
                            self.act(t1[:], ps[:], AF.Sigmoid, bias=pv[:, PV_W0 + d * 4 + hp:PV_W0 + d * 4 + hp + 1],
                                     r=[kps, "pv"], w=[k1])
                            self.ts("pool", logw[:, rg], t1[:], NEG_E, None, ALU.mult, r=[k1], w=[K_LW])
                            s2, ks2 = shr.next()
                            em.dma("sp", s2[:], self.shT[1, :, grg], reads=[("shT", 1, b)], writes=[ks2])
                            ps, kps = self.psb()
                            self.mm(ps[:], a2sb[dsl, hp * 128:(hp + 1) * 128], s2[dsl, :], r=["a2sb", ks2], w=[kps])
                            self.act(a_[:, rg], ps[:], AF.Sigmoid, bias=pv[:, PV_A0 + d * 4 + hp:PV_A0 + d * 4 + hp + 1],
                                     r=[kps, "pv"], w=[K_A])
                        self.ts("pool", kd, a_, pv[:, PV_KA + hp:PV_KA + hp + 1], omka[:, hp:hp + 1], ALU.mult, ALU.add,
                                r=[K_A, "pv", "womka"], w=[K_KD])
                        self.tt("pool", kd, kd, k_, ALU.mult, r=[K_KD, K_K], w=[K_KD])
                        if self.dbg == 3:
                            em.flush(); return
                        self.memset("pool", STt[0][:], 0.0, w=["wST0"])
                        stcur = 0
                        pending = []
                        MT = SLm if rev else SUm
                        MN = SUm if rev else SLm
                        MI = ILm if rev else IUm
                        groups = list(range(ngr))
                        if rev:
                            groups = groups[::-1]
                        for gi, grp in enumerate(groups):
                            sset = gi % 2
                            g0 = grp * GT
                            gs = slice(g0, g0 + GT)
                            G = gtile["G"]

                            def g3(ap):
                                return ap.rearrange("p (c t) -> p c t", c=NCG)
                            em.op("dve", lambda e, G=G, gs=gs: e.tensor_tensor_scan(
                                out=G[:], data0=scanm[:], data1=logw[:, gs], initial=0.0, op0=ALU.mult, op1=ALU.add),
                                ["wscanm", K_LW], ["wg_G"])
                            self.cp("dve", gt[:], g3(G[:])[:, :, C - 1], r=["wg_G"], w=["wgt"])
                            if rev:
                                self.tt("dve", g3(G[:]), gt[:].unsqueeze(2).broadcast_to([128, NCG, C]), g3(G[:]),
                                        ALU.subtract, r=["wgt", "wg_G"], w=["wg_G"])
                                self.tt("dve", G[:], G[:], logw[:, gs], ALU.add, r=["wg_G", K_LW], w=["wg_G"])
                            g_ = gtile
                            self.tt("dve", g_["tA"][:], G[:], logw[:, gs], ALU.subtract, r=["wg_G", K_LW], w=["wg_tA"])
                            self.act(g_["E1"][:], g_["tA"][:], AF.Exp, r=["wg_tA"], w=["wg_E1"])
                            self.stt(g_["at"][:], kkn[:, gs], -1.0, g_["E1"][:], ALU.mult, ALU.mult, r=[K_KK, "wg_E1"], w=["wg_at"])
                            self.act(g_["E2"][:], G[:], AF.Exp, r=["wg_G"], w=["wg_E2"])
                            self.tt("pool", g_["rt"][:], r_[:, gs], g_["E2"][:], ALU.mult, r=[K_R, "wg_E2"], w=["wg_rt"])
                            self.act(g_["E3"][:], G[:], AF.Exp, scale=-1.0, r=["wg_G"], w=["wg_E3"])
                            self.tt("pool", g_["ba"][:], kkn[:, gs], a_[:, gs], ALU.mult, r=[K_KK, K_A], w=["wg_ba"])
                            self.tt("dve", g_["bt"][:], g_["ba"][:], g_["E3"][:], ALU.mult, r=["wg_ba", "wg_E3"], w=["wg_bt"])
                            self.tt("pool", g_["kt"][:], kd[:, gs], g_["E3"][:], ALU.mult, r=[K_KD, "wg_E3"], w=["wg_kt"])
                            self.tt("dve", g3(g_["tB"][:]), gt[:].unsqueeze(2).broadcast_to([128, NCG, C]), g3(G[:]),
                                    ALU.subtract, r=["wgt", "wg_G"], w=["wg_tB"])
                            self.act(g_["E4"][:], g_["tB"][:], AF.Exp, r=["wg_tB"], w=["wg_E4"])
                            self.tt("pool", g_["bh"][:], g_["ba"][:], g_["E4"][:], ALU.mult, r=["wg_ba", "wg_E4"], w=["wg_bh"])
                            self.tt("dve", g_["kh"][:], kd[:, gs], g_["E4"][:], ALU.mult, r=[K_KD, "wg_E4"], w=["wg_kh"])
                            self.act(gam[:], gt[:], AF.Exp, r=["wgt"], w=["wgam"])

                            def pad(dst, kdst, src, ksrc, engs=("dve", "pool")):
                                for h in range(2):
                                    self.ts(engs[h], dst[:, :, h * 64:(h + 1) * 64], g3(src), hmask[:, h:h + 1], None, ALU.mult,
                                            r=[ksrc, "whmask"], w=[kdst])
                            pad(atp, "watp", g_["at"][:], "wg_at")
                            pad(btp, "wbtp", g_["bt"][:], "wg_bt")
                            pad(ktp, "wktp", g_["kt"][:], "wg_kt")
                            TS = setT[sset]

                            def transp(dst, kdst, srcp, ksrcp):
                                ps, kps = self.psb()
                                for c in range(NCG):
                                    self.tr(ps[:, c * 128:(c + 1) * 128], srcp[:, c, :], ident[:], r=[ksrcp, "ident"], w=[kps])
                                evac_copy(dst[:], ps[:].rearrange("p (c t) -> p c t", c=NCG), r=[kps], w=[kdst])
                            transp(atT, "watT", atp, "watp")
                            for nm, srcn in (("bhT", "bh"), ("khT", "kh")):
                                pd, kpd = padr.next()
                                pad(pd, kpd, g_[srcn][:], "wg_" + srcn)
                                transp(TS[nm], "w%s%d" % (nm, sset), pd, kpd)
                            pd, kpd = padr.next()
                            pad(pd, kpd, v_[:, gs], K_V)
                            transp(TS["vT"], "wvT%d" % sset, pd, kpd)
                            kvT, kbhT, kkhT = "wvT%d" % sset, "wbhT%d" % sset, "wkhT%d" % sset

                            if self.dbg == 4:
                                em.flush(); return

                            def uk(ci, nm):
                                return ("wu", ci, nm)
                            WUn, Mn, ARBn, ARKn, QTn = ["%s%d" % (x, sset) for x in ("WU", "M", "ARBT", "ARKT", "QT")]
                            steps = []

                            def s1():
                                for ci in range(NCG):
                                    u = U[ci]
                                    ps, kps = self.psb()
                                    self.mm(ps[:, 0:128], btp[:, ci, :], atp[:, ci, :], r=["wbtp", "watp"], w=[kps])
                                    self.mm(ps[:, 128:192], btp[:, ci, :], g_["rt"][:, ci * C:(ci + 1) * C], r=["wbtp", "wg_rt"], w=[kps])
                                    self.tt("dve", u["XT0"][:], ps[:, 0:128], MT, ALU.mult, r=[kps, "wmsk"], w=[uk(ci, "XT0")])
                                    self.tt("dve", u[ARBn][:], ps[:, 128:192], MI, ALU.mult, r=[kps, "wmsk"], w=[uk(ci, ARBn)])
                                    self.tt("pool", u["PT0"][:], u["XT0"][:], ident[:], ALU.add, r=[uk(ci, "XT0"), "ident"], w=[uk(ci, "PT0")])
                            steps.append(s1)

                            def s2():
                                for ci in range(NCG):
                                    u = U[ci]
                                    ps, kps = self.psb()
                                    self.mm(ps[:, 0:128], ktp[:, ci, :], atp[:, ci, :], r=["wktp", "watp"], w=[kps])
                                    self.mm(ps[:, 128:192], ktp[:, ci, :], g_["rt"][:, ci * C:(ci + 1) * C], r=["wktp", "wg_rt"], w=[kps])
                                    self.tt("dve", u["AKT"][:], ps[:, 0:128], MT, ALU.mult, r=[kps, "wmsk"], w=[uk(ci, "AKT")])
                                    self.tt("dve", u[ARKn][:], ps[:, 128:192], MI, ALU.mult, r=[kps, "wmsk"], w=[uk(ci, ARKn)])
                            steps.append(s2)

                            def s3():
                                for ci in range(NCG):
                                    u = U[ci]
                                    ps, kps = self.psb()
                                    self.mm(ps[:, 0:128], atp[:, ci, :], btp[:, ci, :], r=["watp", "wbtp"], w=[kps])
                                    self.tt("dve", u["X0"][:], ps[:, 0:128], MN, ALU.mult, r=[kps, "wmsk"], w=[uk(ci, "X0")])
                            steps.append(s3)

                            for lev in range(1, 6):
                                pi, ni = (lev - 1) % 2, lev % 2
                                Xp, XTp, Xn, XTn = "X%d" % pi, "XT%d" % pi, "X%d" % ni, "XT%d" % ni
                                PTp, PTn = "PT%d" % pi, "PT%d" % ni

                                def sa(Xp=Xp, XTp=XTp, Xn=Xn):
                                    for ci in range(NCG):
                                        u = U[ci]
                                        ps, kps = self.psb()
                                        self.mm(ps[:, 0:128], u[XTp][:], u[Xp][:], r=[uk(ci, XTp), uk(ci, Xp)], w=[kps])
                                        evac_copy(u[Xn][:], ps[:, 0:128], r=[kps], w=[uk(ci, Xn)])
                                steps.append(sa)
                                if lev < 5:
                                    def sb_(Xp=Xp, XTp=XTp, XTn=XTn):
                                        for ci in range(NCG):
                                            u = U[ci]
                                            ps, kps = self.psb()
                                            self.mm(ps[:, 0:128], u[Xp][:], u[XTp][:], r=[uk(ci, XTp), uk(ci, Xp)], w=[kps])
                                            evac_copy(u[XTn][:], ps[:, 0:128], r=[kps], w=[uk(ci, XTn)])
                                    steps.append(sb_)

                                def sc(Xn=Xn, PTp=PTp, PTn=PTn):
                                    for ci in range(NCG):
                                        u = U[ci]
                                        ps, kps = self.psb()
                                        self.mm(ps[:, 0:128], u[Xn][:], u[PTp][:], r=[uk(ci, Xn), uk(ci, PTp)], w=[kps])
                                        self.tt("dve", u[PTn][:], ps[:, 0:128], u[PTp][:], ALU.add, r=[kps, uk(ci, PTp)], w=[uk(ci, PTn)])
                                steps.append(sc)
                            TTn = "PT1"

                            def s5():
                                for ci in range(NCG):
                                    u = U[ci]
                                    ps, kps = self.psb()
                                    self.mm(ps[:, 0:128], u["AKT"][:], TS["vT"][:, ci, :], r=[uk(ci, "AKT"), kvT], w=[kps])
                                    evac_copy(u["AKV"][:], ps[:, 0:128], r=[kps], w=[uk(ci, "AKV")])
                            steps.append(s5)

                            def s6():
                                for ci in range(NCG):
                                    u = U[ci]
                                    ps, kps = self.psb()
                                    self.mm(ps[:, 0:128], u[TTn][:], atT[:, ci, :], r=[uk(ci, TTn), "watT"], w=[kps])
                                    self.mm(ps[:, 128:256], u[TTn][:], u["AKV"][:], r=[uk(ci, TTn), uk(ci, "AKV")], w=[kps])
                                    evac_copy(u[WUn][:], ps[:, 0:256], r=[kps], w=[uk(ci, WUn)])
                            steps.append(s6)

                            def s7():
                                for ci in range(NCG):
                                    u = U[ci]
                                    ps, kps = self.psb()
                                    self.mm(ps[:, 0:128], u[WUn][:, 0:128], TS["bhT"][:, ci, :], r=[uk(ci, WUn), kbhT], w=[kps])
                                    self.stt(u[Mn][:], ident[:], gam[:, ci:ci + 1], ps[:, 0:128], ALU.mult, ALU.add,
                                             r=[kps, "ident", "wgam"], w=[uk(ci, Mn)])
                                    ps2, kps2 = self.psb()
                                    self.mm(ps2[:, 0:64], u[WUn][:, 0:128], u[ARBn][:], r=[uk(ci, WUn), uk(ci, ARBn)], w=[kps2])
                                    self.tt("dve", u[QTn][:], ps2[:, 0:64], g_["rt"][:, ci * C:(ci + 1) * C], ALU.add,
                                            r=[kps2, "wg_rt"], w=[uk(ci, QTn)])
                            steps.append(s7)

                            order = list(range(NCG))
                            if rev:
                                order = order[::-1]
                            for ci in order:
                                def seq(ci=ci, TS=TS, sset=sset, g0=g0, d=d, WUn=WUn, Mn=Mn, ARBn=ARBn, ARKn=ARKn, QTn=QTn,
                                        kvT=kvT, kbhT=kbhT, kkhT=kkhT):
                                    nonlocal stcur
                                    u = U[ci]
                                    STc, kSTc = STt[stcur], "wST%d" % stcur
                                    STn, kSTn = STt[1 - stcur], "wST%d" % (1 - stcur)
                                    psy, kpy = self.psb()
                                    self.mm(psy[:, 0:64], STc[:], u[QTn][:], start=True, stop=False, r=[kSTc, uk(ci, QTn)], w=[kpy])
                                    self.mm(psy[:, 0:64], u[WUn][:, 128:256], u[ARBn][:], start=False, stop=False,
                                            r=[uk(ci, WUn), uk(ci, ARBn)], w=[kpy])
                                    self.mm(psy[:, 0:64], TS["vT"][:, ci, :], u[ARKn][:], start=False, stop=True,
                                            r=[kvT, uk(ci, ARKn)], w=[kpy])
                                    ysl = yacc[:, g0 + ci * C:g0 + (ci + 1) * C]
                                    if d == 0:
                                        self.cp("act", ysl, psy[:, 0:64], r=[kpy], w=[K_Y])
                                    else:
                                        self.tt("dve", ysl, psy[:, 0:64], ysl, ALU.add, r=[kpy, K_Y], w=[K_Y])
                                    pss, kpss = self.psb()
                                    self.mm(pss[:, 0:128], u[Mn][:], STc[:], start=True, stop=False, r=[uk(ci, Mn), kSTc], w=[kpss])
                                    self.mm(pss[:, 0:128], TS["bhT"][:, ci, :], u[WUn][:, 128:256], start=False, stop=False,
                                            r=[kbhT, uk(ci, WUn)], w=[kpss])
                                    self.mm(pss[:, 0:128], TS["khT"][:, ci, :], TS["vT"][:, ci, :], start=False, stop=True,
                                            r=[kkhT, kvT], w=[kpss])
                                    self.cp("act", STn[:], pss[:, 0:128], r=[kpss], w=[kSTn])
                                    stcur = 1 - stcur
                                pending.append(seq)
                            npar = len(steps)
                            old = pending[:-NCG]
                            del pending[:-NCG]
                            for si_, s_ in enumerate(steps):
                                s_()
                                if old and si_ % 4 == 3:
                                    old.pop(0)()
                            for s_ in old:
                                s_()
                            if self.dbg == 5:
                                em.flush(); return
                        for s_ in pending:
                            s_()
                        pending = []
                        if self.dbg == 6:
                            em.flush(); return
                    if self.dbg == 7:
                        em.flush(); return
                    for q4 in range(n512):
                        rg = slice(q4 * 512, (q4 + 1) * 512)
                        grg = slice(b * S + q4 * 512, b * S + (q4 + 1) * 512)
                        t1, k1 = t5.next(); t2, k2 = t5.next()
                        ps, kps = self.psb()
                        self.mm(ps[:], bones64[:], yacc[:, rg], r=["wbones64", K_Y], w=[kps])
                        self.tt("dve", t1[:], yacc[:, rg], ps[:], ALU.subtract, r=[K_Y, kps], w=[k1])
                        self.tt("pool", t2[:], t1[:], t1[:], ALU.mult, r=[k1], w=[k2])
                        ps, kps = self.psb()
                        self.mm(ps[:], bones64[:], t2[:], r=["wbones64", k2], w=[kps])
                        self.act(t2[:], ps[:], AF.Sqrt, bias=self.epsv[:, 1:2], r=[kps, "epsv"], w=[k2])
                        self.recip(t2[:], t2[:], r=[k2], w=[k2])
                        self.tt("dve", t1[:], t1[:], t2[:], ALU.mult, r=[k1, k2], w=[k1])
                        self.ts("pool", t1[:], t1[:], pv[:, PV_GG + hp:PV_GG + hp + 1], pv[:, PV_GB + hp:PV_GB + hp + 1],
                                ALU.mult, ALU.add, r=[k1, "pv"], w=[k1])
                        self.stt(t2[:], r_[:, rg], pv[:, PV_RK + hp:PV_RK + hp + 1], k_[:, rg], ALU.mult, ALU.mult,
                                 r=[K_R, K_K, "pv", k2], w=[k2])
                        ps, kps = self.psb()
                        self.mm(ps[:], bones[:], t2[:], r=["wbones", k2], w=[kps])
                        self.tt("dve", t2[:], ps[:], v_[:, rg], ALU.mult, r=[kps, K_V], w=[k2])
                        self.tt("pool", t1[:], t1[:], t2[:], ALU.add, r=[k1, k2], w=[k1])
                        s3_, ks3 = shr.next()
                        em.dma("sp", s3_[:], self.shT[2, :, grg], reads=[("shT", 2, b)], writes=[ks3])
                        ps, kps = self.psb()
                        self.mm(ps[:], g2sb[:, hp * 128:(hp + 1) * 128], s3_[:], r=["g2sb", ks3], w=[kps])
                        y_, ky = yo.next()
                        self.tt("dve", y_[:], t1[:], ps[:], ALU.mult, r=[k1, kps], w=[ky])
                        em.dma("sp", self.yT[2, hp * 128:(hp + 1) * 128, grg], y_[:], reads=[ky], writes=[("yT2", hp, b, q4)])
                    if self.dbg == 8:
                        em.flush(); return
                if self.dbg == 9:
                    em.flush(); return
            em.flush()

    def stage_merge(self, l):
        nc, em, T = self.nc, self.em, self.T
        with contextlib.ExitStack() as st:
            wbr = st.enter_context(nc.sbuf_tensor(_nm("mwbr"), [128, 3, 4, 1024], BF16))
            wo = st.enter_context(nc.sbuf_tensor(_nm("mwo"), [128, 8, 1024], BF16))
            g_bc = st.enter_context(nc.sbuf_tensor(_nm("mlng"), [128, 1024], F32))
            b_bc = st.enter_context(nc.sbuf_tensor(_nm("mlnb"), [128, 1024], F32))
            self.load_bc(g_bc, "mlngb", self.ln1_g[l])
            self.load_bc(b_bc, "mlngb", self.ln1_b[l])
            with contextlib.ExitStack() as st2:
                wstg = Rot(nc, st2, "mwst", 2, [128, 4, 1024], F32)
                for n in range(3):
                    ws, kws = wstg.next()
                    em.dma("sp", ws[:], self.w_branch[l, n].rearrange("(c p) d -> p c d", p=128), writes=[kws])
                    self.cp("pool" if n % 2 else "act", wbr[:, n], ws[:], r=[kws], w=["mwbr"])
                wov = self.w_out[l].rearrange("(c p) e -> p c e", p=128)
                for hf in range(2):
                    ws, kws = wstg.next()
                    em.dma("sp", ws[:], wov[:, hf * 4:(hf + 1) * 4, :], writes=[kws])
                    self.cp("pool" if hf else "act", wo[:, hf * 4:(hf + 1) * 4, :], ws[:], r=[kws], w=["mwo"])
                em.flush()
            yin = Rot(nc, st, "myin", 2, [128, 3, 4, 512], BF16)
            mTr = Rot(nc, st, "mmT", 2, [128, 8, 512], BF16)
            gtr = Rot(nc, st, "mgt", 4, [128, 512], F32)
            accr = Rot(nc, st, "macc", 2, [128, 512], F32)
            tmr = Rot(nc, st, "mtm", 2, [128, 512], F32)
            xr = Rot(nc, st, "mx", 2, [128, 1024], F32)
            prer = Rot(nc, st, "mpre", 1, [128, 1024], F32)
            xor_ = Rot(nc, st, "mxo", 2, [128, 1024], F32)
            for tr_ in range(T // 512):
                tsl = slice(tr_ * 512, (tr_ + 1) * 512)
                yi, kyi = yin.next()
                for n in range(3):
                    em.dma("sp", yi[:, n], self.yT[n, :, tsl].rearrange("(c p) t -> p c t", p=128), writes=[kyi])
                mT, kmT = mTr.next()
                for db in range(8):
                    acc, kacc = accr.next()
                    for n in range(3):
                        ps, kps = self.psb()
                        for cc in range(4):
                            self.mm(ps[:], wbr[:, n, cc, db * 128:(db + 1) * 128], yi[:, n, cc, :],
                                    start=(cc == 0), stop=(cc == 3), r=["mwbr", kyi], w=[kps])
                        gt_, kgt = gtr.next()
                        em.dma("sp", gt_[:], self.gT[(n * 8 + db) * 128:(n * 8 + db + 1) * 128, tsl], writes=[kgt])
                        if n == 0:
                            self.tt("dve", acc[:], ps[:], gt_[:], ALU.mult, r=[kps, kgt], w=[kacc])
                        else:
                            tm, ktm = tmr.next()
                            self.tt("dve", tm[:], ps[:], gt_[:], ALU.mult, r=[kps, kgt], w=[ktm])
                            if n == 1:
                                self.tt("pool", acc[:], acc[:], tm[:], ALU.add, r=[kacc, ktm], w=[kacc])
                            else:
                                self.tt("pool", mT[:, db, :], acc[:], tm[:], ALU.add, r=[kacc, ktm], w=[kmT])
                for t4 in range(4):
                    tt = tr_ * 4 + t4
                    xt, kx = xr.next()
                    em.dma("sp", xt[:], self.x_tm[tt * 128:(tt + 1) * 128, :], writes=[kx])
                    pre, kpre = prer.next()
                    for hf in range(2):
                        ps, kps = self.psb()
                        for dc in range(8):
                            self.mm(ps[:], mT[:, dc, t4 * 128:(t4 + 1) * 128], wo[:, dc, hf * 512:(hf + 1) * 512],
                                    start=(dc == 0), stop=(dc == 7), r=[kmT, "mwo"], w=[kps])
                        self.stt(pre[:, hf * 512:(hf + 1) * 512], xt[:, hf * 512:(hf + 1) * 512], ALPHA, ps[:],
                                 ALU.mult, ALU.add, r=[kx, kps], w=[kpre])
                    xo, kxo = xor_.next()
                    self.ln_tile(pre, kpre, xo, kxo, g_bc, b_bc, "mlngb")
                    em.dma("sp", self.x1_tm[tt * 128:(tt + 1) * 128, :], xo[:], reads=[kxo], writes=[("x1_tm", tt)])
                    self.transpose_into_xT(xo, kxo, tt)
            em.flush()

    def stage_ffn(self, l, last):
        nc, em, T = self.nc, self.em, self.T
        with contextlib.ExitStack() as st:
            s32 = Rot(nc, st, "fs32", 2, [128, 4096], F32)
            s16 = Rot(nc, st, "fs16", 2, [128, 4096], BF16)
            wuv = self.w_up[l].rearrange("(c p) f -> p c f", p=128)
            wdv = self.w_down[l].rearrange("(fc p) e -> p fc e", p=128)
            engs = ("pool", "act", "dve")
            for i in range(8):
                a, ka = s32.next(); bq, kb_ = s16.next()
                a3 = a[:].rearrange("p (c f) -> p c f", c=8)
                b3 = bq[:].rearrange("p (c f) -> p c f", c=8)
                em.dma("sp", a3, wuv[:, :, i * 512:(i + 1) * 512], writes=[ka])
                self.cp(engs[i % 3], bq[:], a[:], r=[ka], w=[kb_])
                em.dma("sp", self.wup_bf[:, :, i * 512:(i + 1) * 512], b3, reads=[kb_], writes=[("wup_bf", i)])
            for eb in range(8):
                a, ka = s32.next(); bq, kb_ = s16.next()
                a3 = a[:].rearrange("p (c f) -> p c f", c=32)
                b3 = bq[:].rearrange("p (c f) -> p c f", c=32)
                em.dma("sp", a3, wdv[:, :, eb * 128:(eb + 1) * 128], writes=[ka])
                self.cp(engs[(eb + 2) % 3], bq[:], a[:], r=[ka], w=[kb_])
                em.dma("sp", self.wdn_bf[eb], b3, reads=[kb_], writes=[("wdn_bf", eb)])
            em.flush()
        with contextlib.ExitStack() as st:
            g_bc = st.enter_context(nc.sbuf_tensor(_nm("flng"), [128, 1024], F32))
            b_bc = st.enter_context(nc.sbuf_tensor(_nm("flnb"), [128, 1024], F32))
            self.load_bc(g_bc, "flngb", self.ln2_g[l])
            self.load_bc(b_bc, "flngb", self.ln2_b[l])
            hidT = st.enter_context(nc.sbuf_tensor(_nm("fhid"), [128, 32, 512], BF16))
            outT = st.enter_context(nc.sbuf_tensor(_nm("fout"), [128, 8, 512], F32))
            wur = Rot(nc, st, "fwu", 2, [128, 8, 512], BF16)
            wdr = Rot(nc, st, "fwd", 2, [128, 32, 128], BF16)
            rlr = Rot(nc, st, "frl", 3, [128, 512], F32)
            xr = Rot(nc, st, "fx", 2, [128, 1024], F32)
            prer = Rot(nc, st, "fpre", 2, [128, 1024], F32)
            xor_ = Rot(nc, st, "fxo", 2, [128, 1024], F32)
            for tg in range(T // 512):
                tsl = slice(tg * 512, (tg + 1) * 512)
                xk = [("xT", tg * 4 + i) for i in range(4)]
                for fg in range(8):
                    wu, kwu = wur.next()
                    em.dma("sp", wu[:], self.wup_bf[:, :, fg * 512:(fg + 1) * 512], writes=[kwu])
                    for f4 in range(4):
                        fb = fg * 4 + f4
                        ps, kps = self.psb()
                        for dc in range(8):
                            self.mm(ps[:], wu[:, dc, f4 * 128:(f4 + 1) * 128], self.xT[:, dc, tsl],
                                    start=(dc == 0), stop=(dc == 7), r=[kwu] + xk, w=[kps])
                        rl, krl = rlr.next()
                        self.act(rl[:], ps[:], AF.Relu, r=[kps], w=[krl])
                        self.tt("pool" if fb % 2 else "dve", hidT[:, fb, :], rl[:], rl[:], ALU.mult, r=[krl], w=[("fhid", fb)])
                for eb in range(8):
                    wd, kwd = wdr.next()
                    em.dma("sp", wd[:], self.wdn_bf[eb], writes=[kwd])
                    ps, kps = self.psb()
                    for fc in range(32):
                        self.mm(ps[:], wd[:, fc, :], hidT[:, fc, :], start=(fc == 0), stop=(fc == 31),
                                r=[kwd, ("fhid", fc)], w=[kps])
                    self.cp("act" if eb % 2 else "dve", outT[:, eb, :], ps[:], r=[kps], w=[("fout", eb)])
                for t4 in range(4):
                    tt = tg * 4 + t4
                    xt, kx = xr.next()
                    em.dma("sp", xt[:], self.x1_tm[tt * 128:(tt + 1) * 128, :], writes=[kx])
                    pre, kpre = prer.next()
                    for hf in range(2):
                        ps, kps = self.psb()
                        for c4 in range(4):
                            self.tr(ps[:, c4 * 128:(c4 + 1) * 128], outT[:, hf * 4 + c4, t4 * 128:(t4 + 1) * 128],
                                    self.ident_f[:], r=[("fout", hf * 4 + c4), "ident"], w=[kps])
                        self.stt(pre[:, hf * 512:(hf + 1) * 512], xt[:, hf * 512:(hf + 1) * 512], ALPHA, ps[:],
                                 ALU.mult, ALU.add, r=[kx, kps], w=[kpre])
                    xo, kxo = xor_.next()
                    self.ln_tile(pre, kpre, xo, kxo, g_bc, b_bc, "flngb")
                    if last:
                        em.dma("sp", self.out[tt * 128:(tt + 1) * 128, :], xo[:], reads=[kxo], writes=[("out", tt)])
                    else:
                        em.dma("sp", self.x_tm[tt * 128:(tt + 1) * 128, :], xo[:], reads=[kxo], writes=[("x_tm", tt)])
                        self.transpose_into_xT(xo, kxo, tt)
            em.flush()

    def build(self):
        nc = self.nc
        self.declare()
        stages = ["ln_in", "proj", "pool", "attn", "rwkv", "merge", "ffn"]
        upto = self.upto
        with contextlib.ExitStack() as st:
            self.em = Em(nc, st)
            em = self.em
            self.psum = [st.enter_context(nc.psum_tensor("psb%d" % i, [128, 512], F32)) for i in range(8)]
            self.ident_f = st.enter_context(nc.sbuf_tensor(_nm("ident_f"), [128, 128], F32))
            self.xT = st.enter_context(nc.sbuf_tensor(_nm("xT"), [128, 8, self.T], BF16))
            self.pv = st.enter_context(nc.sbuf_tensor(_nm("pv"), [128, NPV], F32))
            self.lnst = Rot(nc, st, "lnst", 4, [128, 16], F32)
            self.band = st.enter_context(nc.sbuf_tensor(_nm("band"), [128, 4, 3, 128], F32))
            self.relb = st.enter_context(nc.sbuf_tensor(_nm("relb"), [128, 128], F32))
            self.epsv = st.enter_context(nc.sbuf_tensor(_nm("epsv"), [128, 2], F32))
            self.memset("dve", self.epsv[:, 0:1], 1e-5, w=["epsv"])
            self.memset("dve", self.epsv[:, 1:2], GN_EPS, w=["epsv"])
            self.build_band()
            em.dma("sp", self.ident_f[:], self.c_ident, writes=["ident"])
            self.stage_ln_in()
            done = upto == "ln_in"
            for l in range(self.L):
                if done:
                    break
                em.dma("sp", self.pv[:], self.pvec_d[l], writes=["pv"])
                self.stage_proj(l)
                if upto == "proj":
                    break
                self.stage_pool(l)
                if upto == "pool":
                    break
                self.stage_attn(l)
                if upto == "attn":
                    break
                self.stage_rwkv(l)
                if upto == "rwkv":
                    break
                self.stage_merge(l)
                if upto == "merge":
                    break
                self.stage_ffn(l, last=(l == self.L - 1))
            em.flush()
        return nc


def shared_in_map(inp, S, L):
    f = lambda a: np.ascontiguousarray(np.asarray(a, np.float32))
    m = {
        "ln0_g": f(inp["ln0_g"]), "ln0_b": f(inp["ln0_b"]),
        "w_in": f(inp["w_in"][:L]), "pool_w": f(inp["pool_w"][:L]),
        "da_lambda": f(inp["da_lambda"][:L]).reshape(L, 256),
        "rel_bias": f(inp["rel_bias"]).reshape(128),
        "rw_w2": f(inp["rw_w2"][:L]), "rw_a2": f(inp["rw_a2"][:L]), "rw_g2": f(inp["rw_g2"][:L]),
        "w_branch": f(inp["w_branch"][:L]), "w_out": f(inp["w_out"][:L]),
        "ln1_g": f(inp["ln1_g"][:L]), "ln1_b": f(inp["ln1_b"][:L]),
        "w_up": f(inp["w_up"][:L]), "w_down": f(inp["w_down"][:L]),
        "ln2_g": f(inp["ln2_g"][:L]), "ln2_b": f(inp["ln2_b"][:L]),
        "pvec": pack_pvec(inp, L),
    }
    m.update(host_consts(S))
    return m


_CACHE = {}


def kernel(**inputs):
    x = np.asarray(inputs["x"], np.float32)
    B, S, D = x.shape
    ncores = B // NSEQ
    key = (S, DEPTH)
    if key not in _CACHE:
        _CACHE[key] = Builder(S=S, L=DEPTH).build()
    nc = _CACHE[key]
    shared = shared_in_map(inputs, S, DEPTH)
    in_maps = []
    for c in range(ncores):
        m = dict(shared)
        m["x"] = np.ascontiguousarray(x[c * NSEQ:(c + 1) * NSEQ].reshape(NSEQ * S, D))
        in_maps.append(m)
    res = run_bass_kernel_spmd(nc, in_maps, core_ids=list(range(ncores)))
    out = np.stack([np.asarray(r["out"], np.float32).reshape(NSEQ, S, D) for r in res.results])
    return out.reshape(B, S, D)
```

```python
import math
import contextlib
import numpy as np
import concourse.bass as bass
import concourse.mybir as mybir
from concourse.bass_utils import run_bass_kernel_spmd

F32 = mybir.dt.float32
BF16 = mybir.dt.bfloat16
AF = mybir.ActivationFunctionType
ALU = mybir.AluOpType

D_MODEL = 1024
DEPTH = 4
NSEQ = 2
SEQ = 2048
D_IN = 7040
D_FF = 4096
ALPHA = (2.0 * DEPTH) ** 0.25
LN_EPS = 1e-5
GN_EPS = 64e-5
NPV = 96
C = 64
NCG = 4

ENGS = ("pe", "act", "dve", "pool", "sp")
N_DMA_SEMS = 24


class Em:
    def __init__(self, nc, stack):
        self.nc = nc
        self.sem = {e: stack.enter_context(nc.semaphore("s_" + e)) for e in ENGS}
        self.semobj = {"s_" + e: self.sem[e] for e in ENGS}
        for i in range(N_DMA_SEMS):
            self.semobj["d%d" % i] = stack.enter_context(nc.semaphore("d%d" % i))
        self.val = {k: 0 for k in self.semobj}
        self.seen = {e: {k: 0 for k in self.semobj} for e in ENGS}
        self.ops = {e: [] for e in ENGS}
        self.state = {}
        self.dma_rr = 0
        self.nops = 0

    def _deps(self, reads, writes):
        toks = []
        for k in reads:
            st = self.state.get(k)
            if st and st[0]:
                toks.append(st[0])
        for k in writes:
            st = self.state.get(k)
            if st:
                if st[0]:
                    toks.append(st[0])
                toks.extend(st[1])
        return toks

    def _commit(self, tok, reads, writes):
        for k in reads:
            st = self.state.setdefault(k, [None, []])
            st[1].append(tok)
            if len(st[1]) > 6:
                best = {}
                for s, v in st[1]:
                    if best.get(s, 0) < v:
                        best[s] = v
                st[1] = list(best.items())
        for k in writes:
            self.state[k] = [tok, []]

    def _waits(self, eng, toks):
        need = {}
        for s, v in toks:
            if eng == "pe" and s == "s_pe":
                continue
            if v > self.seen[eng][s] and v > need.get(s, 0):
                need[s] = v
        for s, v in need.items():
            self.seen[eng][s] = v
        return list(need.items())

    def op(self, eng, fn, reads=(), writes=()):
        waits = self._waits(eng, self._deps(reads, writes))
        s = "s_" + eng
        self.val[s] += 1
        tok = (s, self.val[s])
        self.ops[eng].append((waits, fn, s, 1))
        self._commit(tok, reads, writes)
        self.nops += 1
        return tok

    def dma(self, eng, out, in_, reads=(), writes=(), **kw):
        i = self.dma_rr
        self.dma_rr = (self.dma_rr + 1) % N_DMA_SEMS
        s = "d%d" % i
        toks = self._deps(reads, writes)
        toks.append((s, self.val[s]))
        waits = self._waits(eng, toks)
        self.val[s] += 16
        tok = (s, self.val[s])

        def fn(e, out=out, in_=in_, kw=kw):
            return e.dma_start(out=out, in_=in_, **kw)
        self.ops[eng].append((waits, fn, s, 16))
        self._commit(tok, reads, writes)
        self.nops += 1
        return tok

    def barrier(self):
        for e in ENGS:
            toks = [(s, v) for s, v in self.val.items() if v > 0]
            waits = self._waits(e, toks)
            if waits:
                self.ops[e].append((waits, None, None, 0))
        self.state = {}

    def flush(self):
        self.barrier()
        nc = self.nc
        with nc.Block() as block:
            def mk(ename):
                def body(e):
                    for waits, fn, s, inc in self.ops[ename]:
                        for ws, wv in waits:
                            e.wait_ge(self.semobj[ws], wv)
                        if fn is not None:
                            fn(e).then_inc(self.semobj[s], inc)
                return body
            block.tensor(mk("pe"))
            block.scalar(mk("act"))
            block.vector(mk("dve"))
            block.gpsimd(mk("pool"))
            block.sync(mk("sp"))
        self.ops = {e: [] for e in ENGS}


_UID = [0]


def _nm(name):
    _UID[0] += 1
    return "%s_u%d" % (name, _UID[0])


class Rot:
    def __init__(self, nc, stack, name, n, shape, dtype):
        self.tiles = [stack.enter_context(nc.sbuf_tensor(_nm("%s_%d") % (name, i), shape, dtype))
                      for i in range(n)]
        self.name = name
        self.i = 0

    def next(self):
        t = self.tiles[self.i]
        k = (self.name, self.i)
        self.i = (self.i + 1) % len(self.tiles)
        return t, k


def _rel_bucket_np(rel):
    half = 16
    max_exact = 8
    n = np.abs(rel)
    nf = np.maximum(n, 1).astype(np.float32)
    large = max_exact + (np.log(nf / np.float32(max_exact)) / np.float32(math.log(128 / max_exact))
                         * np.float32(half - max_exact)).astype(np.int32)
    large = np.minimum(large, half - 1)
    return np.where(rel > 0, half, 0) + np.where(n < max_exact, n, large)


def host_consts(S):
    cs = {}
    cs["ident"] = np.eye(128, dtype=np.float32)
    p = np.arange(128)[:, None]
    f = np.arange(128)[None, :]
    cs["relidx"] = np.stack([_rel_bucket_np(128 * d + p - f) for d in (-1, 0, 1)]).astype(np.float32)
    t = np.arange(S)
    ic = []
    for win in (2, 4, 8, 16):
        left = win // 2
        right = win - 1 - left
        hi = np.minimum(t + right + 1, S)
        lo = np.maximum(t - left, 0)
        ic.append(1.0 / (hi - lo).astype(np.float32))
    cs["pool_icnt"] = np.stack(ic).astype(np.float32)
    r = np.arange(128)
    rl = r % 64
    same = (r[:, None] // 64) == (r[None, :] // 64)
    SL = (same & (rl[:, None] > rl[None, :])).astype(np.float32)
    SU = (same & (rl[:, None] < rl[None, :])).astype(np.float32)
    c64 = np.arange(64)
    IU = (rl[:, None] <= c64[None, :]).astype(np.float32)
    IL = (rl[:, None] >= c64[None, :]).astype(np.float32)
    cs["wkvmask"] = np.concatenate([SL, SU, IU, IL], axis=1).astype(np.float32)
    cs["bones"] = same.astype(np.float32)
    hm = np.zeros((128, 2), np.float32)
    hm[:64, 0] = 1.0
    hm[64:, 1] = 1.0
    cs["hmask"] = hm
    m = np.ones((128, S), np.float32)
    m[:, ::C] = 0.0
    cs["scanmask"] = m
    return cs


def pack_pvec(inp, L):
    out = np.zeros((L, 128, NPV), np.float32)
    for l in range(L):
        col = 0

        def put(vec):
            nonlocal col
            v = np.asarray(vec, np.float32).reshape(-1, 128)
            out[l, :, col:col + v.shape[0]] = v.T
            col += v.shape[0]
        put(inp["b_gate"][l].reshape(-1))
        put(inp["pool_scale"][l])
        put(inp["rw_mu_prev"][l])
        put(inp["rw_mu_next"][l])
        put(inp["rw_w0"][l].reshape(-1))
        put(inp["rw_a0"][l].reshape(-1))
        put(inp["rw_k_k"][l])
        put(inp["rw_k_a"][l])
        put(inp["rw_r_k"][l])
        put(inp["rw_gn_g"][l])
        put(inp["rw_gn_b"][l])
        put(inp["da_subln_g"][l])
        assert col == 95
    return out


PV_BG, PV_PS, PV_MUP, PV_MUN, PV_W0, PV_A0, PV_KK, PV_KA, PV_RK, PV_GG, PV_GB, PV_SG = \
    0, 24, 28, 43, 58, 66, 74, 78, 82, 86, 90, 94


class Builder:
    def __init__(self, S=SEQ, L=DEPTH, upto="all", dump=(), dbg=0, wd=BF16):
        self.dbg = dbg
        self.wd = wd
        self.S, self.L, self.upto, self.dump = S, L, upto, set(dump)
        self.T = NSEQ * S
        self.nc = bass.Bass("TRN2", target_bir_lowering=False)
        self.ps_i = 0

    def din(self, name, shape, dtype=F32):
        return self.nc.dram_tensor(name, list(shape), dtype, kind="ExternalInput").ap()

    def dscr(self, name, shape, dtype=F32):
        kind = "ExternalOutput" if name in self.dump else "Internal"
        return self.nc.dram_tensor(name, list(shape), dtype, kind=kind).ap()

    def psb(self):
        t = self.psum[self.ps_i]
        k = ("ps", self.ps_i)
        self.ps_i = (self.ps_i + 1) % 8
        return t, k

    def mm(self, out, lhsT, rhs, start=True, stop=True, r=(), w=()):
        return self.em.op("pe", lambda e: e.matmul(out, lhsT=lhsT, rhs=rhs, start=start, stop=stop,
                                                   skip_group_check=True), r, w)

    def tr(self, out, in_, ident, r=(), w=()):
        return self.em.op("pe", lambda e: e.transpose(out, in_, ident), r, w)

    def act(self, out, in_, func, bias=None, scale=None, r=(), w=()):
        kw = {}
        if bias is not None:
            kw["bias"] = bias
        if scale is not None:
            kw["scale"] = scale
        return self.em.op("act", lambda e: e.activation(out=out, in_=in_, func=func, **kw), r, w)

    def tt(self, eng, out, in0, in1, op, r=(), w=()):
        return self.em.op(eng, lambda e: e.tensor_tensor(out=out, in0=in0, in1=in1, op=op), r, w)

    def ts(self, eng, out, in0, s1, s2, op0, op1=None, r=(), w=()):
        if op1 is None:
            return self.em.op(eng, lambda e: e.tensor_scalar(out=out, in0=in0, scalar1=s1, scalar2=None,
                                                             op0=op0), r, w)
        return self.em.op(eng, lambda e: e.tensor_scalar(out=out, in0=in0, scalar1=s1, scalar2=s2,
                                                         op0=op0, op1=op1), r, w)

    def stt(self, out, in0, scalar, in1, op0, op1, r=(), w=()):
        return self.em.op("dve", lambda e: e.scalar_tensor_tensor(out=out, in0=in0, scalar=scalar, in1=in1,
                                                                  op0=op0, op1=op1), r, w)

    def cp(self, eng, out, in_, r=(), w=()):
        if eng == "act":
            return self.em.op("act", lambda e: e.copy(out=out, in_=in_), r, w)
        return self.em.op(eng, lambda e: e.tensor_copy(out=out, in_=in_), r, w)

    def memset(self, eng, ap, val, w=()):
        return self.em.op(eng, lambda e: e.memset(ap, val), (), w)

    def recip(self, out, in_, r=(), w=()):
        return self.em.op("dve", lambda e: e.reciprocal(out=out, in_=in_), r, w)

    def declare(self):
        S, T, L = self.S, self.T, self.L
        d = self.din
        self.x = d("x", [T, D_MODEL])
        self.ln0_g = d("ln0_g", [D_MODEL]); self.ln0_b = d("ln0_b", [D_MODEL])
        self.w_in = d("w_in", [L, D_MODEL, D_IN])
        self.pool_w = d("pool_w", [L, 4, 128, 128])
        self.da_lambda = d("da_lambda", [L, 256])
        self.rel_bias = d("rel_bias", [128])
        self.rw_w2 = d("rw_w2", [L, 2, 64, 512]); self.rw_a2 = d("rw_a2", [L, 2, 64, 512])
        self.rw_g2 = d("rw_g2", [L, 128, 512])
        self.w_branch = d("w_branch", [L, 3, 512, D_MODEL])
        self.w_out = d("w_out", [L, D_MODEL, D_MODEL])
        self.ln1_g = d("ln1_g", [L, D_MODEL]); self.ln1_b = d("ln1_b", [L, D_MODEL])
        self.w_up = d("w_up", [L, D_MODEL, D_FF]); self.w_down = d("w_down", [L, D_FF, D_MODEL])
        self.ln2_g = d("ln2_g", [L, D_MODEL]); self.ln2_b = d("ln2_b", [L, D_MODEL])
        self.pvec_d = d("pvec", [L, 128, NPV])
        self.c_ident = d("ident", [128, 128]); self.c_relidx = d("relidx", [3, 128, 128])
        self.c_icnt = d("pool_icnt", [4, S]); self.c_wkvmask = d("wkvmask", [128, 384])
        self.c_bones = d("bones", [128, 128]); self.c_hmask = d("hmask", [128, 2])
        self.c_scanmask = d("scanmask", [128, S])
        self.out = self.nc.dram_tensor("out", [T, D_MODEL], F32, kind="ExternalOutput").ap()
        s = self.dscr
        self.x_tm = s("x_tm", [T, D_MODEL])
        self.x1_tm = s("x1_tm", [T, D_MODEL])
        self.uT = s("uT", [512, T]); self.qT = s("qT", [512, T], BF16); self.kT = s("kT", [512, T], BF16)
        self.v_tm = s("v_tm", [T, 512], BF16)
        self.zrT = s("zrT", [1920, T]); self.gT = s("gT", [3072, T])
        self.yT = s("yT", [3, 512, T], BF16)
        self.shT = s("shT", [3, 128, T])
        self.wup_bf = s("wup_bf", [128, 8, D_FF], BF16)
        self.wdn_bf = s("wdn_bf", [8, 128, 32, 128], BF16)

    def ln_tile(self, src, ksrc, dst, kdst, g_bc, b_bc, kgb):
        st, kst = self.lnst.next()
        em = self.em
        em.op("dve", lambda e: e.bn_stats(out=st[:, 0:6], in_=src[:, 0:512]), [ksrc], [kst])
        em.op("dve", lambda e: e.bn_stats(out=st[:, 6:12], in_=src[:, 512:1024]), [ksrc], [kst])
        em.op("dve", lambda e: e.bn_aggr(out=st[:, 12:14], in_=st[:, 0:12]), [kst], [kst])
        self.ts("dve", st[:, 14:15], st[:, 13:14], LN_EPS, None, ALU.add, r=[kst], w=[kst])
        self.act(st[:, 14:15], st[:, 14:15], AF.Sqrt, r=[kst], w=[kst])
        self.recip(st[:, 14:15], st[:, 14:15], r=[kst], w=[kst])
        self.ts("dve", dst[:], src[:], st[:, 12:13], st[:, 14:15], ALU.subtract, ALU.mult, r=[ksrc, kst], w=[kdst])
        self.tt("dve", dst[:], dst[:], g_bc[:], ALU.mult, r=[kdst, kgb], w=[kdst])
        self.tt("pool", dst[:], dst[:], b_bc[:], ALU.add, r=[kdst, kgb], w=[kdst])

    def transpose_into_xT(self, src, ksrc, tt):
        for half in range(2):
            ps, kps = self.psb()
            for c4 in range(4):
                c = half * 4 + c4
                self.tr(ps[:, c4 * 128:(c4 + 1) * 128], src[:, c * 128:(c + 1) * 128], self.ident_f[:],
                        r=[ksrc, "ident"], w=[kps])
            eng = "act" if half == 0 else "dve"
            self.cp(eng, self.xT[:, half * 4:half * 4 + 4, tt * 128:(tt + 1) * 128],
                    ps[:].rearrange("p (c t) -> p c t", c=4), r=[kps], w=[("xT", tt)])

    def load_bc(self, tile, key, vec_ap):
        self.em.dma("sp", tile[:], vec_ap.partition_broadcast(128), writes=[key])

    def stage_ln_in(self):
        nc, em = self.nc, self.em
        with contextlib.ExitStack() as st:
            xin = Rot(nc, st, "lnx", 2, [128, 1024], F32)
            xo = Rot(nc, st, "lno", 2, [128, 1024], F32)
            g_bc = st.enter_context(nc.sbuf_tensor(_nm("lng"), [128, 1024], F32))
            b_bc = st.enter_context(nc.sbuf_tensor(_nm("lnb"), [128, 1024], F32))
            self.load_bc(g_bc, "lngb", self.ln0_g)
            self.load_bc(b_bc, "lngb", self.ln0_b)
            for tt in range(self.T // 128):
                xt, kx = xin.next()
                em.dma("sp", xt[:], self.x[tt * 128:(tt + 1) * 128, :], writes=[kx])
                xn, kn = xo.next()
                self.ln_tile(xt, kx, xn, kn, g_bc, b_bc, "lngb")
                em.dma("sp", self.x_tm[tt * 128:(tt + 1) * 128, :], xn[:], reads=[kn], writes=[("x_tm", tt)])
                self.transpose_into_xT(xn, kn, tt)
            em.flush()

    def stage_proj(self, l):
        nc, em, T = self.nc, self.em, self.T
        with contextlib.ExitStack() as st:
            wst = Rot(nc, st, "wst", 2, [128, 8, 512], F32)
            wbf = Rot(nc, st, "wbf", 2, [128, 8, 512], BF16)
            ev32 = Rot(nc, st, "ev32", 4, [128, 512], F32)
            ev16 = Rot(nc, st, "ev16", 4, [128, 512], BF16)
            wv = self.w_in[l].rearrange("(c p) n -> p c n", p=128)
            ntr = T // 512
            xkeys = [("xT", tt) for tt in range(T // 128)]
            for cg in range(14):
                c0 = cg * 512
                cw = min(512, D_IN - c0)
                ws, wsk = wst.next()
                em.dma("sp", ws[:, :, 0:cw], wv[:, :, c0:c0 + cw], writes=[wsk])
                wb, wbk = wbf.next()
                self.cp("pool", wb[:, :, 0:cw], ws[:, :, 0:cw], r=[wsk], w=[wbk])
                if cg == 3:
                    for tt in range(T // 128):
                        ps, kps = self.psb()
                        for dc in range(8):
                            self.mm(ps[:], self.xT[:, dc, tt * 128:(tt + 1) * 128], wb[:, dc, :],
                                    start=(dc == 0), stop=(dc == 7), r=[wbk, ("xT", tt)], w=[kps])
                        ev, kev = ev16.next()
                        self.cp("dve" if tt % 2 else "act", ev[:], ps[:], r=[kps], w=[kev])
                        em.dma("sp", self.v_tm[tt * 128:(tt + 1) * 128, :], ev[:], reads=[kev], writes=[("v_tm", tt)])
                    continue
                for cb in range(cw // 128):
                    gb = cg * 4 + cb
                    for tr_ in range(ntr):
                        ps, kps = self.psb()
                        xk = xkeys[tr_ * 4:(tr_ + 1) * 4]
                        for dc in range(8):
                            self.mm(ps[:], wb[:, dc, cb * 128:(cb + 1) * 128], self.xT[:, dc, tr_ * 512:(tr_ + 1) * 512],
                                    start=(dc == 0), stop=(dc == 7), r=[wbk] + xk, w=[kps])
                        tsl = slice(tr_ * 512, (tr_ + 1) * 512)
                        if gb < 4:
                            ev, kev = ev32.next()
                            self.cp("dve", ev[:], ps[:], r=[kps], w=[kev])
                            dst = self.uT[gb * 128:(gb + 1) * 128, tsl]
                        elif gb < 8:
                            ev, kev = ev16.next()
                            self.ts("dve", ev[:], ps[:], 0.125, None, ALU.mult, r=[kps], w=[kev])
                            dst = self.qT[(gb - 4) * 128:(gb - 3) * 128, tsl]
                        elif gb < 12:
                            ev, kev = ev16.next()
                            self.cp("dve", ev[:], ps[:], r=[kps], w=[kev])
                            dst = self.kT[(gb - 8) * 128:(gb - 7) * 128, tsl]
                        elif gb < 31:
                            ev, kev = ev32.next()
                            self.cp("dve" if (gb + tr_) % 2 else "act", ev[:], ps[:], r=[kps], w=[kev])
                            dst = self.zrT[(gb - 16) * 128:(gb - 15) * 128, tsl]
                        else:
                            ev, kev = ev32.next()
                            gi = gb - 31
                            self.act(ev[:], ps[:], AF.Sigmoid, bias=self.pv[:, PV_BG + gi:PV_BG + gi + 1],
                                     r=[kps, "pv"], w=[kev])
                            dst = self.gT[gi * 128:(gi + 1) * 128, tsl]
                        em.dma("sp", dst, ev[:], reads=[kev], writes=[("scr", gb, tr_)])
            em.flush()

    def stage_pool(self, l):
        nc, em, S = self.nc, self.em, self.S
        with contextlib.ExitStack() as st:
            pw32 = st.enter_context(nc.sbuf_tensor(_nm("pw32"), [128, 4, 128], F32))
            pwbf = st.enter_context(nc.sbuf_tensor(_nm("pwbf"), [128, 4, 128], BF16))
            em.dma("sp", pw32[:], self.pool_w[l].rearrange("g c d -> c g d"), writes=["pw32"])
            self.cp("dve", pwbf[:], pw32[:], r=["pw32"], w=["pwbf"])
            icr = Rot(nc, st, "icnt", 2, [128, S], F32)
            ub = Rot(nc, st, "ub", 2, [128, S + 32], F32)
            sa = Rot(nc, st, "psa", 2, [128, S + 32], F32)
            sb = Rot(nc, st, "psb_", 2, [128, S + 32], F32)
            mxr = Rot(nc, st, "pmx", 2, [128, S], BF16)
            evr = Rot(nc, st, "pev", 3, [128, 512], BF16)
            for t_, k_ in zip(ub.tiles, [("ub", 0), ("ub", 1)]):
                self.memset("pool", t_[:], 0.0, w=[k_])
            for g in range(4):
                ic, kic = icr.next()
                em.dma("sp", ic[:], self.c_icnt[g:g + 1, :].partition_broadcast(128), writes=[kic])
                win = 2 ** (g + 1)
                right = win - 1 - win // 2
                for b in range(NSEQ):
                    u, ku = ub.next()
                    em.dma("sp", u[:, 16:16 + S], self.uT[g * 128:(g + 1) * 128, b * S:(b + 1) * S], writes=[ku])
                    cur, kcur = u, ku
                    for k in range(g + 1):
                        sh = 2 ** k
                        lo = 2 ** (k + 1) - 1
                        nxt, knxt = (sa if k % 2 == 0 else sb).next()
                        self.tt("pool" if k % 2 else "dve", nxt[:, lo:S + 32], cur[:, lo:S + 32],
                                cur[:, lo - sh:S + 32 - sh], ALU.add, r=[kcur], w=[knxt])
                        cur, kcur = nxt, knxt
                    tmp, ktmp = (sa if (g + 1) % 2 == 0 else sb).next()
                    self.tt("dve", tmp[:, 0:S], cur[:, 16 + right:16 + right + S], ic[:], ALU.mult,
                            r=[kcur, kic], w=[ktmp])
                    mx, kmx = mxr.next()
                    self.tt("pool", mx[:], tmp[:, 0:S], u[:, 16:16 + S], ALU.subtract, r=[ktmp, ku], w=[kmx])
                    for tr_ in range(S // 512):
                        ps, kps = self.psb()
                        self.mm(ps[:], pwbf[:, g, :], mx[:, tr_ * 512:(tr_ + 1) * 512], r=["pwbf", kmx], w=[kps])
                        ev, kev = evr.next()
                        self.ts("dve", ev[:], ps[:], self.pv[:, PV_PS + g:PV_PS + g + 1], None, ALU.mult,
                                r=[kps, "pv"], w=[kev])
                        em.dma("sp", self.yT[0, g * 128:(g + 1) * 128, b * S + tr_ * 512:b * S + (tr_ + 1) * 512],
                               ev[:], reads=[kev], writes=[("yT0", g, b, tr_)])
            em.flush()

    def build_band(self):
        nc, em = self.nc, self.em
        with contextlib.ExitStack() as st:
            idx = st.enter_context(nc.sbuf_tensor(_nm("ridx"), [128, 3, 128], F32))
            eq = Rot(nc, st, "req", 2, [128, 128], F32)
            em.dma("sp", idx[:], self.c_relidx.rearrange("d p f -> p d f"), writes=["ridx"])
            em.dma("sp", self.relb[:], self.rel_bias.partition_broadcast(128), writes=["relb"])
            self.memset("dve", self.band[:], 0.0, w=["band"])
            for d3 in range(3):
                for bk in range(32):
                    e_, ke = eq.next()
                    self.ts("pool", e_[:], idx[:, d3, :], float(bk), None, ALU.is_equal, r=["ridx"], w=[ke])
                    for h in range(4):
                        self.stt(self.band[:, h, d3, :], e_[:], self.relb[:, bk * 4 + h:bk * 4 + h + 1],
                                 self.band[:, h, d3, :], ALU.mult, ALU.add, r=[ke, "relb", "band"], w=["band"])
            em.flush()

    def stage_attn(self, l):
        nc, em, S = self.nc, self.em, self.S
        nqr, nkb = S // 512, S // 128
        lam_init = 0.8 - 0.6 * math.exp(-0.3 * l)
        with contextlib.ExitStack() as st:
            strips = st.enter_context(nc.sbuf_tensor(_nm("strips"), [128, 4, 6, 512], F32))
            zeros = st.enter_context(nc.sbuf_tensor(_nm("azero"), [128, 128], F32))
            ones_b = st.enter_context(nc.sbuf_tensor(_nm("aones_b"), [128, 128], BF16))
            ones_f = st.enter_context(nc.sbuf_tensor(_nm("aones_f"), [128, 128], F32))
            lamv = st.enter_context(nc.sbuf_tensor(_nm("lamv"), [128, 256], F32))
            lsm = st.enter_context(nc.sbuf_tensor(_nm("lsm"), [128, 8], F32))
            self.memset("dve", zeros[:], 0.0, w=["azero"])
            self.memset("dve", ones_b[:], 1.0, w=["aones_b"])
            self.memset("dve", ones_f[:], 1.0, w=["aones_f"])
            for h in range(4):
                for dl in range(-1, 5):
                    for bq in range(4):
                        d = dl - bq
                        dst = strips[:, h, dl + 1, bq * 128:(bq + 1) * 128]
                        if abs(d) <= 1:
                            self.cp("pool", dst, self.band[:, h, d + 1, :], r=["band"], w=["strips"])
                        else:
                            col = (31 if d >= 2 else 15) * 4 + h
                            self.ts("dve", dst, zeros[:], self.relb[:, col:col + 1], None, ALU.add,
                                    r=["azero", "relb"], w=["strips"])
            em.dma("sp", lamv[:], self.da_lambda[l:l + 1, :].partition_broadcast(128), writes=["lamv"])
            self.tt("dve", lamv[:, 0:64], lamv[:, 0:64], lamv[:, 64:128], ALU.mult, r=["lamv"], w=["lamv"])
            self.tt("dve", lamv[:, 128:192], lamv[:, 128:192], lamv[:, 192:256], ALU.mult, r=["lamv"], w=["lamv"])
            em.op("dve", lambda e: e.tensor_reduce(out=lsm[:, 0:1], in_=lamv[:, 0:64], axis=mybir.AxisListType.X,
                                                   op=ALU.add), ["lamv"], ["lsm"])
            em.op("dve", lambda e: e.tensor_reduce(out=lsm[:, 1:2], in_=lamv[:, 128:192], axis=mybir.AxisListType.X,
                                                   op=ALU.add), ["lamv"], ["lsm"])
            self.act(lsm[:, 2:4], lsm[:, 0:2], AF.Exp, r=["lsm"], w=["lsm"])
            self.tt("dve", lsm[:, 4:5], lsm[:, 3:4], lsm[:, 2:3], ALU.subtract, r=["lsm"], w=["lsm"])
            self.ts("dve", lsm[:, 4:5], lsm[:, 4:5], -lam_init, None, ALU.add, r=["lsm"], w=["lsm"])
            self.ts("dve", lsm[:, 5:6], self.pv[:, PV_SG:PV_SG + 1], 1.0 - lam_init, None, ALU.mult,
                    r=["pv", "lsm"], w=["lsm"])
            neglam = lsm[:, 4:5]
            sgl = lsm[:, 5:6]

            qr = Rot(nc, st, "aq", 2, [128, S], BF16)
            kr = Rot(nc, st, "ak", 2, [128, S], BF16)
            vr = Rot(nc, st, "av", 2, [128, nkb, 128], BF16)
            ptr = Rot(nc, st, "apt", 5, [128, 512], BF16)
            tmr = Rot(nc, st, "atm", 3, [128, 512], F32)
            e1 = Rot(nc, st, "ae1", 2, [128, 512], F32)
            e2 = Rot(nc, st, "ae2", 2, [128, 512], F32)
            e3 = Rot(nc, st, "ae3", 2, [128, 512], F32)
            e4 = Rot(nc, st, "ae4", 2, [128, 512], F32)
            yo = Rot(nc, st, "ayo", 2, [128, 512], BF16)
            P = self.psum
            sbank = 0
            for b in range(NSEQ):
                for h in range(4):
                    qh, kq = qr.next(); kh, kk_ = kr.next(); vh, kv = vr.next()
                    em.dma("sp", qh[:], self.qT[h * 128:(h + 1) * 128, b * S:(b + 1) * S], writes=[kq])
                    em.dma("sp", kh[:], self.kT[h * 128:(h + 1) * 128, b * S:(b + 1) * S], writes=[kk_])
                    em.dma("sp", vh[:], self.v_tm[b * S:(b + 1) * S, h * 128:(h + 1) * 128]
                           .rearrange("(j p) c -> p j c", p=128), writes=[kv])
                    for i in range(nqr):
                        tiles = [(m, j) for m in range(2) for j in range(nkb)]
                        LOOK = 2
                        pts = []

                        def emit_qk(m, j, i=i, h=h, qh=qh, kh=kh, kq=kq, kk_=kk_):
                            nonlocal sbank
                            sb_i = 4 + sbank
                            sbank = (sbank + 1) % 3
                            ps, kps = P[sb_i], ("ps", sb_i)
                            self.mm(ps[:], kh[m * 64:(m + 1) * 64, j * 128:(j + 1) * 128],
                                    qh[m * 64:(m + 1) * 64, i * 512:(i + 1) * 512], r=[kk_, kq], w=[kps])
                            dl = j - 4 * i
                            pt, kpt = ptr.next()
                            if -1 <= dl <= 4:
                                tm, ktm = tmr.next()
                                self.tt("dve", tm[:], ps[:], strips[:, h, dl + 1, :], ALU.add,
                                        r=[kps, "strips"], w=[ktm])
                                self.act(pt[:], tm[:], AF.Exp, r=[ktm], w=[kpt])
                            else:
                                col = (31 if dl >= 5 else 15) * 4 + h
                                self.act(pt[:], ps[:], AF.Exp, bias=self.relb[:, col:col + 1],
                                         r=[kps, "relb"], w=[kpt])
                            return pt, kpt

                        def emit_pv(m, j, pt, kpt, vh=vh, kv=kv):
                            O, kO = P[2 * m], ("ps", 2 * m)
                            Z, kZ = P[2 * m + 1], ("ps", 2 * m + 1)
                            self.mm(O[:], vh[:, j, :], pt[:], start=(j == 0), stop=(j == nkb - 1),
                                    r=[kv, kpt], w=[kO])
                            self.mm(Z[:], ones_b[:], pt[:], start=(j == 0), stop=(j == nkb - 1),
                                    r=["aones_b", kpt], w=[kZ])
                        for t_ in range(len(tiles) + LOOK):
                            if t_ < len(tiles):
                                pts.append(emit_qk(*tiles[t_]))
                            if t_ >= LOOK:
                                emit_pv(*tiles[t_ - LOOK], *pts[t_ - LOOK])
                        a1, ka1 = e1.next(); a2, ka2 = e2.next(); a3, ka3 = e3.next(); a4, ka4 = e4.next()
                        self.cp("act", a2[:], P[0][:], r=[("ps", 0)], w=[ka2])
                        self.recip(a1[:], P[1][:], r=[("ps", 1)], w=[ka1])
                        self.cp("act", a3[:], P[2][:], r=[("ps", 2)], w=[ka3])
                        self.recip(a4[:], P[3][:], r=[("ps", 3)], w=[ka4])
                        self.tt("pool", a2[:], a2[:], a1[:], ALU.mult, r=[ka2, ka1], w=[ka2])
                        self.tt("pool", a3[:], a3[:], a4[:], ALU.mult, r=[ka3, ka4], w=[ka3])
                        self.stt(a2[:], a3[:], neglam, a2[:], ALU.mult, ALU.add, r=[ka3, ka2, "lsm"], w=[ka2])
                        self.tt("pool", a3[:], a2[:], a2[:], ALU.mult, r=[ka2, ka3], w=[ka3])
                        self.mm(P[7][:], ones_f[:], a3[:], r=["aones_f", ka3], w=[("ps", 7)])
                        self.act(a1[:], P[7][:], AF.Sqrt, bias=self.epsv[:, 0:1], scale=1.0 / 128.0,
                                 r=[("ps", 7), ka1, "epsv"], w=[ka1])
                        self.recip(a1[:], a1[:], r=[ka1], w=[ka1])
                        y_, ky = yo.next()
                        self.stt(y_[:], a2[:], sgl, a1[:], ALU.mult, ALU.mult, r=[ka2, ka1, "lsm"], w=[ky])
                        em.dma("sp", self.yT[1, h * 128:(h + 1) * 128, b * S + i * 512:b * S + (i + 1) * 512],
                               y_[:], reads=[ky], writes=[("yT1", h, b, i)])
            em.flush()

    def stage_rwkv(self, l):
        nc, em, S = self.nc, self.em, self.S
        WD = self.wd
        GT = NCG * C
        ngr = S // GT
        n512 = S // 512
        NEG_E = -math.exp(-0.5)
        pv = self.pv
        with contextlib.ExitStack() as st:
            def alloc(name, shape, dt=F32):
                return st.enter_context(nc.sbuf_tensor(_nm(name), shape, dt))
            msk = alloc("wmsk", [128, 384])
            bones = alloc("wbones", [128, 128]); bones64 = alloc("wbones64", [128, 128])
            hmask = alloc("whmask", [128, 2])
            scanm = alloc("wscanm", [128, GT])
            w2sb = alloc("w2sb", [128, 512]); a2sb = alloc("a2sb", [128, 512]); g2sb = alloc("g2sb", [128, 512])
            c0v = alloc("wc0", [128, 16]); omka = alloc("womka", [128, 4])
            em.dma("sp", msk[:], self.c_wkvmask, writes=["wmsk"])
            em.dma("sp", bones[:], self.c_bones, writes=["wbones"])
            em.dma("sp", hmask[:], self.c_hmask, writes=["whmask"])
            em.dma("sp", scanm[:], self.c_scanmask[:, 0:GT], writes=["wscanm"])
            em.dma("sp", w2sb[:], self.rw_w2[l].rearrange("d r c -> (d r) c"), writes=["w2sb"])
            em.dma("sp", a2sb[:], self.rw_a2[l].rearrange("d r c -> (d r) c"), writes=["a2sb"])
            em.dma("sp", g2sb[:], self.rw_g2[l], writes=["g2sb"])
            self.ts("dve", bones64[:], bones[:], 1.0 / 64.0, None, ALU.mult, r=["wbones"], w=["wbones64"])
            self.tt("dve", c0v[:, 0:15], pv[:, PV_MUP:PV_MUP + 15], pv[:, PV_MUN:PV_MUN + 15], ALU.add, r=["pv"], w=["wc0"])
            self.ts("dve", c0v[:, 0:15], c0v[:, 0:15], -1.0, 1.0, ALU.mult, ALU.add, r=["wc0"], w=["wc0"])
            self.ts("dve", omka[:], pv[:, PV_KA:PV_KA + 4], -1.0, 1.0, ALU.mult, ALU.add, r=["pv"], w=["womka"])
            SLm, SUm, IUm, ILm = msk[:, 0:128], msk[:, 128:256], msk[:, 256:320], msk[:, 320:384]

            big = self.xT[:].rearrange("p c t -> p (c t)").bitcast(F32).rearrange("p (a s) -> p a s", a=8)
            r_, k_, v_, kkn, yacc, logw, a_, kd = [big[:, i, :] for i in range(8)]
            kb = ["xT%d" % i for i in range(self.T // 128)]
            K_R, K_K, K_V, K_KK, K_Y, K_LW, K_A, K_KD = ["wbig%d" % i for i in range(8)]

            zb = alloc("wzb", [128, S + 2])
            self.memset("dve", zb[:], 0.0, w=["wzb"])
            shr = Rot(nc, st, "wsh", 3, [128, 512], F32)
            t5 = Rot(nc, st, "wt5", 4, [128, 512], F32)
            yo = Rot(nc, st, "wyo", 2, [128, 512], BF16)
            gtile = {}
            for nm in ("G", "tA", "E1", "E2", "rt", "E3", "ba", "tB", "E4"):
                gtile[nm] = alloc("wg_" + nm, [128, GT])
            gtile["rtb"] = alloc("wg_rtb", [128, GT], WD)
            gt = alloc("wgt", [128, NCG]); gam = alloc("wgam", [128, NCG])
            atp = alloc("watp", [128, NCG, 128], WD); btp = alloc("wbtp", [128, NCG, 128], WD)
            ktp = alloc("wktp", [128, NCG, 128], WD)
            padr = Rot(nc, st, "wpad", 8, [128, NCG, 128], F32)
            for i_, t_ in enumerate(padr.tiles):
                self.memset("pool" if i_ % 2 else "dve", t_[:], 0.0, w=[("wpad", i_)])
            for zt_, zk_ in ((btp, "wbtp"), (ktp, "wktp")):
                self.memset("dve", zt_[:], 0.0, w=[zk_])
            atT = alloc("watT", [128, NCG, 128], WD)
            setT = [{nm: alloc("w%s%d" % (nm, s_), [128, NCG, 128], WD) for nm in ("vT", "bhT", "khT")}
                    for s_ in range(2)]
            U = []
            for ci in range(NCG):
                d_ = {}
                for nm in ("X0", "X1", "XT0", "XT1", "PT0", "PT1", "AKT", "AKV"):
                    d_[nm] = alloc("wu%s_%d" % (nm, ci), [128, 128], WD)
                for s_ in range(2):
                    d_["WU%d" % s_] = alloc("wuWU%d_%d" % (s_, ci), [128, 256], WD)
                    d_["M%d" % s_] = alloc("wuM%d_%d" % (s_, ci), [128, 128], F32)
                    d_["ARBT%d" % s_] = alloc("wuARBT%d_%d" % (s_, ci), [128, 64], WD)
                    d_["ARKT%d" % s_] = alloc("wuARKT%d_%d" % (s_, ci), [128, 64], WD)
                    d_["QT%d" % s_] = alloc("wuQT%d_%d" % (s_, ci), [128, 64], F32)
                U.append(d_)
            STt = [alloc("wST%d" % i, [128, 128], F32) for i in range(2)]
            ident = self.ident_f

            def shift_block(blk, b, dst, kdst, post=None):
                em.dma("sp", zb[:, 1:S + 1], self.zrT[blk * 128:(blk + 1) * 128, b * S:(b + 1) * S], writes=["wzb"])
                self.act(dst, zb[:, 1:S + 1], AF.Identity, scale=c0v[:, blk:blk + 1], r=["wzb", "wc0"], w=[kdst])
                self.stt(dst, zb[:, 0:S], pv[:, PV_MUP + blk:PV_MUP + blk + 1], dst, ALU.mult, ALU.add,
                         r=["wzb", "pv", kdst], w=[kdst])
                self.stt(dst, zb[:, 2:S + 2], pv[:, PV_MUN + blk:PV_MUN + blk + 1], dst, ALU.mult, ALU.add,
                         r=["wzb", "pv", kdst], w=[kdst])
                if post is not None:
                    self.act(dst, dst, post, r=[kdst], w=[kdst])

            cnt = [0]

            def evac_copy(out, in_, r, w):
                cnt[0] += 1
                self.cp("act" if cnt[0] % 2 else "dve", out, in_, r=r, w=w)

            for b in range(NSEQ):
                for si, (blk, post) in enumerate(((12, AF.Tanh), (13, None), (14, AF.Sigmoid))):
                    shift_block(blk, b, r_, K_R, post)
                    em.dma("sp", self.shT[si, :, b * S:(b + 1) * S], r_, reads=[K_R], writes=[("shT", si, b)])
                if self.dbg == 1:
                    em.flush(); return
                for hp in range(4):
                    shift_block(hp, b, r_, K_R)
                    shift_block(4 + hp, b, k_, K_K)
                    shift_block(8 + hp, b, v_, K_V)
                    self.ts("pool", kkn, k_, pv[:, PV_KK + hp:PV_KK + hp + 1], None, ALU.mult, r=[K_K, "pv"], w=[K_KK])
                    for q4 in range(n512):
                        rg = slice(q4 * 512, (q4 + 1) * 512)
                        t1, k1 = t5.next()
                        self.tt("pool", t1[:], kkn[:, rg], kkn[:, rg], ALU.mult, r=[K_KK], w=[k1])
                        ps, kps = self.psb()
                        self.mm(ps[:], bones[:], t1[:], r=["wbones", k1], w=[kps])
                        self.act(t1[:], ps[:], AF.Sqrt, r=[kps], w=[k1])
                        self.ts("dve", t1[:], t1[:], 1e-12, None, ALU.max, r=[k1], w=[k1])
                        self.recip(t1[:], t1[:], r=[k1], w=[k1])
                        self.tt("dve", kkn[:, rg], kkn[:, rg], t1[:], ALU.mult, r=[K_KK, k1], w=[K_KK])
                    if self.dbg == 2:
                        em.flush(); return
                    for d in range(2):
                        rev = (d == 1)
                        dsl = slice(d * 64, (d + 1) * 64)
                        for q4 in range(n512):
                            rg = slice(q4 * 512, (q4 + 1) * 512)
                            grg = slice(b * S + q4 * 512, b * S + (q4 + 1) * 512)
                            s1, ks1 = shr.next()
                            em.dma("sp", s1[:], self.shT[0, :, grg], reads=[("shT", 0, b)], writes=[ks1])
                            ps, kps = self.psb()
                            self.mm(ps[:], w2sb[dsl, hp * 128:(hp + 1) * 128], s1[dsl, :], r=["w2sb", ks1], w=[kps])
                            t1, k1 = t5.next()
                            self.act(t1[:], ps[:], AF.Sigmoid, bias=pv[:, PV_W0 + d * 4 + hp:PV_W0 + d * 4 + hp + 1],
                                     r=[kps, "pv"], w=[k1])
                            self.ts("pool", logw[:, rg], t1[:], NEG_E, None, ALU.mult, r=[k1], w=[K_LW])
                            s2, ks2 = shr.next()
                            em.dma("sp", s2[:], self.shT[1, :, grg], reads=[("shT", 1, b)], writes=[ks2])
                            ps, kps = self.psb()
                            self.mm(ps[:], a2sb[dsl, hp * 128:(hp + 1) * 128], s2[dsl, :], r=["a2sb", ks2], w=[kps])
                            self.act(a_[:, rg], ps[:], AF.Sigmoid, bias=pv[:, PV_A0 + d * 4 + hp:PV_A0 + d * 4 + hp + 1],
                                     r=[kps, "pv"], w=[K_A])
                        self.ts("pool", kd, a_, pv[:, PV_KA + hp:PV_KA + hp + 1], omka[:, hp:hp + 1], ALU.mult, ALU.add,
                                r=[K_A, "pv", "womka"], w=[K_KD])
                        self.tt("pool", kd, kd, k_, ALU.mult, r=[K_KD, K_K], w=[K_KD])
                        if self.dbg == 3:
                            em.flush(); return
                        self.memset("pool", STt[0][:], 0.0, w=["wST0"])
                        stcur = 0
                        pending = []
                        MT = SLm if rev else SUm
                        MN = SUm if rev else SLm
                        MI = ILm if rev else IUm
                        groups = list(range(ngr))
                        if rev:
                            groups = groups[::-1]
                        for gi, grp in enumerate(groups):
                            sset = gi % 2
                            g0 = grp * GT
                            gs = slice(g0, g0 + GT)
                            G = gtile["G"]

                            def g3(ap):
                                return ap.rearrange("p (c t) -> p c t", c=NCG)
                            em.op("dve", lambda e, G=G, gs=gs: e.tensor_tensor_scan(
                                out=G[:], data0=scanm[:], data1=logw[:, gs], initial=0.0, op0=ALU.mult, op1=ALU.add),
                                ["wscanm", K_LW], ["wg_G"])
                            self.cp("dve", gt[:], g3(G[:])[:, :, C - 1], r=["wg_G"], w=["wgt"])
                            if rev:
                                self.tt("dve", g3(G[:]), gt[:].unsqueeze(2).broadcast_to([128, NCG, C]), g3(G[:]),
                                        ALU.subtract, r=["wgt", "wg_G"], w=["wg_G"])
                                self.tt("dve", G[:], G[:], logw[:, gs], ALU.add, r=["wg_G", K_LW], w=["wg_G"])
                            g_ = gtile
                            H2 = (slice(0, 64), slice(64, 128))

                            def blk(t, h):
                                return t[H2[h], :, h * 64:(h + 1) * 64]

                            def hv(ap, h):
                                return ap[H2[h]].rearrange("p (c t) -> p c t", c=NCG)
                            pdA, kpdA = padr.next(); pdB, kpdB = padr.next(); pdK, kpdK = padr.next(); pdV, kpdV = padr.next()
                            self.tt("dve", g_["tA"][:], G[:], logw[:, gs], ALU.subtract, r=["wg_G", K_LW], w=["wg_tA"])
                            self.act(g_["E1"][:], g_["tA"][:], AF.Exp, r=["wg_tA"], w=["wg_E1"])
                            for h in range(2):
                                self.stt(blk(pdA, h), hv(kkn[:, gs], h), -1.0, hv(g_["E1"][:], h), ALU.mult, ALU.mult,
                                         r=[K_KK, "wg_E1"], w=[kpdA])
                            self.cp("act", atp[:], pdA[:], r=[kpdA], w=["watp"])
                            self.act(g_["E2"][:], G[:], AF.Exp, r=["wg_G"], w=["wg_E2"])
                            self.tt("pool", g_["rt"][:], r_[:, gs], g_["E2"][:], ALU.mult, r=[K_R, "wg_E2"], w=["wg_rt"])
                            self.cp("pool", g_["rtb"][:], g_["rt"][:], r=["wg_rt"], w=["wg_rtb"])
                            self.act(g_["E3"][:], G[:], AF.Exp, scale=-1.0, r=["wg_G"], w=["wg_E3"])
                            self.tt("pool", g_["ba"][:], kkn[:, gs], a_[:, gs], ALU.mult, r=[K_KK, K_A], w=["wg_ba"])
                            for h in range(2):
                                self.tt("dve" if h else "pool", blk(btp, h), hv(g_["ba"][:], h), hv(g_["E3"][:], h), ALU.mult,
                                        r=["wg_ba", "wg_E3"], w=["wbtp"])
                                self.tt("pool" if h else "dve", blk(ktp, h), hv(kd[:, gs], h), hv(g_["E3"][:], h), ALU.mult,
                                        r=[K_KD, "wg_E3"], w=["wktp"])
                            self.tt("dve", g3(g_["tB"][:]), gt[:].unsqueeze(2).broadcast_to([128, NCG, C]), g3(G[:]),
                                    ALU.subtract, r=["wgt", "wg_G"], w=["wg_tB"])
                            self.act(g_["E4"][:], g_["tB"][:], AF.Exp, r=["wg_tB"], w=["wg_E4"])
                            for h in range(2):
                                self.tt("dve" if h else "pool", blk(pdB, h), hv(g_["ba"][:], h), hv(g_["E4"][:], h), ALU.mult,
                                        r=["wg_ba", "wg_E4"], w=[kpdB])
                                self.tt("pool" if h else "dve", blk(pdK, h), hv(kd[:, gs], h), hv(g_["E4"][:], h), ALU.mult,
                                        r=[K_KD, "wg_E4"], w=[kpdK])
                                self.cp("act", blk(pdV, h), hv(v_[:, gs], h), r=[K_V], w=[kpdV])
                            self.act(gam[:], gt[:], AF.Exp, r=["wgt"], w=["wgam"])
                            TS = setT[sset]

                            def transp(dst, kdst, srcp, ksrcp):
                                ps, kps = self.psb()
                                for c in range(NCG):
                                    self.tr(ps[:, c * 128:(c + 1) * 128], srcp[:, c, :], ident[:], r=[ksrcp, "ident"], w=[kps])
                                evac_copy(dst[:], ps[:].rearrange("p (c t) -> p c t", c=NCG), r=[kps], w=[kdst])
                            transp(atT, "watT", pdA, kpdA)
                            transp(TS["bhT"], "wbhT%d" % sset, pdB, kpdB)
                            transp(TS["khT"], "wkhT%d" % sset, pdK, kpdK)
                            transp(TS["vT"], "wvT%d" % sset, pdV, kpdV)
                            kvT, kbhT, kkhT = "wvT%d" % sset, "wbhT%d" % sset, "wkhT%d" % sset

                            if self.dbg == 4:
                                em.flush(); return

                            def uk(ci, nm):
                                return ("wu", ci, nm)
                            WUn, Mn, ARBn, ARKn, QTn = ["%s%d" % (x, sset) for x in ("WU", "M", "ARBT", "ARKT", "QT")]
                            steps = []

                            def s1():
                                for ci in range(NCG):
                                    u = U[ci]
                                    ps, kps = self.psb()
                                    self.mm(ps[:, 0:128], btp[:, ci, :], atp[:, ci, :], r=["wbtp", "watp"], w=[kps])
                                    self.mm(ps[:, 128:192], btp[:, ci, :], g_["rtb"][:, ci * C:(ci + 1) * C], r=["wbtp", "wg_rtb"], w=[kps])
                                    self.tt("dve", u["XT0"][:], ps[:, 0:128], MT, ALU.mult, r=[kps, "wmsk"], w=[uk(ci, "XT0")])
                                    self.tt("dve", u[ARBn][:], ps[:, 128:192], MI, ALU.mult, r=[kps, "wmsk"], w=[uk(ci, ARBn)])
                                    self.tt("pool", u["PT0"][:], u["XT0"][:], ident[:], ALU.add, r=[uk(ci, "XT0"), "ident"], w=[uk(ci, "PT0")])
                            steps.append(s1)

                            def s2():
                                for ci in range(NCG):
                                    u = U[ci]
                                    ps, kps = self.psb()
                                    self.mm(ps[:, 0:128], ktp[:, ci, :], atp[:, ci, :], r=["wktp", "watp"], w=[kps])
                                    self.mm(ps[:, 128:192], ktp[:, ci, :], g_["rtb"][:, ci * C:(ci + 1) * C], r=["wktp", "wg_rtb"], w=[kps])
                                    self.tt("dve", u["AKT"][:], ps[:, 0:128], MT, ALU.mult, r=[kps, "wmsk"], w=[uk(ci, "AKT")])
                                    self.tt("dve", u[ARKn][:], ps[:, 128:192], MI, ALU.mult, r=[kps, "wmsk"], w=[uk(ci, ARKn)])
                            steps.append(s2)

                            def s3():
                                for ci in range(NCG):
                                    u = U[ci]
                                    ps, kps = self.psb()
                                    self.mm(ps[:, 0:128], atp[:, ci, :], btp[:, ci, :], r=["watp", "wbtp"], w=[kps])
                                    self.tt("dve", u["X0"][:], ps[:, 0:128], MN, ALU.mult, r=[kps, "wmsk"], w=[uk(ci, "X0")])
                            steps.append(s3)

                            for lev in range(1, 6):
                                pi, ni = (lev - 1) % 2, lev % 2
                                Xp, XTp, Xn, XTn = "X%d" % pi, "XT%d" % pi, "X%d" % ni, "XT%d" % ni
                                PTp, PTn = "PT%d" % pi, "PT%d" % ni

                                def sa(Xp=Xp, XTp=XTp, Xn=Xn):
                                    for ci in range(NCG):
                                        u = U[ci]
                                        ps, kps = self.psb()
                                        self.mm(ps[:, 0:128], u[XTp][:], u[Xp][:], r=[uk(ci, XTp), uk(ci, Xp)], w=[kps])
                                        evac_copy(u[Xn][:], ps[:, 0:128], r=[kps], w=[uk(ci, Xn)])
                                steps.append(sa)
                                if lev < 5:
                                    def sb_(Xp=Xp, XTp=XTp, XTn=XTn):
                                        for ci in range(NCG):
                                            u = U[ci]
                                            ps, kps = self.psb()
                                            self.mm(ps[:, 0:128], u[Xp][:], u[XTp][:], r=[uk(ci, XTp), uk(ci, Xp)], w=[kps])
                                            evac_copy(u[XTn][:], ps[:, 0:128], r=[kps], w=[uk(ci, XTn)])
                                    steps.append(sb_)

                                def sc(Xn=Xn, PTp=PTp, PTn=PTn):
                                    for ci in range(NCG):
                                        u = U[ci]
                                        ps, kps = self.psb()
                                        self.mm(ps[:, 0:128], u[Xn][:], u[PTp][:], r=[uk(ci, Xn), uk(ci, PTp)], w=[kps])
                                        self.tt("dve", u[PTn][:], ps[:, 0:128], u[PTp][:], ALU.add, r=[kps, uk(ci, PTp)], w=[uk(ci, PTn)])
                                steps.append(sc)
                            TTn = "PT1"

                            def s5():
                                for ci in range(NCG):
                                    u = U[ci]
                                    ps, kps = self.psb()
                                    self.mm(ps[:, 0:128], u["AKT"][:], TS["vT"][:, ci, :], r=[uk(ci, "AKT"), kvT], w=[kps])
                                    evac_copy(u["AKV"][:], ps[:, 0:128], r=[kps], w=[uk(ci, "AKV")])
                            steps.append(s5)

                            def s6():
                                for ci in range(NCG):
                                    u = U[ci]
                                    ps, kps = self.psb()
                                    self.mm(ps[:, 0:128], u[TTn][:], atT[:, ci, :], r=[uk(ci, TTn), "watT"], w=[kps])
                                    self.mm(ps[:, 128:256], u[TTn][:], u["AKV"][:], r=[uk(ci, TTn), uk(ci, "AKV")], w=[kps])
                                    evac_copy(u[WUn][:], ps[:, 0:256], r=[kps], w=[uk(ci, WUn)])
                            steps.append(s6)

                            def s7():
                                for ci in range(NCG):
                                    u = U[ci]
                                    ps, kps = self.psb()
                                    self.mm(ps[:, 0:128], u[WUn][:, 0:128], TS["bhT"][:, ci, :], r=[uk(ci, WUn), kbhT], w=[kps])
                                    self.stt(u[Mn][:], ident[:], gam[:, ci:ci + 1], ps[:, 0:128], ALU.mult, ALU.add,
                                             r=[kps, "ident", "wgam"], w=[uk(ci, Mn)])
                                    ps2, kps2 = self.psb()
                                    self.mm(ps2[:, 0:64], u[WUn][:, 0:128], u[ARBn][:], r=[uk(ci, WUn), uk(ci, ARBn)], w=[kps2])
                                    self.tt("dve", u[QTn][:], ps2[:, 0:64], g_["rt"][:, ci * C:(ci + 1) * C], ALU.add,
                                            r=[kps2, "wg_rt"], w=[uk(ci, QTn)])
                            steps.append(s7)

                            order = list(range(NCG))
                            if rev:
                                order = order[::-1]
                            for ci in order:
                                def seq(ci=ci, TS=TS, sset=sset, g0=g0, d=d, WUn=WUn, Mn=Mn, ARBn=ARBn, ARKn=ARKn, QTn=QTn,
                                        kvT=kvT, kbhT=kbhT, kkhT=kkhT):
                                    nonlocal stcur
                                    u = U[ci]
                                    STc, kSTc = STt[stcur], "wST%d" % stcur
                                    STn, kSTn = STt[1 - stcur], "wST%d" % (1 - stcur)
                                    psy, kpy = self.psb()
                                    self.mm(psy[:, 0:64], STc[:], u[QTn][:], start=True, stop=False, r=[kSTc, uk(ci, QTn)], w=[kpy])
                                    self.mm(psy[:, 0:64], u[WUn][:, 128:256], u[ARBn][:], start=False, stop=False,
                                            r=[uk(ci, WUn), uk(ci, ARBn)], w=[kpy])
                                    self.mm(psy[:, 0:64], TS["vT"][:, ci, :], u[ARKn][:], start=False, stop=True,
                                            r=[kvT, uk(ci, ARKn)], w=[kpy])
                                    ysl = yacc[:, g0 + ci * C:g0 + (ci + 1) * C]
                                    if d == 0:
                                        self.cp("act", ysl, psy[:, 0:64], r=[kpy], w=[K_Y])
                                    else:
                                        self.tt("dve", ysl, psy[:, 0:64], ysl, ALU.add, r=[kpy, K_Y], w=[K_Y])
                                    pss, kpss = self.psb()
                                    self.mm(pss[:, 0:128], u[Mn][:], STc[:], start=True, stop=False, r=[uk(ci, Mn), kSTc], w=[kpss])
                                    self.mm(pss[:, 0:128], TS["bhT"][:, ci, :], u[WUn][:, 128:256], start=False, stop=False,
                                            r=[kbhT, uk(ci, WUn)], w=[kpss])
                                    self.mm(pss[:, 0:128], TS["khT"][:, ci, :], TS["vT"][:, ci, :], start=False, stop=True,
                                            r=[kkhT, kvT], w=[kpss])
                                    self.cp("act", STn[:], pss[:, 0:128], r=[kpss], w=[kSTn])
                                    stcur = 1 - stcur
                                pending.append(seq)
                            npar = len(steps)
                            old = pending[:-NCG]
                            del pending[:-NCG]
                            for si_, s_ in enumerate(steps):
                                s_()
                                if old and si_ % 4 == 3:
                                    old.pop(0)()
                            for s_ in old:
                                s_()
                            if self.dbg == 5:
                                em.flush(); return
                        for s_ in pending:
                            s_()
                        pending = []
                        if self.dbg == 6:
                            em.flush(); return
                    if self.dbg == 7:
                        em.flush(); return
                    for q4 in range(n512):
                        rg = slice(q4 * 512, (q4 + 1) * 512)
                        grg = slice(b * S + q4 * 512, b * S + (q4 + 1) * 512)
                        t1, k1 = t5.next(); t2, k2 = t5.next()
                        ps, kps = self.psb()
                        self.mm(ps[:], bones64[:], yacc[:, rg], r=["wbones64", K_Y], w=[kps])
                        self.tt("dve", t1[:], yacc[:, rg], ps[:], ALU.subtract, r=[K_Y, kps], w=[k1])
                        self.tt("pool", t2[:], t1[:], t1[:], ALU.mult, r=[k1], w=[k2])
                        ps, kps = self.psb()
                        self.mm(ps[:], bones64[:], t2[:], r=["wbones64", k2], w=[kps])
                        self.act(t2[:], ps[:], AF.Sqrt, bias=self.epsv[:, 1:2], r=[kps, "epsv"], w=[k2])
                        self.recip(t2[:], t2[:], r=[k2], w=[k2])
                        self.tt("dve", t1[:], t1[:], t2[:], ALU.mult, r=[k1, k2], w=[k1])
                        self.ts("pool", t1[:], t1[:], pv[:, PV_GG + hp:PV_GG + hp + 1], pv[:, PV_GB + hp:PV_GB + hp + 1],
                                ALU.mult, ALU.add, r=[k1, "pv"], w=[k1])
                        self.stt(t2[:], r_[:, rg], pv[:, PV_RK + hp:PV_RK + hp + 1], k_[:, rg], ALU.mult, ALU.mult,
                                 r=[K_R, K_K, "pv", k2], w=[k2])
                        ps, kps = self.psb()
                        self.mm(ps[:], bones[:], t2[:], r=["wbones", k2], w=[kps])
                        self.tt("dve", t2[:], ps[:], v_[:, rg], ALU.mult, r=[kps, K_V], w=[k2])
                        self.tt("pool", t1[:], t1[:], t2[:], ALU.add, r=[k1, k2], w=[k1])
                        s3_, ks3 = shr.next()
                        em.dma("sp", s3_[:], self.shT[2, :, grg], reads=[("shT", 2, b)], writes=[ks3])
                        ps, kps = self.psb()
                        self.mm(ps[:], g2sb[:, hp * 128:(hp + 1) * 128], s3_[:], r=["g2sb", ks3], w=[kps])
                        y_, ky = yo.next()
                        self.tt("dve", y_[:], t1[:], ps[:], ALU.mult, r=[k1, kps], w=[ky])
                        em.dma("sp", self.yT[2, hp * 128:(hp + 1) * 128, grg], y_[:], reads=[ky], writes=[("yT2", hp, b, q4)])
                    if self.dbg == 8:
                        em.flush(); return
                if self.dbg == 9:
                    em.flush(); return
            em.flush()

    def stage_merge(self, l):
        nc, em, T = self.nc, self.em, self.T
        with contextlib.ExitStack() as st:
            wbr = st.enter_context(nc.sbuf_tensor(_nm("mwbr"), [128, 3, 4, 1024], BF16))
            wo = st.enter_context(nc.sbuf_tensor(_nm("mwo"), [128, 8, 1024], BF16))
            g_bc = st.enter_context(nc.sbuf_tensor(_nm("mlng"), [128, 1024], F32))
            b_bc = st.enter_context(nc.sbuf_tensor(_nm("mlnb"), [128, 1024], F32))
            self.load_bc(g_bc, "mlngb", self.ln1_g[l])
            self.load_bc(b_bc, "mlngb", self.ln1_b[l])
            with contextlib.ExitStack() as st2:
                wstg = Rot(nc, st2, "mwst", 2, [128, 4, 1024], F32)
                for n in range(3):
                    ws, kws = wstg.next()
                    em.dma("sp", ws[:], self.w_branch[l, n].rearrange("(c p) d -> p c d", p=128), writes=[kws])
                    self.cp("pool" if n % 2 else "act", wbr[:, n], ws[:], r=[kws], w=["mwbr"])
                wov = self.w_out[l].rearrange("(c p) e -> p c e", p=128)
                for hf in range(2):
                    ws, kws = wstg.next()
                    em.dma("sp", ws[:], wov[:, hf * 4:(hf + 1) * 4, :], writes=[kws])
                    self.cp("pool" if hf else "act", wo[:, hf * 4:(hf + 1) * 4, :], ws[:], r=[kws], w=["mwo"])
                em.flush()
            yin = Rot(nc, st, "myin", 2, [128, 3, 4, 512], BF16)
            mTr = Rot(nc, st, "mmT", 2, [128, 8, 512], BF16)
            gtr = Rot(nc, st, "mgt", 4, [128, 512], F32)
            accr = Rot(nc, st, "macc", 2, [128, 512], F32)
            tmr = Rot(nc, st, "mtm", 2, [128, 512], F32)
            xr = Rot(nc, st, "mx", 2, [128, 1024], F32)
            prer = Rot(nc, st, "mpre", 1, [128, 1024], F32)
            xor_ = Rot(nc, st, "mxo", 2, [128, 1024], F32)
            for tr_ in range(T // 512):
                tsl = slice(tr_ * 512, (tr_ + 1) * 512)
                yi, kyi = yin.next()
                for n in range(3):
                    em.dma("sp", yi[:, n], self.yT[n, :, tsl].rearrange("(c p) t -> p c t", p=128), writes=[kyi])
                mT, kmT = mTr.next()
                for db in range(8):
                    acc, kacc = accr.next()
                    for n in range(3):
                        ps, kps = self.psb()
                        for cc in range(4):
                            self.mm(ps[:], wbr[:, n, cc, db * 128:(db + 1) * 128], yi[:, n, cc, :],
                                    start=(cc == 0), stop=(cc == 3), r=["mwbr", kyi], w=[kps])
                        gt_, kgt = gtr.next()
                        em.dma("sp", gt_[:], self.gT[(n * 8 + db) * 128:(n * 8 + db + 1) * 128, tsl], writes=[kgt])
                        if n == 0:
                            self.tt("dve", acc[:], ps[:], gt_[:], ALU.mult, r=[kps, kgt], w=[kacc])
                        else:
                            tm, ktm = tmr.next()
                            self.tt("dve", tm[:], ps[:], gt_[:], ALU.mult, r=[kps, kgt], w=[ktm])
                            if n == 1:
                                self.tt("pool", acc[:], acc[:], tm[:], ALU.add, r=[kacc, ktm], w=[kacc])
                            else:
                                self.tt("pool", mT[:, db, :], acc[:], tm[:], ALU.add, r=[kacc, ktm], w=[kmT])
                for t4 in range(4):
                    tt = tr_ * 4 + t4
                    xt, kx = xr.next()
                    em.dma("sp", xt[:], self.x_tm[tt * 128:(tt + 1) * 128, :], writes=[kx])
                    pre, kpre = prer.next()
                    for hf in range(2):
                        ps, kps = self.psb()
                        for dc in range(8):
                            self.mm(ps[:], mT[:, dc, t4 * 128:(t4 + 1) * 128], wo[:, dc, hf * 512:(hf + 1) * 512],
                                    start=(dc == 0), stop=(dc == 7), r=[kmT, "mwo"], w=[kps])
                        self.stt(pre[:, hf * 512:(hf + 1) * 512], xt[:, hf * 512:(hf + 1) * 512], ALPHA, ps[:],
                                 ALU.mult, ALU.add, r=[kx, kps], w=[kpre])
                    xo, kxo = xor_.next()
                    self.ln_tile(pre, kpre, xo, kxo, g_bc, b_bc, "mlngb")
                    em.dma("sp", self.x1_tm[tt * 128:(tt + 1) * 128, :], xo[:], reads=[kxo], writes=[("x1_tm", tt)])
                    self.transpose_into_xT(xo, kxo, tt)
            em.flush()

    def stage_ffn(self, l, last):
        nc, em, T = self.nc, self.em, self.T
        with contextlib.ExitStack() as st:
            s32 = Rot(nc, st, "fs32", 2, [128, 4096], F32)
            s16 = Rot(nc, st, "fs16", 2, [128, 4096], BF16)
            wuv = self.w_up[l].rearrange("(c p) f -> p c f", p=128)
            wdv = self.w_down[l].rearrange("(fc p) e -> p fc e", p=128)
            engs = ("pool", "act", "dve")
            for i in range(8):
                a, ka = s32.next(); bq, kb_ = s16.next()
                a3 = a[:].rearrange("p (c f) -> p c f", c=8)
                b3 = bq[:].rearrange("p (c f) -> p c f", c=8)
                em.dma("sp", a3, wuv[:, :, i * 512:(i + 1) * 512], writes=[ka])
                self.cp(engs[i % 3], bq[:], a[:], r=[ka], w=[kb_])
                em.dma("sp", self.wup_bf[:, :, i * 512:(i + 1) * 512], b3, reads=[kb_], writes=[("wup_bf", i)])
            for eb in range(8):
                a, ka = s32.next(); bq, kb_ = s16.next()
                a3 = a[:].rearrange("p (c f) -> p c f", c=32)
                b3 = bq[:].rearrange("p (c f) -> p c f", c=32)
                em.dma("sp", a3, wdv[:, :, eb * 128:(eb + 1) * 128], writes=[ka])
                self.cp(engs[(eb + 2) % 3], bq[:], a[:], r=[ka], w=[kb_])
                em.dma("sp", self.wdn_bf[eb], b3, reads=[kb_], writes=[("wdn_bf", eb)])
            em.flush()
        with contextlib.ExitStack() as st:
            g_bc = st.enter_context(nc.sbuf_tensor(_nm("flng"), [128, 1024], F32))
            b_bc = st.enter_context(nc.sbuf_tensor(_nm("flnb"), [128, 1024], F32))
            self.load_bc(g_bc, "flngb", self.ln2_g[l])
            self.load_bc(b_bc, "flngb", self.ln2_b[l])
            hidT = st.enter_context(nc.sbuf_tensor(_nm("fhid"), [128, 32, 512], BF16))
            outT = st.enter_context(nc.sbuf_tensor(_nm("fout"), [128, 8, 512], F32))
            wur = Rot(nc, st, "fwu", 2, [128, 8, 512], BF16)
            wdr = Rot(nc, st, "fwd", 2, [128, 32, 128], BF16)
            rlr = Rot(nc, st, "frl", 3, [128, 512], F32)
            xr = Rot(nc, st, "fx", 2, [128, 1024], F32)
            prer = Rot(nc, st, "fpre", 2, [128, 1024], F32)
            xor_ = Rot(nc, st, "fxo", 2, [128, 1024], F32)
            for tg in range(T // 512):
                tsl = slice(tg * 512, (tg + 1) * 512)
                xk = [("xT", tg * 4 + i) for i in range(4)]
                for fg in range(8):
                    wu, kwu = wur.next()
                    em.dma("sp", wu[:], self.wup_bf[:, :, fg * 512:(fg + 1) * 512], writes=[kwu])
                    for f4 in range(4):
                        fb = fg * 4 + f4
                        ps, kps = self.psb()
                        for dc in range(8):
                            self.mm(ps[:], wu[:, dc, f4 * 128:(f4 + 1) * 128], self.xT[:, dc, tsl],
                                    start=(dc == 0), stop=(dc == 7), r=[kwu] + xk, w=[kps])
                        rl, krl = rlr.next()
                        self.act(rl[:], ps[:], AF.Relu, r=[kps], w=[krl])
                        self.tt("pool" if fb % 2 else "dve", hidT[:, fb, :], rl[:], rl[:], ALU.mult, r=[krl], w=[("fhid", fb)])
                for eb in range(8):
                    wd, kwd = wdr.next()
                    em.dma("sp", wd[:], self.wdn_bf[eb], writes=[kwd])
                    ps, kps = self.psb()
                    for fc in range(32):
                        self.mm(ps[:], wd[:, fc, :], hidT[:, fc, :], start=(fc == 0), stop=(fc == 31),
                                r=[kwd, ("fhid", fc)], w=[kps])
                    self.cp("act" if eb % 2 else "dve", outT[:, eb, :], ps[:], r=[kps], w=[("fout", eb)])
                for t4 in range(4):
                    tt = tg * 4 + t4
                    xt, kx = xr.next()
                    em.dma("sp", xt[:], self.x1_tm[tt * 128:(tt + 1) * 128, :], writes=[kx])
                    pre, kpre = prer.next()
                    for hf in range(2):
                        ps, kps = self.psb()
                        for c4 in range(4):
                            self.tr(ps[:, c4 * 128:(c4 + 1) * 128], outT[:, hf * 4 + c4, t4 * 128:(t4 + 1) * 128],
                                    self.ident_f[:], r=[("fout", hf * 4 + c4), "ident"], w=[kps])
                        self.stt(pre[:, hf * 512:(hf + 1) * 512], xt[:, hf * 512:(hf + 1) * 512], ALPHA, ps[:],
                                 ALU.mult, ALU.add, r=[kx, kps], w=[kpre])
                    xo, kxo = xor_.next()
                    self.ln_tile(pre, kpre, xo, kxo, g_bc, b_bc, "flngb")
                    if last:
                        em.dma("sp", self.out[tt * 128:(tt + 1) * 128, :], xo[:], reads=[kxo], writes=[("out", tt)])
                    else:
                        em.dma("sp", self.x_tm[tt * 128:(tt + 1) * 128, :], xo[:], reads=[kxo], writes=[("x_tm", tt)])
                        self.transpose_into_xT(xo, kxo, tt)
            em.flush()

    def build(self):
        nc = self.nc
        self.declare()
        stages = ["ln_in", "proj", "pool", "attn", "rwkv", "merge", "ffn"]
        upto = self.upto
        with contextlib.ExitStack() as st:
            self.em = Em(nc, st)
            em = self.em
            self.psum = [st.enter_context(nc.psum_tensor("psb%d" % i, [128, 512], F32)) for i in range(8)]
            self.ident_f = st.enter_context(nc.sbuf_tensor(_nm("ident_f"), [128, 128], F32))
            self.xT = st.enter_context(nc.sbuf_tensor(_nm("xT"), [128, 8, self.T], BF16))
            self.pv = st.enter_context(nc.sbuf_tensor(_nm("pv"), [128, NPV], F32))
            self.lnst = Rot(nc, st, "lnst", 4, [128, 16], F32)
            self.band = st.enter_context(nc.sbuf_tensor(_nm("band"), [128, 4, 3, 128], F32))
            self.relb = st.enter_context(nc.sbuf_tensor(_nm("relb"), [128, 128], F32))
            self.epsv = st.enter_context(nc.sbuf_tensor(_nm("epsv"), [128, 2], F32))
            self.memset("dve", self.epsv[:, 0:1], 1e-5, w=["epsv"])
            self.memset("dve", self.epsv[:, 1:2], GN_EPS, w=["epsv"])
            self.build_band()
            em.dma("sp", self.ident_f[:], self.c_ident, writes=["ident"])
            self.stage_ln_in()
            done = upto == "ln_in"
            for l in range(self.L):
                if done:
                    break
                em.dma("sp", self.pv[:], self.pvec_d[l], writes=["pv"])
                self.stage_proj(l)
                if upto == "proj":
                    break
                self.stage_pool(l)
                if upto == "pool":
                    break
                self.stage_attn(l)
                if upto == "attn":
                    break
                self.stage_rwkv(l)
                if upto == "rwkv":
                    break
                self.stage_merge(l)
                if upto == "merge":
                    break
                self.stage_ffn(l, last=(l == self.L - 1))
            em.flush()
        return nc


def shared_in_map(inp, S, L):
    f = lambda a: np.ascontiguousarray(np.asarray(a, np.float32))
    m = {
        "ln0_g": f(inp["ln0_g"]), "ln0_b": f(inp["ln0_b"]),
        "w_in": f(inp["w_in"][:L]), "pool_w": f(inp["pool_w"][:L]),
        "da_lambda": f(inp["da_lambda"][:L]).reshape(L, 256),
        "rel_bias": f(inp["rel_bias"]).reshape(128),
        "rw_w2": f(inp["rw_w2"][:L]), "rw_a2": f(inp["rw_a2"][:L]), "rw_g2": f(inp["rw_g2"][:L]),
        "w_branch": f(inp["w_branch"][:L]), "w_out": f(inp["w_out"][:L]),
        "ln1_g": f(inp["ln1_g"][:L]), "ln1_b": f(inp["ln1_b"][:L]),
        "w_up": f(inp["w_up"][:L]), "w_down": f(inp["w_down"][:L]),
        "ln2_g": f(inp["ln2_g"][:L]), "ln2_b": f(inp["ln2_b"][:L]),
        "pvec": pack_pvec(inp, L),
    }
    m.update(host_consts(S))
    return m


_CACHE = {}


def kernel(**inputs):
    x = np.asarray(inputs["x"], np.float32)
    B, S, D = x.shape
    ncores = B // NSEQ
    key = (S, DEPTH)
    if key not in _CACHE:
        _CACHE[key] = Builder(S=S, L=DEPTH).build()
    nc = _CACHE[key]
    shared = shared_in_map(inputs, S, DEPTH)
    in_maps = []
    for c in range(ncores):
        m = dict(shared)
        m["x"] = np.ascontiguousarray(x[c * NSEQ:(c + 1) * NSEQ].reshape(NSEQ * S, D))
        in_maps.append(m)
    res = run_bass_kernel_spmd(nc, in_maps, core_ids=list(range(ncores)))
    out = np.stack([np.asarray(r["out"], np.float32).reshape(NSEQ, S, D) for r in res.results])
    return out.reshape(B, S, D)
```

```python
import math
import contextlib
import numpy as np
import concourse.bass as bass
import concourse.mybir as mybir
from concourse.bass_utils import run_bass_kernel_spmd

F32 = mybir.dt.float32
BF16 = mybir.dt.bfloat16
AF = mybir.ActivationFunctionType
ALU = mybir.AluOpType

D_MODEL = 1024
DEPTH = 4
NSEQ = 2
SEQ = 2048
D_IN = 7040
D_FF = 4096
ALPHA = (2.0 * DEPTH) ** 0.25
LN_EPS = 1e-5
GN_EPS = 64e-5
NPV = 96
C = 64
NCG = 4

ENGS = ("pe", "act", "dve", "pool", "sp")
N_DMA_SEMS = 24
EMBED_WAIT = True


class Em:
    def __init__(self, nc, stack):
        self.nc = nc
        self.sem = {e: stack.enter_context(nc.semaphore("s_" + e)) for e in ENGS}
        self.semobj = {"s_" + e: self.sem[e] for e in ENGS}
        for i in range(N_DMA_SEMS):
            self.semobj["d%d" % i] = stack.enter_context(nc.semaphore("d%d" % i))
        self.val = {k: 0 for k in self.semobj}
        self.seen = {e: {k: 0 for k in self.semobj} for e in ENGS}
        self.ops = {e: [] for e in ENGS}
        self.state = {}
        self.dma_rr = 0
        self.nops = 0

    def _deps(self, reads, writes):
        toks = []
        for k in reads:
            st = self.state.get(k)
            if st and st[0]:
                toks.append(st[0])
        for k in writes:
            st = self.state.get(k)
            if st:
                if st[0]:
                    toks.append(st[0])
                toks.extend(st[1])
        return toks

    def _commit(self, tok, reads, writes):
        for k in reads:
            st = self.state.setdefault(k, [None, []])
            st[1].append(tok)
            if len(st[1]) > 6:
                best = {}
                for s, v in st[1]:
                    if best.get(s, 0) < v:
                        best[s] = v
                st[1] = list(best.items())
        for k in writes:
            self.state[k] = [tok, []]

    def _waits(self, eng, toks):
        need = {}
        for s, v in toks:
            if eng == "pe" and s == "s_pe":
                continue
            if v > self.seen[eng][s] and v > need.get(s, 0):
                need[s] = v
        for s, v in need.items():
            self.seen[eng][s] = v
        return list(need.items())

    def op(self, eng, fn, reads=(), writes=()):
        waits = self._waits(eng, self._deps(reads, writes))
        s = "s_" + eng
        self.val[s] += 1
        tok = (s, self.val[s])
        self.ops[eng].append((waits, fn, s, 1))
        self._commit(tok, reads, writes)
        self.nops += 1
        return tok

    def dma(self, eng, out, in_, reads=(), writes=(), **kw):
        i = self.dma_rr
        self.dma_rr = (self.dma_rr + 1) % N_DMA_SEMS
        s = "d%d" % i
        toks = self._deps(reads, writes)
        toks.append((s, self.val[s]))
        waits = self._waits(eng, toks)
        self.val[s] += 16
        tok = (s, self.val[s])

        def fn(e, out=out, in_=in_, kw=kw):
            return e.dma_start(out=out, in_=in_, **kw)
        self.ops[eng].append((waits, fn, s, 16))
        self._commit(tok, reads, writes)
        self.nops += 1
        return tok

    def barrier(self):
        for e in ENGS:
            toks = [(s, v) for s, v in self.val.items() if v > 0]
            waits = self._waits(e, toks)
            if waits:
                self.ops[e].append((waits, None, None, 0))
        self.state = {}

    def flush(self):
        self.barrier()
        nc = self.nc
        with nc.Block() as block:
            def mk(ename):
                def body(e):
                    for waits, fn, s, inc in self.ops[ename]:
                        if fn is None or not EMBED_WAIT:
                            for ws, wv in waits:
                                e.wait_ge(self.semobj[ws], wv)
                            if fn is not None:
                                fn(e).then_inc(self.semobj[s], inc)
                            continue
                        for ws, wv in waits[:-1]:
                            e.wait_ge(self.semobj[ws], wv)
                        ins = fn(e)
                        if waits:
                            ins._wait_ge(self.semobj[waits[-1][0]], waits[-1][1])
                        ins.then_inc(self.semobj[s], inc)
                return body
            block.tensor(mk("pe"))
            block.scalar(mk("act"))
            block.vector(mk("dve"))
            block.gpsimd(mk("pool"))
            block.sync(mk("sp"))
        self.ops = {e: [] for e in ENGS}


_UID = [0]


def _nm(name):
    _UID[0] += 1
    return "%s_u%d" % (name, _UID[0])


class Rot:
    def __init__(self, nc, stack, name, n, shape, dtype):
        self.tiles = [stack.enter_context(nc.sbuf_tensor(_nm("%s_%d") % (name, i), shape, dtype))
                      for i in range(n)]
        self.name = name
        self.i = 0

    def next(self):
        t = self.tiles[self.i]
        k = (self.name, self.i)
        self.i = (self.i + 1) % len(self.tiles)
        return t, k


def _rel_bucket_np(rel):
    half = 16
    max_exact = 8
    n = np.abs(rel)
    nf = np.maximum(n, 1).astype(np.float32)
    large = max_exact + (np.log(nf / np.float32(max_exact)) / np.float32(math.log(128 / max_exact))
                         * np.float32(half - max_exact)).astype(np.int32)
    large = np.minimum(large, half - 1)
    return np.where(rel > 0, half, 0) + np.where(n < max_exact, n, large)


def host_consts(S):
    cs = {}
    cs["ident"] = np.eye(128, dtype=np.float32)
    p = np.arange(128)[:, None]
    f = np.arange(128)[None, :]
    cs["relidx"] = np.stack([_rel_bucket_np(128 * d + p - f) for d in (-1, 0, 1)]).astype(np.float32)
    t = np.arange(S)
    ic = []
    for win in (2, 4, 8, 16):
        left = win // 2
        right = win - 1 - left
        hi = np.minimum(t + right + 1, S)
        lo = np.maximum(t - left, 0)
        ic.append(1.0 / (hi - lo).astype(np.float32))
    cs["pool_icnt"] = np.stack(ic).astype(np.float32)
    r = np.arange(128)
    rl = r % 64
    same = (r[:, None] // 64) == (r[None, :] // 64)
    SL = (same & (rl[:, None] > rl[None, :])).astype(np.float32)
    SU = (same & (rl[:, None] < rl[None, :])).astype(np.float32)
    c64 = np.arange(64)
    IU = (rl[:, None] <= c64[None, :]).astype(np.float32)
    IL = (rl[:, None] >= c64[None, :]).astype(np.float32)
    cs["wkvmask"] = np.concatenate([SL, SU, IU, IL], axis=1).astype(np.float32)
    cs["bones"] = same.astype(np.float32)
    hm = np.zeros((128, 2), np.float32)
    hm[:64, 0] = 1.0
    hm[64:, 1] = 1.0
    cs["hmask"] = hm
    m = np.ones((128, S), np.float32)
    m[:, ::C] = 0.0
    cs["scanmask"] = m
    return cs


def pack_pvec(inp, L):
    out = np.zeros((L, 128, NPV), np.float32)
    for l in range(L):
        col = 0

        def put(vec):
            nonlocal col
            v = np.asarray(vec, np.float32).reshape(-1, 128)
            out[l, :, col:col + v.shape[0]] = v.T
            col += v.shape[0]
        put(inp["b_gate"][l].reshape(-1))
        put(inp["pool_scale"][l])
        put(inp["rw_mu_prev"][l])
        put(inp["rw_mu_next"][l])
        put(inp["rw_w0"][l].reshape(-1))
        put(inp["rw_a0"][l].reshape(-1))
        put(inp["rw_k_k"][l])
        put(inp["rw_k_a"][l])
        put(inp["rw_r_k"][l])
        put(inp["rw_gn_g"][l])
        put(inp["rw_gn_b"][l])
        put(inp["da_subln_g"][l])
        assert col == 95
    return out


PV_BG, PV_PS, PV_MUP, PV_MUN, PV_W0, PV_A0, PV_KK, PV_KA, PV_RK, PV_GG, PV_GB, PV_SG = \
    0, 24, 28, 43, 58, 66, 74, 78, 82, 86, 90, 94


class Builder:
    def __init__(self, S=SEQ, L=DEPTH, upto="all", dump=(), dbg=0, wd=BF16):
        self.dbg = dbg
        self.wd = wd
        self.S, self.L, self.upto, self.dump = S, L, upto, set(dump)
        self.T = NSEQ * S
        self.nc = bass.Bass("TRN2", target_bir_lowering=False)
        self.ps_i = 0

    def din(self, name, shape, dtype=F32):
        return self.nc.dram_tensor(name, list(shape), dtype, kind="ExternalInput").ap()

    def dscr(self, name, shape, dtype=F32):
        kind = "ExternalOutput" if name in self.dump else "Internal"
        return self.nc.dram_tensor(name, list(shape), dtype, kind=kind).ap()

    def psb(self):
        t = self.psum[self.ps_i]
        k = ("ps", self.ps_i)
        self.ps_i = (self.ps_i + 1) % 8
        return t, k

    def mm(self, out, lhsT, rhs, start=True, stop=True, r=(), w=()):
        return self.em.op("pe", lambda e: e.matmul(out, lhsT=lhsT, rhs=rhs, start=start, stop=stop,
                                                   skip_group_check=True), r, w)

    def tr(self, out, in_, ident, r=(), w=()):
        return self.em.op("pe", lambda e: e.transpose(out, in_, ident), r, w)

    def act(self, out, in_, func, bias=None, scale=None, r=(), w=()):
        kw = {}
        if bias is not None:
            kw["bias"] = bias
        if scale is not None:
            kw["scale"] = scale
        return self.em.op("act", lambda e: e.activation(out=out, in_=in_, func=func, **kw), r, w)

    def tt(self, eng, out, in0, in1, op, r=(), w=()):
        return self.em.op(eng, lambda e: e.tensor_tensor(out=out, in0=in0, in1=in1, op=op), r, w)

    def ts(self, eng, out, in0, s1, s2, op0, op1=None, r=(), w=()):
        if op1 is None:
            return self.em.op(eng, lambda e: e.tensor_scalar(out=out, in0=in0, scalar1=s1, scalar2=None,
                                                             op0=op0), r, w)
        return self.em.op(eng, lambda e: e.tensor_scalar(out=out, in0=in0, scalar1=s1, scalar2=s2,
                                                         op0=op0, op1=op1), r, w)

    def stt(self, out, in0, scalar, in1, op0, op1, r=(), w=()):
        return self.em.op("dve", lambda e: e.scalar_tensor_tensor(out=out, in0=in0, scalar=scalar, in1=in1,
                                                                  op0=op0, op1=op1), r, w)

    def cp(self, eng, out, in_, r=(), w=()):
        if eng == "act":
            return self.em.op("act", lambda e: e.copy(out=out, in_=in_), r, w)
        return self.em.op(eng, lambda e: e.tensor_copy(out=out, in_=in_), r, w)

    def memset(self, eng, ap, val, w=()):
        return self.em.op(eng, lambda e: e.memset(ap, val), (), w)

    def recip(self, out, in_, r=(), w=()):
        return self.em.op("dve", lambda e: e.reciprocal(out=out, in_=in_), r, w)

    def declare(self):
        S, T, L = self.S, self.T, self.L
        d = self.din
        self.x = d("x", [T, D_MODEL])
        self.ln0_g = d("ln0_g", [D_MODEL]); self.ln0_b = d("ln0_b", [D_MODEL])
        self.w_in = d("w_in", [L, D_MODEL, D_IN])
        self.pool_w = d("pool_w", [L, 4, 128, 128])
        self.da_lambda = d("da_lambda", [L, 256])
        self.rel_bias = d("rel_bias", [128])
        self.rw_w2 = d("rw_w2", [L, 2, 64, 512]); self.rw_a2 = d("rw_a2", [L, 2, 64, 512])
        self.rw_g2 = d("rw_g2", [L, 128, 512])
        self.w_branch = d("w_branch", [L, 3, 512, D_MODEL])
        self.w_out = d("w_out", [L, D_MODEL, D_MODEL])
        self.ln1_g = d("ln1_g", [L, D_MODEL]); self.ln1_b = d("ln1_b", [L, D_MODEL])
        self.w_up = d("w_up", [L, D_MODEL, D_FF]); self.w_down = d("w_down", [L, D_FF, D_MODEL])
        self.ln2_g = d("ln2_g", [L, D_MODEL]); self.ln2_b = d("ln2_b", [L, D_MODEL])
        self.pvec_d = d("pvec", [L, 128, NPV])
        self.c_ident = d("ident", [128, 128]); self.c_relidx = d("relidx", [3, 128, 128])
        self.c_icnt = d("pool_icnt", [4, S]); self.c_wkvmask = d("wkvmask", [128, 384])
        self.c_bones = d("bones", [128, 128]); self.c_hmask = d("hmask", [128, 2])
        self.c_scanmask = d("scanmask", [128, S])
        self.out = self.nc.dram_tensor("out", [T, D_MODEL], F32, kind="ExternalOutput").ap()
        s = self.dscr
        self.x_tm = s("x_tm", [T, D_MODEL])
        self.x1_tm = s("x1_tm", [T, D_MODEL])
        self.uT = s("uT", [512, T]); self.qT = s("qT", [512, T], BF16); self.kT = s("kT", [512, T], BF16)
        self.v_tm = s("v_tm", [T, 512], BF16)
        self.zrT = s("zrT", [1920, T]); self.gT = s("gT", [3072, T])
        self.yT = s("yT", [3, 512, T], BF16)
        self.shT = s("shT", [3, 128, T])
        self.wup_bf = s("wup_bf", [128, 8, D_FF], BF16)
        self.wdn_bf = s("wdn_bf", [8, 128, 32, 128], BF16)

    def ln_tile(self, src, ksrc, dst, kdst, g_bc, b_bc, kgb):
        st, kst = self.lnst.next()
        em = self.em
        em.op("dve", lambda e: e.bn_stats(out=st[:, 0:6], in_=src[:, 0:512]), [ksrc], [kst])
        em.op("dve", lambda e: e.bn_stats(out=st[:, 6:12], in_=src[:, 512:1024]), [ksrc], [kst])
        em.op("dve", lambda e: e.bn_aggr(out=st[:, 12:14], in_=st[:, 0:12]), [kst], [kst])
        self.ts("dve", st[:, 14:15], st[:, 13:14], LN_EPS, None, ALU.add, r=[kst], w=[kst])
        self.act(st[:, 14:15], st[:, 14:15], AF.Sqrt, r=[kst], w=[kst])
        self.recip(st[:, 14:15], st[:, 14:15], r=[kst], w=[kst])
        self.ts("dve", dst[:], src[:], st[:, 12:13], st[:, 14:15], ALU.subtract, ALU.mult, r=[ksrc, kst], w=[kdst])
        self.tt("dve", dst[:], dst[:], g_bc[:], ALU.mult, r=[kdst, kgb], w=[kdst])
        self.tt("pool", dst[:], dst[:], b_bc[:], ALU.add, r=[kdst, kgb], w=[kdst])

    def transpose_into_xT(self, src, ksrc, tt):
        for half in range(2):
            ps, kps = self.psb()
            for c4 in range(4):
                c = half * 4 + c4
                self.tr(ps[:, c4 * 128:(c4 + 1) * 128], src[:, c * 128:(c + 1) * 128], self.ident_f[:],
                        r=[ksrc, "ident"], w=[kps])
            eng = "act" if half == 0 else "dve"
            self.cp(eng, self.xT[:, half * 4:half * 4 + 4, tt * 128:(tt + 1) * 128],
                    ps[:].rearrange("p (c t) -> p c t", c=4), r=[kps], w=[("xT", tt)])

    def load_bc(self, tile, key, vec_ap):
        self.em.dma("sp", tile[:], vec_ap.partition_broadcast(128), writes=[key])

    def stage_ln_in(self):
        nc, em = self.nc, self.em
        with contextlib.ExitStack() as st:
            xin = Rot(nc, st, "lnx", 2, [128, 1024], F32)
            xo = Rot(nc, st, "lno", 2, [128, 1024], F32)
            g_bc = st.enter_context(nc.sbuf_tensor(_nm("lng"), [128, 1024], F32))
            b_bc = st.enter_context(nc.sbuf_tensor(_nm("lnb"), [128, 1024], F32))
            self.load_bc(g_bc, "lngb", self.ln0_g)
            self.load_bc(b_bc, "lngb", self.ln0_b)
            for tt in range(self.T // 128):
                xt, kx = xin.next()
                em.dma("sp", xt[:], self.x[tt * 128:(tt + 1) * 128, :], writes=[kx])
                xn, kn = xo.next()
                self.ln_tile(xt, kx, xn, kn, g_bc, b_bc, "lngb")
                em.dma("sp", self.x_tm[tt * 128:(tt + 1) * 128, :], xn[:], reads=[kn], writes=[("x_tm", tt)])
                self.transpose_into_xT(xn, kn, tt)
            em.flush()

    def stage_proj(self, l):
        nc, em, T = self.nc, self.em, self.T
        with contextlib.ExitStack() as st:
            wst = Rot(nc, st, "wst", 2, [128, 8, 512], F32)
            wbf = Rot(nc, st, "wbf", 2, [128, 8, 512], BF16)
            ev32 = Rot(nc, st, "ev32", 4, [128, 512], F32)
            ev16 = Rot(nc, st, "ev16", 4, [128, 512], BF16)
            wv = self.w_in[l].rearrange("(c p) n -> p c n", p=128)
            ntr = T // 512
            xkeys = [("xT", tt) for tt in range(T // 128)]
            for cg in range(14):
                c0 = cg * 512
                cw = min(512, D_IN - c0)
                ws, wsk = wst.next()
                em.dma("sp", ws[:, :, 0:cw], wv[:, :, c0:c0 + cw], writes=[wsk])
                wb, wbk = wbf.next()
                self.cp("pool", wb[:, :, 0:cw], ws[:, :, 0:cw], r=[wsk], w=[wbk])
                if cg == 3:
                    for tt in range(T // 128):
                        ps, kps = self.psb()
                        for dc in range(8):
                            self.mm(ps[:], self.xT[:, dc, tt * 128:(tt + 1) * 128], wb[:, dc, :],
                                    start=(dc == 0), stop=(dc == 7), r=[wbk, ("xT", tt)], w=[kps])
                        ev, kev = ev16.next()
                        self.cp("dve" if tt % 2 else "act", ev[:], ps[:], r=[kps], w=[kev])
                        em.dma("sp", self.v_tm[tt * 128:(tt + 1) * 128, :], ev[:], reads=[kev], writes=[("v_tm", tt)])
                    continue
                for cb in range(cw // 128):
                    gb = cg * 4 + cb
                    for tr_ in range(ntr):
                        ps, kps = self.psb()
                        xk = xkeys[tr_ * 4:(tr_ + 1) * 4]
                        for dc in range(8):
                            self.mm(ps[:], wb[:, dc, cb * 128:(cb + 1) * 128], self.xT[:, dc, tr_ * 512:(tr_ + 1) * 512],
                                    start=(dc == 0), stop=(dc == 7), r=[wbk] + xk, w=[kps])
                        tsl = slice(tr_ * 512, (tr_ + 1) * 512)
                        if gb < 4:
                            ev, kev = ev32.next()
                            self.cp("dve", ev[:], ps[:], r=[kps], w=[kev])
                            dst = self.uT[gb * 128:(gb + 1) * 128, tsl]
                        elif gb < 8:
                            ev, kev = ev16.next()
                            self.ts("dve", ev[:], ps[:], 0.125, None, ALU.mult, r=[kps], w=[kev])
                            dst = self.qT[(gb - 4) * 128:(gb - 3) * 128, tsl]
                        elif gb < 12:
                            ev, kev = ev16.next()
                            self.cp("dve", ev[:], ps[:], r=[kps], w=[kev])
                            dst = self.kT[(gb - 8) * 128:(gb - 7) * 128, tsl]
                        elif gb < 31:
                            ev, kev = ev32.next()
                            self.cp("dve" if (gb + tr_) % 2 else "act", ev[:], ps[:], r=[kps], w=[kev])
                            dst = self.zrT[(gb - 16) * 128:(gb - 15) * 128, tsl]
                        else:
                            ev, kev = ev32.next()
                            gi = gb - 31
                            self.act(ev[:], ps[:], AF.Sigmoid, bias=self.pv[:, PV_BG + gi:PV_BG + gi + 1],
                                     r=[kps, "pv"], w=[kev])
                            dst = self.gT[gi * 128:(gi + 1) * 128, tsl]
                        em.dma("sp", dst, ev[:], reads=[kev], writes=[("scr", gb, tr_)])
            em.flush()

    def stage_pool(self, l):
        nc, em, S = self.nc, self.em, self.S
        with contextlib.ExitStack() as st:
            pw32 = st.enter_context(nc.sbuf_tensor(_nm("pw32"), [128, 4, 128], F32))
            pwbf = st.enter_context(nc.sbuf_tensor(_nm("pwbf"), [128, 4, 128], BF16))
            em.dma("sp", pw32[:], self.pool_w[l].rearrange("g c d -> c g d"), writes=["pw32"])
            self.cp("dve", pwbf[:], pw32[:], r=["pw32"], w=["pwbf"])
            icr = Rot(nc, st, "icnt", 2, [128, S], F32)
            ub = Rot(nc, st, "ub", 2, [128, S + 32], F32)
            sa = Rot(nc, st, "psa", 2, [128, S + 32], F32)
            sb = Rot(nc, st, "psb_", 2, [128, S + 32], F32)
            mxr = Rot(nc, st, "pmx", 2, [128, S], BF16)
            evr = Rot(nc, st, "pev", 3, [128, 512], BF16)
            for t_, k_ in zip(ub.tiles, [("ub", 0), ("ub", 1)]):
                self.memset("pool", t_[:], 0.0, w=[k_])
            for g in range(4):
                ic, kic = icr.next()
                em.dma("sp", ic[:], self.c_icnt[g:g + 1, :].partition_broadcast(128), writes=[kic])
                win = 2 ** (g + 1)
                right = win - 1 - win // 2
                for b in range(NSEQ):
                    u, ku = ub.next()
                    em.dma("sp", u[:, 16:16 + S], self.uT[g * 128:(g + 1) * 128, b * S:(b + 1) * S], writes=[ku])
                    cur, kcur = u, ku
                    for k in range(g + 1):
                        sh = 2 ** k
                        lo = 2 ** (k + 1) - 1
                        nxt, knxt = (sa if k % 2 == 0 else sb).next()
                        self.tt("pool" if k % 2 else "dve", nxt[:, lo:S + 32], cur[:, lo:S + 32],
                                cur[:, lo - sh:S + 32 - sh], ALU.add, r=[kcur], w=[knxt])
                        cur, kcur = nxt, knxt
                    tmp, ktmp = (sa if (g + 1) % 2 == 0 else sb).next()
                    self.tt("dve", tmp[:, 0:S], cur[:, 16 + right:16 + right + S], ic[:], ALU.mult,
                            r=[kcur, kic], w=[ktmp])
                    mx, kmx = mxr.next()
                    self.tt("pool", mx[:], tmp[:, 0:S], u[:, 16:16 + S], ALU.subtract, r=[ktmp, ku], w=[kmx])
                    for tr_ in range(S // 512):
                        ps, kps = self.psb()
                        self.mm(ps[:], pwbf[:, g, :], mx[:, tr_ * 512:(tr_ + 1) * 512], r=["pwbf", kmx], w=[kps])
                        ev, kev = evr.next()
                        self.ts("dve", ev[:], ps[:], self.pv[:, PV_PS + g:PV_PS + g + 1], None, ALU.mult,
                                r=[kps, "pv"], w=[kev])
                        em.dma("sp", self.yT[0, g * 128:(g + 1) * 128, b * S + tr_ * 512:b * S + (tr_ + 1) * 512],
                               ev[:], reads=[kev], writes=[("yT0", g, b, tr_)])
            em.flush()

    def build_band(self):
        nc, em = self.nc, self.em
        with contextlib.ExitStack() as st:
            idx = st.enter_context(nc.sbuf_tensor(_nm("ridx"), [128, 3, 128], F32))
            eq = Rot(nc, st, "req", 2, [128, 128], F32)
            em.dma("sp", idx[:], self.c_relidx.rearrange("d p f -> p d f"), writes=["ridx"])
            em.dma("sp", self.relb[:], self.rel_bias.partition_broadcast(128), writes=["relb"])
            self.memset("dve", self.band[:], 0.0, w=["band"])
            for d3 in range(3):
                for bk in range(32):
                    e_, ke = eq.next()
                    self.ts("pool", e_[:], idx[:, d3, :], float(bk), None, ALU.is_equal, r=["ridx"], w=[ke])
                    for h in range(4):
                        self.stt(self.band[:, h, d3, :], e_[:], self.relb[:, bk * 4 + h:bk * 4 + h + 1],
                                 self.band[:, h, d3, :], ALU.mult, ALU.add, r=[ke, "relb", "band"], w=["band"])
            em.flush()

    def stage_attn(self, l):
        nc, em, S = self.nc, self.em, self.S
        nqr, nkb = S // 512, S // 128
        lam_init = 0.8 - 0.6 * math.exp(-0.3 * l)
        with contextlib.ExitStack() as st:
            strips = st.enter_context(nc.sbuf_tensor(_nm("strips"), [128, 4, 6, 512], F32))
            zeros = st.enter_context(nc.sbuf_tensor(_nm("azero"), [128, 128], F32))
            ones_b = st.enter_context(nc.sbuf_tensor(_nm("aones_b"), [128, 128], BF16))
            ones_f = st.enter_context(nc.sbuf_tensor(_nm("aones_f"), [128, 128], F32))
            lamv = st.enter_context(nc.sbuf_tensor(_nm("lamv"), [128, 256], F32))
            lsm = st.enter_context(nc.sbuf_tensor(_nm("lsm"), [128, 8], F32))
            self.memset("dve", zeros[:], 0.0, w=["azero"])
            self.memset("dve", ones_b[:], 1.0, w=["aones_b"])
            self.memset("dve", ones_f[:], 1.0, w=["aones_f"])
            for h in range(4):
                for dl in range(-1, 5):
                    for bq in range(4):
                        d = dl - bq
                        dst = strips[:, h, dl + 1, bq * 128:(bq + 1) * 128]
                        if abs(d) <= 1:
                            self.cp("pool", dst, self.band[:, h, d + 1, :], r=["band"], w=["strips"])
                        else:
                            col = (31 if d >= 2 else 15) * 4 + h
                            self.ts("dve", dst, zeros[:], self.relb[:, col:col + 1], None, ALU.add,
                                    r=["azero", "relb"], w=["strips"])
            em.dma("sp", lamv[:], self.da_lambda[l:l + 1, :].partition_broadcast(128), writes=["lamv"])
            self.tt("dve", lamv[:, 0:64], lamv[:, 0:64], lamv[:, 64:128], ALU.mult, r=["lamv"], w=["lamv"])
            self.tt("dve", lamv[:, 128:192], lamv[:, 128:192], lamv[:, 192:256], ALU.mult, r=["lamv"], w=["lamv"])
            em.op("dve", lambda e: e.tensor_reduce(out=lsm[:, 0:1], in_=lamv[:, 0:64], axis=mybir.AxisListType.X,
                                                   op=ALU.add), ["lamv"], ["lsm"])
            em.op("dve", lambda e: e.tensor_reduce(out=lsm[:, 1:2], in_=lamv[:, 128:192], axis=mybir.AxisListType.X,
                                                   op=ALU.add), ["lamv"], ["lsm"])
            self.act(lsm[:, 2:4], lsm[:, 0:2], AF.Exp, r=["lsm"], w=["lsm"])
            self.tt("dve", lsm[:, 4:5], lsm[:, 3:4], lsm[:, 2:3], ALU.subtract, r=["lsm"], w=["lsm"])
            self.ts("dve", lsm[:, 4:5], lsm[:, 4:5], -lam_init, None, ALU.add, r=["lsm"], w=["lsm"])
            self.ts("dve", lsm[:, 5:6], self.pv[:, PV_SG:PV_SG + 1], 1.0 - lam_init, None, ALU.mult,
                    r=["pv", "lsm"], w=["lsm"])
            neglam = lsm[:, 4:5]
            sgl = lsm[:, 5:6]

            qr = Rot(nc, st, "aq", 2, [128, S], BF16)
            kr = [Rot(nc, st, "ak%d" % m_, 2, [128, S], BF16) for m_ in range(2)]
            for m_ in range(2):
                for i_, t_ in enumerate(kr[m_].tiles):
                    self.memset("pool" if i_ else "dve", t_[:], 0.0, w=[("ak%d" % m_, i_)])
            vr = Rot(nc, st, "av", 2, [128, nkb, 128], BF16)
            ptr = Rot(nc, st, "apt", 5, [128, 512], BF16)
            tmr = Rot(nc, st, "atm", 3, [128, 512], F32)
            e1 = Rot(nc, st, "ae1", 2, [128, 512], F32)
            e2 = Rot(nc, st, "ae2", 2, [128, 512], F32)
            e3 = Rot(nc, st, "ae3", 2, [128, 512], F32)
            e4 = Rot(nc, st, "ae4", 2, [128, 512], F32)
            yo = Rot(nc, st, "ayo", 2, [128, 512], BF16)
            zacc = [st.enter_context(nc.sbuf_tensor(_nm("azacc%d" % m_), [128, 512], F32)) for m_ in range(2)]
            P = self.psum
            sbank = 0
            for b in range(NSEQ):
                for h in range(4):
                    qh, kq = qr.next(); vh, kv = vr.next()
                    khm = [kr[0].next(), kr[1].next()]
                    em.dma("sp", qh[:], self.qT[h * 128:(h + 1) * 128, b * S:(b + 1) * S], writes=[kq])
                    for m_ in range(2):
                        em.dma("sp", khm[m_][0][m_ * 64:(m_ + 1) * 64, :],
                               self.kT[h * 128 + m_ * 64:h * 128 + (m_ + 1) * 64, b * S:(b + 1) * S], writes=[khm[m_][1]])
                    em.dma("sp", vh[:], self.v_tm[b * S:(b + 1) * S, h * 128:(h + 1) * 128]
                           .rearrange("(j p) c -> p j c", p=128), writes=[kv])
                    for i in range(nqr):
                        tiles = [(m, j) for j in range(nkb) for m in range(2)]
                        LOOK = 2
                        pts = []

                        def emit_qk(m, j, i=i, h=h, qh=qh, khm=khm, kq=kq):
                            nonlocal sbank
                            sb_i = 4 + sbank
                            sbank = (sbank + 1) % 3
                            ps, kps = P[sb_i], ("ps", sb_i)
                            self.mm(ps[:], khm[m][0][:, j * 128:(j + 1) * 128],
                                    qh[:, i * 512:(i + 1) * 512], r=[khm[m][1], kq], w=[kps])
                            dl = j - 4 * i
                            pt, kpt = ptr.next()
                            if -1 <= dl <= 4:
                                tm, ktm = tmr.next()
                                self.tt("dve", tm[:], ps[:], strips[:, h, dl + 1, :], ALU.add,
                                        r=[kps, "strips"], w=[ktm])
                                self.act(pt[:], tm[:], AF.Exp, r=[ktm], w=[kpt])
                            else:
                                col = (31 if dl >= 5 else 15) * 4 + h
                                self.act(pt[:], ps[:], AF.Exp, bias=self.relb[:, col:col + 1],
                                         r=[kps, "relb"], w=[kpt])
                            return pt, kpt

                        def emit_pv(m, j, pt, kpt, vh=vh, kv=kv):
                            O, kO = P[2 * m], ("ps", 2 * m)
                            Z, kZ = P[2 * m + 1], ("ps", 2 * m + 1)
                            self.mm(O[:], vh[:, j, :], pt[:], start=(j == 0), stop=(j == nkb - 1),
                                    r=[kv, kpt], w=[kO])
                            self.mm(Z[:], ones_b[:], pt[:], start=(j == 0), stop=(j == nkb - 1),
                                    r=["aones_b", kpt], w=[kZ])
                        for t_ in range(len(tiles) + LOOK):
                            if t_ < len(tiles):
                                pts.append(emit_qk(*tiles[t_]))
                            if t_ >= LOOK:
                                emit_pv(*tiles[t_ - LOOK], *pts[t_ - LOOK])
                        a1, ka1 = e1.next(); a2, ka2 = e2.next(); a3, ka3 = e3.next(); a4, ka4 = e4.next()
                        self.cp("act", a2[:], P[0][:], r=[("ps", 0)], w=[ka2])
                        self.act(a1[:], P[1][:], AF.Ln, r=[("ps", 1)], w=[ka1])
                        self.cp("act", a3[:], P[2][:], r=[("ps", 2)], w=[ka3])
                        self.act(a4[:], P[3][:], AF.Ln, r=[("ps", 3)], w=[ka4])
                        self.act(a1[:], a1[:], AF.Exp, scale=-1.0, r=[ka1], w=[ka1])
                        self.act(a4[:], a4[:], AF.Exp, scale=-1.0, r=[ka4], w=[ka4])
                        self.tt("pool", a2[:], a2[:], a1[:], ALU.mult, r=[ka2, ka1], w=[ka2])
                        self.tt("pool", a3[:], a3[:], a4[:], ALU.mult, r=[ka3, ka4], w=[ka3])
                        self.stt(a2[:], a3[:], neglam, a2[:], ALU.mult, ALU.add, r=[ka3, ka2, "lsm"], w=[ka2])
                        self.tt("pool", a3[:], a2[:], a2[:], ALU.mult, r=[ka2, ka3], w=[ka3])
                        self.mm(P[7][:], ones_f[:], a3[:], r=["aones_f", ka3], w=[("ps", 7)])
                        self.act(a1[:], P[7][:], AF.Ln, bias=self.epsv[:, 0:1], scale=1.0 / 128.0,
                                 r=[("ps", 7), ka1, "epsv"], w=[ka1])
                        self.act(a1[:], a1[:], AF.Exp, scale=-0.5, r=[ka1], w=[ka1])
                        y_, ky = yo.next()
                        self.stt(y_[:], a2[:], sgl, a1[:], ALU.mult, ALU.mult, r=[ka2, ka1, "lsm"], w=[ky])
                        em.dma("sp", self.yT[1, h * 128:(h + 1) * 128, b * S + i * 512:b * S + (i + 1) * 512],
                               y_[:], reads=[ky], writes=[("yT1", h, b, i)])
            em.flush()

    def stage_rwkv(self, l):
        nc, em, S = self.nc, self.em, self.S
        WD = self.wd
        GT = NCG * C
        ngr = S // GT
        n512 = S // 512
        NEG_E = -math.exp(-0.5)
        pv = self.pv
        with contextlib.ExitStack() as st:
            def alloc(name, shape, dt=F32):
                return st.enter_context(nc.sbuf_tensor(_nm(name), shape, dt))
            msk = alloc("wmsk", [128, 384])
            bones = alloc("wbones", [128, 128]); bones64 = alloc("wbones64", [128, 128])
            hmask = alloc("whmask", [128, 2])
            scanm = alloc("wscanm", [128, GT])
            w2sb = alloc("w2sb", [128, 512]); a2sb = alloc("a2sb", [128, 512]); g2sb = alloc("g2sb", [128, 512])
            c0v = alloc("wc0", [128, 16]); omka = alloc("womka", [128, 4])
            em.dma("sp", msk[:], self.c_wkvmask, writes=["wmsk"])
            em.dma("sp", bones[:], self.c_bones, writes=["wbones"])
            em.dma("sp", hmask[:], self.c_hmask, writes=["whmask"])
            em.dma("sp", scanm[:], self.c_scanmask[:, 0:GT], writes=["wscanm"])
            em.dma("sp", w2sb[:], self.rw_w2[l].rearrange("d r c -> (d r) c"), writes=["w2sb"])
            em.dma("sp", a2sb[:], self.rw_a2[l].rearrange("d r c -> (d r) c"), writes=["a2sb"])
            em.dma("sp", g2sb[:], self.rw_g2[l], writes=["g2sb"])
            self.ts("dve", bones64[:], bones[:], 1.0 / 64.0, None, ALU.mult, r=["wbones"], w=["wbones64"])
            self.tt("dve", c0v[:, 0:15], pv[:, PV_MUP:PV_MUP + 15], pv[:, PV_MUN:PV_MUN + 15], ALU.add, r=["pv"], w=["wc0"])
            self.ts("dve", c0v[:, 0:15], c0v[:, 0:15], -1.0, 1.0, ALU.mult, ALU.add, r=["wc0"], w=["wc0"])
            self.ts("dve", omka[:], pv[:, PV_KA:PV_KA + 4], -1.0, 1.0, ALU.mult, ALU.add, r=["pv"], w=["womka"])
            SLm, SUm, IUm, ILm = msk[:, 0:128], msk[:, 128:256], msk[:, 256:320], msk[:, 320:384]

            big = self.xT[:].rearrange("p c t -> p (c t)").bitcast(F32).rearrange("p (a s) -> p a s", a=8)
            r_, k_, v_, kkn, yacc, logw, a_, kd = [big[:, i, :] for i in range(8)]
            kb = ["xT%d" % i for i in range(self.T // 128)]
            K_R, K_K, K_V, K_KK, K_Y, K_LW, K_A, K_KD = ["wbig%d" % i for i in range(8)]

            zb = alloc("wzb", [128, S + 2])
            self.memset("dve", zb[:], 0.0, w=["wzb"])
            shr = Rot(nc, st, "wsh", 6, [128, 512], F32)
            t5 = Rot(nc, st, "wt5", 4, [128, 512], F32)
            yo = Rot(nc, st, "wyo", 2, [128, 512], BF16)
            gsets = []
            for ss_ in range(2):
                g_ = {}
                for nm in ("G", "tA", "E1", "E2", "rt", "E3", "ba", "tB", "E4"):
                    g_[nm] = alloc("wg%d_%s" % (ss_, nm), [128, GT])
                g_["rtb"] = alloc("wg%d_rtb" % ss_, [128, GT], WD)
                g_["gt"] = alloc("wg%d_gt" % ss_, [128, NCG]); g_["gam"] = alloc("wg%d_gam" % ss_, [128, NCG])
                for nm in ("atp", "btp", "ktp", "atT"):
                    g_[nm] = alloc("wg%d_%s" % (ss_, nm), [128, NCG, 128], WD)
                gsets.append(g_)
                for nm in ("btp", "ktp"):
                    self.memset("dve", g_[nm][:], 0.0, w=["wg%d_%s" % (ss_, nm)])
            ident_w = alloc("wident_w", [128, 128], WD)
            self.cp("dve", ident_w[:], self.ident_f[:], r=["ident"], w=["wident_w"])
            padr = Rot(nc, st, "wpad", 8, [128, NCG, 128], F32)
            for i_, t_ in enumerate(padr.tiles):
                self.memset("pool" if i_ % 2 else "dve", t_[:], 0.0, w=[("wpad", i_)])
            setT = [{nm: alloc("w%s%d" % (nm, s_), [128, NCG, 128], WD) for nm in ("vT", "bhT", "khT")}
                    for s_ in range(3)]
            U = []
            for ci in range(NCG):
                d_ = {}
                for nm in ("X0", "X1", "XT0", "XT1", "PT0", "PT1", "AKT", "AKV"):
                    d_[nm] = alloc("wu%s_%d" % (nm, ci), [128, 128], WD)
                for s_ in range(2):
                    d_["WU%d" % s_] = alloc("wuWU%d_%d" % (s_, ci), [128, 256], WD)
                    d_["M%d" % s_] = alloc("wuM%d_%d" % (s_, ci), [128, 128], F32)
                    d_["ARBT%d" % s_] = alloc("wuARBT%d_%d" % (s_, ci), [128, 64], WD)
                    d_["ARKT%d" % s_] = alloc("wuARKT%d_%d" % (s_, ci), [128, 64], WD)
                    d_["QT%d" % s_] = alloc("wuQT%d_%d" % (s_, ci), [128, 64], F32)
                U.append(d_)
            STt = [alloc("wST%d" % i, [128, 128], F32) for i in range(2)]
            ident = self.ident_f

            def shift_block(blk, b, dst, kdst, post=None):
                em.dma("sp", zb[:, 1:S + 1], self.zrT[blk * 128:(blk + 1) * 128, b * S:(b + 1) * S], writes=["wzb"])
                self.act(dst, zb[:, 1:S + 1], AF.Identity, scale=c0v[:, blk:blk + 1], r=["wzb", "wc0"], w=[kdst])
                self.stt(dst, zb[:, 0:S], pv[:, PV_MUP + blk:PV_MUP + blk + 1], dst, ALU.mult, ALU.add,
                         r=["wzb", "pv", kdst], w=[kdst])
                self.stt(dst, zb[:, 2:S + 2], pv[:, PV_MUN + blk:PV_MUN + blk + 1], dst, ALU.mult, ALU.add,
                         r=["wzb", "pv", kdst], w=[kdst])
                if post is not None:
                    self.act(dst, dst, post, r=[kdst], w=[kdst])

            cnt = [0]

            def evac_copy(out, in_, r, w):
                cnt[0] += 1
                self.cp("act" if cnt[0] % 2 else "dve", out, in_, r=r, w=w)

            for b in range(NSEQ):
                for si, (blk, post) in enumerate(((12, AF.Tanh), (13, None), (14, AF.Sigmoid))):
                    shift_block(blk, b, r_, K_R, post)
                    em.dma("sp", self.shT[si, :, b * S:(b + 1) * S], r_, reads=[K_R], writes=[("shT", si, b)])
                if self.dbg == 1:
                    em.flush(); return
                for hp in range(4):
                    shift_block(hp, b, r_, K_R)
                    shift_block(4 + hp, b, k_, K_K)
                    shift_block(8 + hp, b, v_, K_V)
                    self.act(kkn, k_, AF.Identity, scale=pv[:, PV_KK + hp:PV_KK + hp + 1], r=[K_K, "pv"], w=[K_KK])
                    for q4 in range(n512):
                        rg = slice(q4 * 512, (q4 + 1) * 512)
                        t1, k1 = t5.next()
                        self.tt("pool", t1[:], kkn[:, rg], kkn[:, rg], ALU.mult, r=[K_KK], w=[k1])
                        ps, kps = self.psb()
                        self.mm(ps[:], bones[:], t1[:], r=["wbones", k1], w=[kps])
                        self.ts("dve", t1[:], ps[:], 1e-24, None, ALU.max, r=[kps, k1], w=[k1])
                        self.act(t1[:], t1[:], AF.Ln, r=[k1], w=[k1])
                        self.act(t1[:], t1[:], AF.Exp, scale=-0.5, r=[k1], w=[k1])
                        self.tt("dve", kkn[:, rg], kkn[:, rg], t1[:], ALU.mult, r=[K_KK, k1], w=[K_KK])
                    if self.dbg == 2:
                        em.flush(); return
                    for d in range(2):
                        rev = (d == 1)
                        dsl = slice(d * 64, (d + 1) * 64)
                        for q4 in range(n512):
                            rg = slice(q4 * 512, (q4 + 1) * 512)
                            grg = slice(b * S + q4 * 512, b * S + (q4 + 1) * 512)
                            s1, ks1 = shr.next()
                            em.dma("sp", s1[:], self.shT[0, :, grg], reads=[("shT", 0, b)], writes=[ks1])
                            ps, kps = self.psb()
                            self.mm(ps[:], w2sb[dsl, hp * 128:(hp + 1) * 128], s1[dsl, :], r=["w2sb", ks1], w=[kps])
                            t1, k1 = t5.next()
                            self.act(t1[:], ps[:], AF.Sigmoid, bias=pv[:, PV_W0 + d * 4 + hp:PV_W0 + d * 4 + hp + 1],
                                     r=[kps, "pv"], w=[k1])
                            self.ts("pool", logw[:, rg], t1[:], NEG_E, None, ALU.mult, r=[k1], w=[K_LW])
                            s2, ks2 = shr.next()
                            em.dma("sp", s2[:], self.shT[1, :, grg], reads=[("shT", 1, b)], writes=[ks2])
                            ps, kps = self.psb()
                            self.mm(ps[:], a2sb[dsl, hp * 128:(hp + 1) * 128], s2[dsl, :], r=["a2sb", ks2], w=[kps])
                            self.act(a_[:, rg], ps[:], AF.Sigmoid, bias=pv[:, PV_A0 + d * 4 + hp:PV_A0 + d * 4 + hp + 1],
                                     r=[kps, "pv"], w=[K_A])
                        self.ts("dve", kd, a_, pv[:, PV_KA + hp:PV_KA + hp + 1], omka[:, hp:hp + 1], ALU.mult, ALU.add,
                                r=[K_A, "pv", "womka"], w=[K_KD])
                        self.tt("dve", kd, kd, k_, ALU.mult, r=[K_KD, K_K], w=[K_KD])
                        if self.dbg == 3:
                            em.flush(); return
                        self.memset("pool", STt[0][:], 0.0, w=["wST0"])
                        stcur = [0]
                        MT = SLm if rev else SUm
                        MN = SUm if rev else SLm
                        MI = ILm if rev else IUm
                        groups = list(range(ngr))
                        if rev:
                            groups = groups[::-1]
                        H2 = (slice(0, 64), slice(64, 128))

                        def g3(ap):
                            return ap.rearrange("p (c t) -> p c t", c=NCG)

                        def blk(t, h):
                            return t[H2[h], :, h * 64:(h + 1) * 64]

                        def hv(ap, h):
                            return ap[H2[h]].rearrange("p (c t) -> p c t", c=NCG)

                        def uk(ci, nm):
                            return ("wu", ci, nm)

                        def prep_ops(gi):
                            grp = groups[gi]
                            ss = gi % 2
                            g_ = gsets[ss]
                            kq = lambda nm: "wg%d_%s" % (ss, nm)
                            gs = slice(grp * GT, (grp + 1) * GT)
                            G = g_["G"]
                            gt_, gam_ = g_["gt"], g_["gam"]
                            atp_, btp_, ktp_, atT_ = g_["atp"], g_["btp"], g_["ktp"], g_["atT"]
                            s3i = gi % 3
                            TS = setT[s3i]
                            pds = [padr.next() for _ in range(4)]
                            (pdA, kpdA), (pdB, kpdB), (pdK, kpdK), (pdV, kpdV) = pds
                            ops = []
                            A = ops.append
                            A(lambda: em.op("dve", lambda e: e.tensor_tensor_scan(
                                out=G[:], data0=scanm[:], data1=logw[:, gs], initial=0.0, op0=ALU.mult, op1=ALU.add),
                                ["wscanm", K_LW], [kq("G")]))
                            A(lambda: self.cp("dve", gt_[:], g3(G[:])[:, :, C - 1], r=[kq("G")], w=[kq("gt")]))
                            if rev:
                                A(lambda: self.tt("dve", g3(G[:]), gt_[:].unsqueeze(2).broadcast_to([128, NCG, C]), g3(G[:]),
                                                  ALU.subtract, r=[kq("gt"), kq("G")], w=[kq("G")]))
                                A(lambda: self.tt("dve", G[:], G[:], logw[:, gs], ALU.add, r=[kq("G"), K_LW], w=[kq("G")]))
                            A(lambda: self.tt("dve", g_["tA"][:], G[:], logw[:, gs], ALU.subtract, r=[kq("G"), K_LW], w=[kq("tA")]))
                            A(lambda: self.act(g_["E1"][:], g_["tA"][:], AF.Exp, r=[kq("tA")], w=[kq("E1")]))
                            for h in range(2):
                                A(lambda h=h: self.stt(blk(pdA, h), hv(kkn[:, gs], h), -1.0, hv(g_["E1"][:], h), ALU.mult, ALU.mult,
                                                       r=[K_KK, kq("E1")], w=[kpdA]))
                            A(lambda: self.cp("act", atp_[:], pdA[:], r=[kpdA], w=[kq("atp")]))
                            A(lambda: self.act(g_["E2"][:], G[:], AF.Exp, r=[kq("G")], w=[kq("E2")]))
                            A(lambda: self.tt("pool", g_["rt"][:], r_[:, gs], g_["E2"][:], ALU.mult, r=[K_R, kq("E2")], w=[kq("rt")]))
                            A(lambda: self.cp("pool", g_["rtb"][:], g_["rt"][:], r=[kq("rt")], w=[kq("rtb")]))
                            A(lambda: self.act(g_["E3"][:], G[:], AF.Exp, scale=-1.0, r=[kq("G")], w=[kq("E3")]))
                            A(lambda: self.tt("pool", g_["ba"][:], kkn[:, gs], a_[:, gs], ALU.mult, r=[K_KK, K_A], w=[kq("ba")]))
                            for h in range(2):
                                A(lambda h=h: self.tt("dve" if h else "pool", blk(btp_, h), hv(g_["ba"][:], h), hv(g_["E3"][:], h), ALU.mult,
                                                      r=[kq("ba"), kq("E3")], w=[kq("btp")]))
                                A(lambda h=h: self.tt("pool" if h else "dve", blk(ktp_, h), hv(kd[:, gs], h), hv(g_["E3"][:], h), ALU.mult,
                                                      r=[K_KD, kq("E3")], w=[kq("ktp")]))
                            A(lambda: self.tt("dve", g3(g_["tB"][:]), gt_[:].unsqueeze(2).broadcast_to([128, NCG, C]), g3(G[:]),
                                              ALU.subtract, r=[kq("gt"), kq("G")], w=[kq("tB")]))
                            A(lambda: self.act(g_["E4"][:], g_["tB"][:], AF.Exp, r=[kq("tB")], w=[kq("E4")]))
                            for h in range(2):
                                A(lambda h=h: self.tt("dve" if h else "pool", blk(pdB, h), hv(g_["ba"][:], h), hv(g_["E4"][:], h), ALU.mult,
                                                      r=[kq("ba"), kq("E4")], w=[kpdB]))
                                A(lambda h=h: self.tt("pool" if h else "dve", blk(pdK, h), hv(kd[:, gs], h), hv(g_["E4"][:], h), ALU.mult,
                                                      r=[K_KD, kq("E4")], w=[kpdK]))
                                A(lambda h=h: self.cp("act", blk(pdV, h), hv(v_[:, gs], h), r=[K_V], w=[kpdV]))
                            A(lambda: self.act(gam_[:], gt_[:], AF.Exp, r=[kq("gt")], w=[kq("gam")]))

                            def transp(dst, kdst, srcp, ksrcp):
                                ps, kps = self.psb()
                                for c in range(NCG):
                                    self.tr(ps[:, c * 128:(c + 1) * 128], srcp[:, c, :], ident[:], r=[ksrcp, "ident"], w=[kps])
                                evac_copy(dst[:], ps[:].rearrange("p (c t) -> p c t", c=NCG), r=[kps], w=[kdst])
                            A(lambda: transp(atT_, kq("atT"), pdA, kpdA))
                            A(lambda: transp(TS["bhT"], "wbhT%d" % s3i, pdB, kpdB))
                            A(lambda: transp(TS["khT"], "wkhT%d" % s3i, pdK, kpdK))
                            A(lambda: transp(TS["vT"], "wvT%d" % s3i, pdV, kpdV))
                            return ops

                        def step_ops(gi):
                            ss = gi % 2
                            g_ = gsets[ss]
                            kq = lambda nm: "wg%d_%s" % (ss, nm)
                            atp_, btp_, ktp_, atT_, gam_ = g_["atp"], g_["btp"], g_["ktp"], g_["atT"], g_["gam"]
                            TS = setT[gi % 3]
                            kvT, kbhT, kkhT = "wvT%d" % (gi % 3), "wbhT%d" % (gi % 3), "wkhT%d" % (gi % 3)
                            WUn, Mn, ARBn, ARKn, QTn = ["%s%d" % (x, ss) for x in ("WU", "M", "ARBT", "ARKT", "QT")]
                            steps = []

                            def s1():
                                for ci in range(NCG):
                                    u = U[ci]
                                    ps, kps = self.psb()
                                    self.mm(ps[:, 0:128], btp_[:, ci, :], atp_[:, ci, :], r=[kq("btp"), kq("atp")], w=[kps])
                                    self.mm(ps[:, 128:192], btp_[:, ci, :], g_["rtb"][:, ci * C:(ci + 1) * C], r=[kq("btp"), kq("rtb")], w=[kps])
                                    self.tt("dve", u["XT0"][:], ps[:, 0:128], MT, ALU.mult, r=[kps, "wmsk"], w=[uk(ci, "XT0")])
                                    self.tt("dve", u[ARBn][:], ps[:, 128:192], MI, ALU.mult, r=[kps, "wmsk"], w=[uk(ci, ARBn)])
                                    self.tt("pool", u["PT0"][:], u["XT0"][:], ident[:], ALU.add, r=[uk(ci, "XT0"), "ident"], w=[uk(ci, "PT0")])
                            steps.append(s1)

                            def s2():
                                for ci in range(NCG):
                                    u = U[ci]
                                    ps, kps = self.psb()
                                    self.mm(ps[:, 0:128], ktp_[:, ci, :], atp_[:, ci, :], r=[kq("ktp"), kq("atp")], w=[kps])
                                    self.mm(ps[:, 128:192], ktp_[:, ci, :], g_["rtb"][:, ci * C:(ci + 1) * C], r=[kq("ktp"), kq("rtb")], w=[kps])
                                    self.tt("dve", u["AKT"][:], ps[:, 0:128], MT, ALU.mult, r=[kps, "wmsk"], w=[uk(ci, "AKT")])
                                    self.tt("dve", u[ARKn][:], ps[:, 128:192], MI, ALU.mult, r=[kps, "wmsk"], w=[uk(ci, ARKn)])
                            steps.append(s2)

                            def s3():
                                for ci in range(NCG):
                                    u = U[ci]
                                    ps, kps = self.psb()
                                    self.mm(ps[:, 0:128], atp_[:, ci, :], btp_[:, ci, :], r=[kq("atp"), kq("btp")], w=[kps])
                                    self.tt("dve", u["X0"][:], ps[:, 0:128], MN, ALU.mult, r=[kps, "wmsk"], w=[uk(ci, "X0")])
                            steps.append(s3)

                            def s5():
                                for ci in range(NCG):
                                    u = U[ci]
                                    ps, kps = self.psb()
                                    self.mm(ps[:, 0:128], u["AKT"][:], TS["vT"][:, ci, :], r=[uk(ci, "AKT"), kvT], w=[kps])
                                    evac_copy(u["AKV"][:], ps[:, 0:128], r=[kps], w=[uk(ci, "AKV")])
                            steps.append(s5)

                            for lev in range(1, 6):
                                pi, ni = (lev - 1) % 2, lev % 2
                                Xp, XTp, Xn, XTn = "X%d" % pi, "XT%d" % pi, "X%d" % ni, "XT%d" % ni
                                PTp, PTn = "PT%d" % pi, "PT%d" % ni

                                def sa(Xp=Xp, XTp=XTp, Xn=Xn):
                                    for ci in range(NCG):
                                        u = U[ci]
                                        ps, kps = self.psb()
                                        self.mm(ps[:, 0:128], u[XTp][:], u[Xp][:], r=[uk(ci, XTp), uk(ci, Xp)], w=[kps])
                                        evac_copy(u[Xn][:], ps[:, 0:128], r=[kps], w=[uk(ci, Xn)])
                                steps.append(sa)
                                if lev < 5:
                                    def sb_(Xp=Xp, XTp=XTp, XTn=XTn):
                                        for ci in range(NCG):
                                            u = U[ci]
                                            ps, kps = self.psb()
                                            self.mm(ps[:, 0:128], u[Xp][:], u[XTp][:], r=[uk(ci, XTp), uk(ci, Xp)], w=[kps])
                                            evac_copy(u[XTn][:], ps[:, 0:128], r=[kps], w=[uk(ci, XTn)])
                                    steps.append(sb_)

                                def sc(Xn=Xn, PTp=PTp, PTn=PTn):
                                    for ci in range(NCG):
                                        u = U[ci]
                                        ps, kps = self.psb()
                                        self.mm(ps[:, 0:128], u[Xn][:], u[PTp][:], start=True, stop=False, r=[uk(ci, Xn), uk(ci, PTp)], w=[kps])
                                        self.mm(ps[:, 0:128], ident_w[:], u[PTp][:], start=False, stop=True, r=["wident_w", uk(ci, PTp)], w=[kps])
                                        evac_copy(u[PTn][:], ps[:, 0:128], r=[kps], w=[uk(ci, PTn)])
                                steps.append(sc)
                            TTn = "PT1"

                            def s6():
                                for ci in range(NCG):
                                    u = U[ci]
                                    ps, kps = self.psb()
                                    self.mm(ps[:, 0:128], u[TTn][:], atT_[:, ci, :], r=[uk(ci, TTn), kq("atT")], w=[kps])
                                    self.mm(ps[:, 128:256], u[TTn][:], u["AKV"][:], r=[uk(ci, TTn), uk(ci, "AKV")], w=[kps])
                                    evac_copy(u[WUn][:], ps[:, 0:256], r=[kps], w=[uk(ci, WUn)])
                            steps.append(s6)

                            def s7():
                                for ci in range(NCG):
                                    u = U[ci]
                                    ps, kps = self.psb()
                                    self.mm(ps[:, 0:128], u[WUn][:, 0:128], TS["bhT"][:, ci, :], r=[uk(ci, WUn), kbhT], w=[kps])
                                    self.stt(u[Mn][:], ident[:], gam_[:, ci:ci + 1], ps[:, 0:128], ALU.mult, ALU.add,
                                             r=[kps, "ident", kq("gam")], w=[uk(ci, Mn)])
                                    ps2, kps2 = self.psb()
                                    self.mm(ps2[:, 0:64], u[WUn][:, 0:128], u[ARBn][:], r=[uk(ci, WUn), uk(ci, ARBn)], w=[kps2])
                                    self.tt("dve", u[QTn][:], ps2[:, 0:64], g_["rt"][:, ci * C:(ci + 1) * C], ALU.add,
                                            r=[kps2, kq("rt")], w=[uk(ci, QTn)])
                            steps.append(s7)
                            return steps

                        def seq_ops(gi):
                            ss = gi % 2
                            grp = groups[gi]
                            g0 = grp * GT
                            TS = setT[gi % 3]
                            kvT, kbhT, kkhT = "wvT%d" % (gi % 3), "wbhT%d" % (gi % 3), "wkhT%d" % (gi % 3)
                            WUn, Mn, ARBn, ARKn, QTn = ["%s%d" % (x, ss) for x in ("WU", "M", "ARBT", "ARKT", "QT")]
                            order = list(range(NCG))
                            if rev:
                                order = order[::-1]
                            out = []
                            for ci in order:
                                def seq(ci=ci):
                                    u = U[ci]
                                    sc_ = stcur[0]
                                    STc, kSTc = STt[sc_], "wST%d" % sc_
                                    STn, kSTn = STt[1 - sc_], "wST%d" % (1 - sc_)
                                    psy, kpy = self.psb()
                                    self.mm(psy[:, 0:64], STc[:], u[QTn][:], start=True, stop=False, r=[kSTc, uk(ci, QTn)], w=[kpy])
                                    self.mm(psy[:, 0:64], u[WUn][:, 128:256], u[ARBn][:], start=False, stop=False,
                                            r=[uk(ci, WUn), uk(ci, ARBn)], w=[kpy])
                                    self.mm(psy[:, 0:64], TS["vT"][:, ci, :], u[ARKn][:], start=False, stop=True,
                                            r=[kvT, uk(ci, ARKn)], w=[kpy])
                                    ysl = yacc[:, g0 + ci * C:g0 + (ci + 1) * C]
                                    if d == 0:
                                        self.cp("act", ysl, psy[:, 0:64], r=[kpy], w=[K_Y])
                                    else:
                                        self.tt("dve", ysl, psy[:, 0:64], ysl, ALU.add, r=[kpy, K_Y], w=[K_Y])
                                    pss, kpss = self.psb()
                                    self.mm(pss[:, 0:128], u[Mn][:], STc[:], start=True, stop=False, r=[uk(ci, Mn), kSTc], w=[kpss])
                                    self.mm(pss[:, 0:128], TS["bhT"][:, ci, :], u[WUn][:, 128:256], start=False, stop=False,
                                            r=[kbhT, uk(ci, WUn)], w=[kpss])
                                    self.mm(pss[:, 0:128], TS["khT"][:, ci, :], TS["vT"][:, ci, :], start=False, stop=True,
                                            r=[kkhT, kvT], w=[kpss])
                                    self.cp("act", STn[:], pss[:, 0:128], r=[kpss], w=[kSTn])
                                    stcur[0] = 1 - sc_
                                out.append(seq)
                            return out

                        for f_ in prep_ops(0):
                            f_()
                        old_seq = []
                        for gi in range(ngr):
                            steps = step_ops(gi)
                            nxt = prep_ops(gi + 1) if gi + 1 < ngr else []
                            npp = -(-len(nxt) // max(1, len(steps) - 2))
                            for si_, s_ in enumerate(steps):
                                s_()
                                for _ in range(npp):
                                    if nxt:
                                        nxt.pop(0)()
                                if old_seq and si_ % 4 == 3:
                                    old_seq.pop(0)()
                            for f_ in nxt:
                                f_()
                            for f_ in old_seq:
                                f_()
                            old_seq = seq_ops(gi)
                            if self.dbg == 5:
                                em.flush(); return
                        for f_ in old_seq:
                            f_()
                        if self.dbg == 6:
                            em.flush(); return
                    if self.dbg == 7:
                        em.flush(); return
                    for q4 in range(n512):
                        rg = slice(q4 * 512, (q4 + 1) * 512)
                        grg = slice(b * S + q4 * 512, b * S + (q4 + 1) * 512)
                        t1, k1 = t5.next(); t2, k2 = t5.next()
                        ps, kps = self.psb()
                        self.mm(ps[:], bones64[:], yacc[:, rg], r=["wbones64", K_Y], w=[kps])
                        self.tt("dve", t1[:], yacc[:, rg], ps[:], ALU.subtract, r=[K_Y, kps], w=[k1])
                        self.tt("pool", t2[:], t1[:], t1[:], ALU.mult, r=[k1], w=[k2])
                        ps, kps = self.psb()
                        self.mm(ps[:], bones64[:], t2[:], r=["wbones64", k2], w=[kps])
                        self.act(t2[:], ps[:], AF.Ln, bias=self.epsv[:, 1:2], r=[kps, "epsv"], w=[k2])
                        self.act(t2[:], t2[:], AF.Exp, scale=-0.5, r=[k2], w=[k2])
                        self.tt("dve", t1[:], t1[:], t2[:], ALU.mult, r=[k1, k2], w=[k1])
                        self.ts("dve", t1[:], t1[:], pv[:, PV_GG + hp:PV_GG + hp + 1], pv[:, PV_GB + hp:PV_GB + hp + 1],
                                ALU.mult, ALU.add, r=[k1, "pv"], w=[k1])
                        self.stt(t2[:], r_[:, rg], pv[:, PV_RK + hp:PV_RK + hp + 1], k_[:, rg], ALU.mult, ALU.mult,
                                 r=[K_R, K_K, "pv", k2], w=[k2])
                        ps, kps = self.psb()
                        self.mm(ps[:], bones[:], t2[:], r=["wbones", k2], w=[kps])
                        self.tt("dve", t2[:], ps[:], v_[:, rg], ALU.mult, r=[kps, K_V], w=[k2])
                        self.tt("pool", t1[:], t1[:], t2[:], ALU.add, r=[k1, k2], w=[k1])
                        s3_, ks3 = shr.next()
                        em.dma("sp", s3_[:], self.shT[2, :, grg], reads=[("shT", 2, b)], writes=[ks3])
                        ps, kps = self.psb()
                        self.mm(ps[:], g2sb[:, hp * 128:(hp + 1) * 128], s3_[:], r=["g2sb", ks3], w=[kps])
                        y_, ky = yo.next()
                        self.tt("dve", y_[:], t1[:], ps[:], ALU.mult, r=[k1, kps], w=[ky])
                        em.dma("sp", self.yT[2, hp * 128:(hp + 1) * 128, grg], y_[:], reads=[ky], writes=[("yT2", hp, b, q4)])
                    if self.dbg == 8:
                        em.flush(); return
                if self.dbg == 9:
                    em.flush(); return
            em.flush()

    def stage_merge(self, l):
        nc, em, T = self.nc, self.em, self.T
        with contextlib.ExitStack() as st:
            wbr = st.enter_context(nc.sbuf_tensor(_nm("mwbr"), [128, 3, 4, 1024], BF16))
            wo = st.enter_context(nc.sbuf_tensor(_nm("mwo"), [128, 8, 1024], BF16))
            g_bc = st.enter_context(nc.sbuf_tensor(_nm("mlng"), [128, 1024], F32))
            b_bc = st.enter_context(nc.sbuf_tensor(_nm("mlnb"), [128, 1024], F32))
            self.load_bc(g_bc, "mlngb", self.ln1_g[l])
            self.load_bc(b_bc, "mlngb", self.ln1_b[l])
            with contextlib.ExitStack() as st2:
                wstg = Rot(nc, st2, "mwst", 2, [128, 4, 1024], F32)
                for n in range(3):
                    ws, kws = wstg.next()
                    em.dma("sp", ws[:], self.w_branch[l, n].rearrange("(c p) d -> p c d", p=128), writes=[kws])
                    self.cp("pool" if n % 2 else "act", wbr[:, n], ws[:], r=[kws], w=["mwbr"])
                wov = self.w_out[l].rearrange("(c p) e -> p c e", p=128)
                for hf in range(2):
                    ws, kws = wstg.next()
                    em.dma("sp", ws[:], wov[:, hf * 4:(hf + 1) * 4, :], writes=[kws])
                    self.cp("pool" if hf else "act", wo[:, hf * 4:(hf + 1) * 4, :], ws[:], r=[kws], w=["mwo"])
                em.flush()
            yin = Rot(nc, st, "myin", 2, [128, 3, 4, 512], BF16)
            mTr = Rot(nc, st, "mmT", 2, [128, 8, 512], BF16)
            gtr = Rot(nc, st, "mgt", 4, [128, 512], F32)
            accr = Rot(nc, st, "macc", 2, [128, 512], F32)
            tmr = Rot(nc, st, "mtm", 2, [128, 512], F32)
            xr = Rot(nc, st, "mx", 2, [128, 1024], F32)
            prer = Rot(nc, st, "mpre", 1, [128, 1024], F32)
            xor_ = Rot(nc, st, "mxo", 2, [128, 1024], F32)
            for tr_ in range(T // 512):
                tsl = slice(tr_ * 512, (tr_ + 1) * 512)
                yi, kyi = yin.next()
                for n in range(3):
                    em.dma("sp", yi[:, n], self.yT[n, :, tsl].rearrange("(c p) t -> p c t", p=128), writes=[kyi])
                mT, kmT = mTr.next()
                for db in range(8):
                    acc, kacc = accr.next()
                    for n in range(3):
                        ps, kps = self.psb()
                        for cc in range(4):
                            self.mm(ps[:], wbr[:, n, cc, db * 128:(db + 1) * 128], yi[:, n, cc, :],
                                    start=(cc == 0), stop=(cc == 3), r=["mwbr", kyi], w=[kps])
                        gt_, kgt = gtr.next()
                        em.dma("sp", gt_[:], self.gT[(n * 8 + db) * 128:(n * 8 + db + 1) * 128, tsl], writes=[kgt])
                        if n == 0:
                            self.tt("dve", acc[:], ps[:], gt_[:], ALU.mult, r=[kps, kgt], w=[kacc])
                        else:
                            tm, ktm = tmr.next()
                            self.tt("dve", tm[:], ps[:], gt_[:], ALU.mult, r=[kps, kgt], w=[ktm])
                            if n == 1:
                                self.tt("pool", acc[:], acc[:], tm[:], ALU.add, r=[kacc, ktm], w=[kacc])
                            else:
                                self.tt("pool", mT[:, db, :], acc[:], tm[:], ALU.add, r=[kacc, ktm], w=[kmT])
                for t4 in range(4):
                    tt = tr_ * 4 + t4
                    xt, kx = xr.next()
                    em.dma("sp", xt[:], self.x_tm[tt * 128:(tt + 1) * 128, :], writes=[kx])
                    pre, kpre = prer.next()
                    for hf in range(2):
                        ps, kps = self.psb()
                        for dc in range(8):
                            self.mm(ps[:], mT[:, dc, t4 * 128:(t4 + 1) * 128], wo[:, dc, hf * 512:(hf + 1) * 512],
                                    start=(dc == 0), stop=(dc == 7), r=[kmT, "mwo"], w=[kps])
                        self.stt(pre[:, hf * 512:(hf + 1) * 512], xt[:, hf * 512:(hf + 1) * 512], ALPHA, ps[:],
                                 ALU.mult, ALU.add, r=[kx, kps], w=[kpre])
                    xo, kxo = xor_.next()
                    self.ln_tile(pre, kpre, xo, kxo, g_bc, b_bc, "mlngb")
                    em.dma("sp", self.x1_tm[tt * 128:(tt + 1) * 128, :], xo[:], reads=[kxo], writes=[("x1_tm", tt)])
                    self.transpose_into_xT(xo, kxo, tt)
            em.flush()

    def stage_ffn(self, l, last):
        nc, em, T = self.nc, self.em, self.T
        with contextlib.ExitStack() as st:
            s32 = Rot(nc, st, "fs32", 2, [128, 4096], F32)
            s16 = Rot(nc, st, "fs16", 2, [128, 4096], BF16)
            wuv = self.w_up[l].rearrange("(c p) f -> p c f", p=128)
            wdv = self.w_down[l].rearrange("(fc p) e -> p fc e", p=128)
            engs = ("pool", "act", "dve")
            for i in range(8):
                a, ka = s32.next(); bq, kb_ = s16.next()
                a3 = a[:].rearrange("p (c f) -> p c f", c=8)
                b3 = bq[:].rearrange("p (c f) -> p c f", c=8)
                em.dma("sp", a3, wuv[:, :, i * 512:(i + 1) * 512], writes=[ka])
                self.cp(engs[i % 3], bq[:], a[:], r=[ka], w=[kb_])
                em.dma("sp", self.wup_bf[:, :, i * 512:(i + 1) * 512], b3, reads=[kb_], writes=[("wup_bf", i)])
            for eb in range(8):
                a, ka = s32.next(); bq, kb_ = s16.next()
                a3 = a[:].rearrange("p (c f) -> p c f", c=32)
                b3 = bq[:].rearrange("p (c f) -> p c f", c=32)
                em.dma("sp", a3, wdv[:, :, eb * 128:(eb + 1) * 128], writes=[ka])
                self.cp(engs[(eb + 2) % 3], bq[:], a[:], r=[ka], w=[kb_])
                em.dma("sp", self.wdn_bf[eb], b3, reads=[kb_], writes=[("wdn_bf", eb)])
            em.flush()
        with contextlib.ExitStack() as st:
            g_bc = st.enter_context(nc.sbuf_tensor(_nm("flng"), [128, 1024], F32))
            b_bc = st.enter_context(nc.sbuf_tensor(_nm("flnb"), [128, 1024], F32))
            self.load_bc(g_bc, "flngb", self.ln2_g[l])
            self.load_bc(b_bc, "flngb", self.ln2_b[l])
            hidT = st.enter_context(nc.sbuf_tensor(_nm("fhid"), [128, 32, 512], BF16))
            outT = st.enter_context(nc.sbuf_tensor(_nm("fout"), [128, 8, 512], F32))
            wur = Rot(nc, st, "fwu", 2, [128, 8, 512], BF16)
            wdr = Rot(nc, st, "fwd", 2, [128, 32, 128], BF16)
            rlr = Rot(nc, st, "frl", 3, [128, 512], F32)
            xr = Rot(nc, st, "fx", 2, [128, 1024], F32)
            prer = Rot(nc, st, "fpre", 2, [128, 1024], F32)
            xor_ = Rot(nc, st, "fxo", 2, [128, 1024], F32)
            for tg in range(T // 512):
                tsl = slice(tg * 512, (tg + 1) * 512)
                xk = [("xT", tg * 4 + i) for i in range(4)]
                for fg in range(8):
                    wu, kwu = wur.next()
                    em.dma("sp", wu[:], self.wup_bf[:, :, fg * 512:(fg + 1) * 512], writes=[kwu])
                    for f4 in range(4):
                        fb = fg * 4 + f4
                        ps, kps = self.psb()
                        for dc in range(8):
                            self.mm(ps[:], wu[:, dc, f4 * 128:(f4 + 1) * 128], self.xT[:, dc, tsl],
                                    start=(dc == 0), stop=(dc == 7), r=[kwu] + xk, w=[kps])
                        rl, krl = rlr.next()
                        self.act(rl[:], ps[:], AF.Relu, r=[kps], w=[krl])
                        self.tt("pool" if fb % 2 else "dve", hidT[:, fb, :], rl[:], rl[:], ALU.mult, r=[krl], w=[("fhid", fb)])
                for eb in range(8):
                    wd, kwd = wdr.next()
                    em.dma("sp", wd[:], self.wdn_bf[eb], writes=[kwd])
                    ps, kps = self.psb()
                    for fc in range(32):
                        self.mm(ps[:], wd[:, fc, :], hidT[:, fc, :], start=(fc == 0), stop=(fc == 31),
                                r=[kwd, ("fhid", fc)], w=[kps])
                    self.cp("act" if eb % 2 else "dve", outT[:, eb, :], ps[:], r=[kps], w=[("fout", eb)])
                for t4 in range(4):
                    tt = tg * 4 + t4
                    xt, kx = xr.next()
                    em.dma("sp", xt[:], self.x1_tm[tt * 128:(tt + 1) * 128, :], writes=[kx])
                    pre, kpre = prer.next()
                    for hf in range(2):
                        ps, kps = self.psb()
                        for c4 in range(4):
                            self.tr(ps[:, c4 * 128:(c4 + 1) * 128], outT[:, hf * 4 + c4, t4 * 128:(t4 + 1) * 128],
                                    self.ident_f[:], r=[("fout", hf * 4 + c4), "ident"], w=[kps])
                        self.stt(pre[:, hf * 512:(hf + 1) * 512], xt[:, hf * 512:(hf + 1) * 512], ALPHA, ps[:],
                                 ALU.mult, ALU.add, r=[kx, kps], w=[kpre])
                    xo, kxo = xor_.next()
                    self.ln_tile(pre, kpre, xo, kxo, g_bc, b_bc, "flngb")
                    if last:
                        em.dma("sp", self.out[tt * 128:(tt + 1) * 128, :], xo[:], reads=[kxo], writes=[("out", tt)])
                    else:
                        em.dma("sp", self.x_tm[tt * 128:(tt + 1) * 128, :], xo[:], reads=[kxo], writes=[("x_tm", tt)])
                        self.transpose_into_xT(xo, kxo, tt)
            em.flush()

    def build(self):
        nc = self.nc
        self.declare()
        stages = ["ln_in", "proj", "pool", "attn", "rwkv", "merge", "ffn"]
        upto = self.upto
        with contextlib.ExitStack() as st:
            self.em = Em(nc, st)
            em = self.em
            self.psum = [st.enter_context(nc.psum_tensor("psb%d" % i, [128, 512], F32)) for i in range(8)]
            self.ident_f = st.enter_context(nc.sbuf_tensor(_nm("ident_f"), [128, 128], F32))
            self.xT = st.enter_context(nc.sbuf_tensor(_nm("xT"), [128, 8, self.T], BF16))
            self.pv = st.enter_context(nc.sbuf_tensor(_nm("pv"), [128, NPV], F32))
            self.lnst = Rot(nc, st, "lnst", 4, [128, 16], F32)
            self.band = st.enter_context(nc.sbuf_tensor(_nm("band"), [128, 4, 3, 128], F32))
            self.relb = st.enter_context(nc.sbuf_tensor(_nm("relb"), [128, 128], F32))
            self.epsv = st.enter_context(nc.sbuf_tensor(_nm("epsv"), [128, 2], F32))
            self.memset("dve", self.epsv[:, 0:1], 1e-5, w=["epsv"])
            self.memset("dve", self.epsv[:, 1:2], GN_EPS, w=["epsv"])
            self.build_band()
            em.dma("sp", self.ident_f[:], self.c_ident, writes=["ident"])
            self.stage_ln_in()
            done = upto == "ln_in"
            for l in range(self.L):
                if done:
                    break
                em.dma("sp", self.pv[:], self.pvec_d[l], writes=["pv"])
                self.stage_proj(l)
                if upto == "proj":
                    break
                self.stage_pool(l)
                if upto == "pool":
                    break
                self.stage_attn(l)
                if upto == "attn":
                    break
                self.stage_rwkv(l)
                if upto == "rwkv":
                    break
                self.stage_merge(l)
                if upto == "merge":
                    break
                self.stage_ffn(l, last=(l == self.L - 1))
            em.flush()
        return nc


def shared_in_map(inp, S, L):
    f = lambda a: np.ascontiguousarray(np.asarray(a, np.float32))
    m = {
        "ln0_g": f(inp["ln0_g"]), "ln0_b": f(inp["ln0_b"]),
        "w_in": f(inp["w_in"][:L]), "pool_w": f(inp["pool_w"][:L]),
        "da_lambda": f(inp["da_lambda"][:L]).reshape(L, 256),
        "rel_bias": f(inp["rel_bias"]).reshape(128),
        "rw_w2": f(inp["rw_w2"][:L]), "rw_a2": f(inp["rw_a2"][:L]), "rw_g2": f(inp["rw_g2"][:L]),
        "w_branch": f(inp["w_branch"][:L]), "w_out": f(inp["w_out"][:L]),
        "ln1_g": f(inp["ln1_g"][:L]), "ln1_b": f(inp["ln1_b"][:L]),
        "w_up": f(inp["w_up"][:L]), "w_down": f(inp["w_down"][:L]),
        "ln2_g": f(inp["ln2_g"][:L]), "ln2_b": f(inp["ln2_b"][:L]),
        "pvec": pack_pvec(inp, L),
    }
    m.update(host_consts(S))
    return m


_CACHE = {}


def kernel(**inputs):
    x = np.asarray(inputs["x"], np.float32)
    B, S, D = x.shape
    ncores = B // NSEQ
    key = (S, DEPTH)
    if key not in _CACHE:
        _CACHE[key] = Builder(S=S, L=DEPTH).build()
    nc = _CACHE[key]
    shared = shared_in_map(inputs, S, DEPTH)
    in_maps = []
    for c in range(ncores):
        m = dict(shared)
        m["x"] = np.ascontiguousarray(x[c * NSEQ:(c + 1) * NSEQ].reshape(NSEQ * S, D))
        in_maps.append(m)
    res = run_bass_kernel_spmd(nc, in_maps, core_ids=list(range(ncores)))
    out = np.stack([np.asarray(r["out"], np.float32).reshape(NSEQ, S, D) for r in res.results])
    return out.reshape(B, S, D)
```
